# Optimizing a Trainium2 kernel written in Bass

```python
import math
import jax, jax.numpy as jnp
from jax import lax
import numpy as np

D_MODEL = 1024
BATCH = 8
SEQ = 4096
DEPTH = 2
DEC_BATCH = 32
DEC_SEQ = 8
PAST_LEN = 16384
PAGE_SIZE = 128

N_GROUPS_MIX = 4
GROUP_W = D_MODEL // N_GROUPS_MIX
POOL_WINDOWS = (2, 4, 8, 16)
POOL_GROUPS = len(POOL_WINDOWS)
POOL_CH = GROUP_W // POOL_GROUPS
POOL_BUF = max(POOL_WINDOWS) - 1
MOBA_HEADS = 4
HEAD_DIM = GROUP_W // MOBA_HEADS
MOBA_BLOCK = 256
MOBA_TOPK = 3
MOBA_QB = 32
ROPE_DIM = HEAD_DIM // 4
ROPE_THETA = 500000.0
ATTN_SCALE = HEAD_DIM ** -0.5
NEG_BIG = -1e30
CONV_W = 31
CONV_BUF = CONV_W - 1
HGRN_HEADS = 4
HGRN_DK = GROUP_W // HGRN_HEADS
HGRN_DV = GROUP_W // HGRN_HEADS
HGRN_CHUNK = 64
D_FF = 2816
N_IN_SPLITS = 10
IN_COLS = N_IN_SPLITS * GROUP_W
NORM_EPS = 1e-6

kernel_name = "hymba_pool_moba_conv_hgrn2_decode_step"


def rmsnorm(x, g):
    xf = x.astype(jnp.float32)
    y = xf * lax.rsqrt(jnp.mean(xf * xf, axis=-1, keepdims=True) + NORM_EPS)
    return (y * g).astype(x.dtype)


def layernorm(x, g, b):
    xf = x.astype(jnp.float32)
    mu = jnp.mean(xf, axis=-1, keepdims=True)
    var = jnp.mean(jnp.square(xf - mu), axis=-1, keepdims=True)
    return ((xf - mu) * lax.rsqrt(var + NORM_EPS) * g + b).astype(x.dtype)


def swiglu(h, wg, wu, wd):
    return (jax.nn.silu(h @ wg) * (h @ wu)) @ wd


def partial_rope(x, pos):
    half = ROPE_DIM // 2
    inv = ROPE_THETA ** (-jnp.arange(half, dtype=jnp.float32) * 2.0 / ROPE_DIM)
    ang = pos.astype(jnp.float32)[:, None] * inv
    cos = jnp.cos(ang)[None, :, None, :]
    sin = jnp.sin(ang)[None, :, None, :]
    xr = x[..., :ROPE_DIM].astype(jnp.float32)
    x1, x2 = xr[..., :half], xr[..., half:]
    rot = jnp.concatenate([x1 * cos - x2 * sin, x2 * cos + x1 * sin], axis=-1).astype(x.dtype)
    return jnp.concatenate([rot, x[..., ROPE_DIM:]], axis=-1)


def pool_mix(u, prev, pos, w_grp, scale):
    B, T, _ = u.shape
    xp = jnp.concatenate([prev.astype(u.dtype), u], axis=1)
    cs = jnp.cumsum(xp.astype(jnp.float32), axis=1)
    cs = jnp.concatenate([jnp.zeros_like(cs[:, :1]), cs], axis=1)
    end = cs[:, POOL_BUF + 1:]
    outs = []
    for gi, w in enumerate(POOL_WINDOWS):
        sl = slice(gi * POOL_CH, (gi + 1) * POOL_CH)
        s = end[..., sl] - cs[:, POOL_BUF + 1 - w:POOL_BUF + 1 - w + T, sl]
        cnt = jnp.minimum(w, pos + 1).astype(jnp.float32)[None, :, None]
        outs.append(s / cnt)
    pooled = (jnp.concatenate(outs, axis=-1) - u.astype(jnp.float32)).astype(u.dtype)
    pooled = pooled.reshape(B, T, POOL_GROUPS, POOL_CH)
    y = jnp.einsum('btgc,gcd->btgd', pooled, w_grp).reshape(B, T, GROUP_W) * scale
    return y, xp[:, -POOL_BUF:]


def moba_attend(q, k_all, v_all, q_pos0):
    B, Tq, H, dh = q.shape
    Tk = k_all.shape[1]
    nblk = -(-Tk // MOBA_BLOCK)
    pad = nblk * MOBA_BLOCK - Tk
    kb = jnp.pad(k_all, ((0, 0), (0, pad), (0, 0), (0, 0))).reshape(B, nblk, MOBA_BLOCK, H, dh)
    vb = jnp.pad(v_all, ((0, 0), (0, pad), (0, 0), (0, 0))).reshape(B, nblk, MOBA_BLOCK, H, dh)
    kmean = jnp.mean(kb, axis=2, dtype=jnp.float32)
    qpos = q_pos0 + jnp.arange(Tq, dtype=jnp.int32)
    own = qpos // MOBA_BLOCK
    gate = jnp.einsum('bqhd,bnhd->bhqn', q.astype(jnp.float32), kmean)
    past_ok = jnp.arange(nblk)[None, :] < own[:, None]
    gate = jnp.where(past_ok[None, None], gate, NEG_BIG)
    kk = min(MOBA_TOPK, nblk)
    _, sel = lax.top_k(gate, kk)
    blocks = jnp.concatenate(
        [sel.astype(jnp.int32), jnp.broadcast_to(own[None, None, :, None], (B, H, Tq, 1))], axis=-1)
    qb = MOBA_QB if Tq % MOBA_QB == 0 else Tq
    nq = Tq // qb
    q_c = q.reshape(B, nq, qb, H, dh).transpose(1, 0, 3, 2, 4)
    blk_c = blocks.reshape(B, H, nq, qb, kk + 1).transpose(2, 0, 1, 3, 4)
    pos_c = qpos.reshape(nq, qb)
    own_c = own.reshape(nq, qb)
    b_idx = jnp.arange(B)[:, None, None, None]
    h_idx = jnp.arange(H)[None, :, None, None]
    is_own = jnp.arange(kk + 1) == kk
    offs = jnp.arange(MOBA_BLOCK, dtype=jnp.int32)

    def attend(args):
        qc, bc, pc, oc = args
        gk = kb[b_idx, bc, :, h_idx, :]
        gv = vb[b_idx, bc, :, h_idx, :]
        s = jnp.einsum('bhqd,bhqjkd->bhqjk', qc, gk).astype(jnp.float32) * ATTN_SCALE
        kpos = bc[..., None] * MOBA_BLOCK + offs
        blk_ok = is_own[None, None, None, :] | (bc < oc[None, None, :, None])
        ok = (kpos <= pc[None, None, :, None, None]) & blk_ok[..., None]
        s = jnp.where(ok, s, NEG_BIG)
        p = jax.nn.softmax(s.reshape(B, H, qb, -1), axis=-1).reshape(s.shape)
        p = jnp.where(ok, p, 0.0).astype(gv.dtype)
        return jnp.einsum('bhqjk,bhqjkd->bhqd', p, gv)

    o = lax.map(attend, (q_c, blk_c, pos_c, own_c))
    return o.transpose(1, 0, 3, 2, 4).reshape(B, Tq, H * dh)


def conv_mix(a, b, prev, w_dw, b_dw, ln_g, ln_b, w_pw):
    u = a * jax.nn.sigmoid(b)
    xp = jnp.concatenate([prev.astype(u.dtype), u], axis=1)
    y = lax.conv_general_dilated(xp, w_dw[:, None, :].astype(xp.dtype), window_strides=(1,),
                                 padding='VALID', dimension_numbers=('NWC', 'WIO', 'NWC'),
                                 feature_group_count=GROUP_W) + b_dw
    y = jax.nn.silu(layernorm(y, ln_g, ln_b))
    return y @ w_pw, xp[:, -CONV_BUF:]


def hgrn2_mix(hq, hf, hi, hg, S0, lb, norm_g):
    B, T, _ = hq.shape
    f32 = jnp.float32
    C = HGRN_CHUNK if T % HGRN_CHUNK == 0 else T
    n = T // C
    fgate = lb + (1.0 - lb) * jax.nn.sigmoid(hf.astype(f32))
    logf = jnp.log(fgate)
    kin = 1.0 - fgate

    def chunks(a, d):
        return a.astype(f32).reshape(B, n, C, HGRN_HEADS, d).transpose(1, 0, 3, 2, 4)

    tri = jnp.tril(jnp.ones((C, C), dtype=bool))[:, :, None]

    def step(S, inp):
        qc, kc, vc, lc = inp
        bc = jnp.cumsum(lc, axis=2)
        o_inter = jnp.einsum('bhtk,bhkv->bhtv', qc * jnp.exp(bc), S)
        diff = bc[:, :, :, None, :] - bc[:, :, None, :, :]
        dec = jnp.where(tri, jnp.exp(jnp.minimum(diff, 0.0)), 0.0)
        att = jnp.einsum('bhtk,bhtsk,bhsk->bhts', qc, dec, kc)
        o = o_inter + jnp.einsum('bhts,bhsv->bhtv', att, vc)
        blast = bc[:, :, -1:, :]
        S = jnp.exp(blast[:, :, 0, :])[..., None] * S + jnp.einsum(
            'bhsk,bhsv->bhkv', kc * jnp.exp(blast - bc), vc)
        return S, o

    S_fin, o = lax.scan(step, S0.astype(f32),
                        (chunks(hq, HGRN_DK), chunks(kin, HGRN_DK), chunks(hi, HGRN_DV), chunks(logf, HGRN_DK)))
    o = o.transpose(1, 0, 3, 2, 4).reshape(B, T, HGRN_HEADS, HGRN_DV)
    o = rmsnorm(o, norm_g) * jax.nn.silu(hg.astype(f32).reshape(B, T, HGRN_HEADS, HGRN_DV))
    return o.reshape(B, T, GROUP_W).astype(hq.dtype), S_fin.astype(S0.dtype)


def run_trunk(x, q_pos0, pool_st, conv_st, hgrn_st, cache_k, cache_v, page_table,
              ln_ffn1, ffn1_w_gate, ffn1_w_up, ffn1_w_down, ln_mix, w_in, w_out,
              pool_w, pool_scale, q_norm, k_norm, conv_w, conv_b, conv_ln_g, conv_ln_b, conv_pw,
              hgrn_lower_bounds, hgrn_norm, ln_ffn2, ffn2_w_gate, ffn2_w_up, ffn2_w_down):
    B, T, _ = x.shape
    pos = q_pos0 + jnp.arange(T, dtype=jnp.int32)
    lbs = jax.nn.softmax(hgrn_lower_bounds.astype(jnp.float32), axis=0)
    lbs = jnp.cumsum(lbs, axis=0) - lbs[0]
    ks, vs, pools, convs, hgrns = [], [], [], [], []
    for l in range(DEPTH):
        h = rmsnorm(x, ln_ffn1[l])
        x = x + 0.5 * swiglu(h, ffn1_w_gate[l], ffn1_w_up[l], ffn1_w_down[l])
        h = rmsnorm(x, ln_mix[l])
        u_pool, q, k, v, ca, cb, hq, hf, hi, hg = jnp.split(h @ w_in[l], N_IN_SPLITS, axis=-1)
        y_a, pool_new = pool_mix(u_pool, pool_st[l], pos, pool_w[l], pool_scale[l])
        q = partial_rope(rmsnorm(q.reshape(B, T, MOBA_HEADS, HEAD_DIM), q_norm[l]), pos)
        k = partial_rope(rmsnorm(k.reshape(B, T, MOBA_HEADS, HEAD_DIM), k_norm[l]), pos)
        v = v.reshape(B, T, MOBA_HEADS, HEAD_DIM)
        if cache_k is None:
            k_all, v_all = k, v
        else:
            past_k = cache_k[l, page_table].reshape(B, -1, MOBA_HEADS, HEAD_DIM)
            past_v = cache_v[l, page_table].reshape(B, -1, MOBA_HEADS, HEAD_DIM)
            k_all = jnp.concatenate([past_k.astype(k.dtype), k], axis=1)
            v_all = jnp.concatenate([past_v.astype(v.dtype), v], axis=1)
        y_b = moba_attend(q, k_all, v_all, q_pos0)
        y_c, conv_new = conv_mix(ca, cb, conv_st[l], conv_w[l], conv_b[l], conv_ln_g[l], conv_ln_b[l], conv_pw[l])
        y_d, S_new = hgrn2_mix(hq, hf, hi, hg, hgrn_st[l], lbs[l], hgrn_norm[l])
        x = x + jnp.concatenate([y_a, y_b, y_c, y_d], axis=-1) @ w_out[l]
        h = rmsnorm(x, ln_ffn2[l])
        x = x + 0.5 * swiglu(h, ffn2_w_gate[l], ffn2_w_up[l], ffn2_w_down[l])
        ks.append(k)
        vs.append(v)
        pools.append(pool_new)
        convs.append(conv_new)
        hgrns.append(S_new)
    return x, jnp.stack(ks), jnp.stack(vs), jnp.stack(pools), jnp.stack(convs), jnp.stack(hgrns)


def setup_inputs(seed: int = 0) -> dict:
    key = jax.random.key(seed)
    ks = jax.random.split(key, 40)
    f32 = jnp.float32
    n_pages = PAST_LEN // PAGE_SIZE
    n_used = DEC_BATCH * n_pages
    n_pool = n_used + n_used // 4

    def nrm(k, shape, scale):
        return jax.random.normal(k, shape, f32) * scale

    def gain(k, shape):
        return 1.0 + 0.1 * jax.random.normal(k, shape, f32)

    page_table = jax.random.permutation(ks[4], n_pool)[:n_used].reshape(DEC_BATCH, n_pages).astype(jnp.int32)
    return {
        "x_prompt": nrm(ks[0], (BATCH, SEQ, D_MODEL), 1.0),
        "x_sample": nrm(ks[1], (DEC_BATCH, DEC_SEQ, D_MODEL), 1.0),
        "cache_k": nrm(ks[2], (DEPTH, n_pool, PAGE_SIZE, MOBA_HEADS, HEAD_DIM), 1.0),
        "cache_v": nrm(ks[3], (DEPTH, n_pool, PAGE_SIZE, MOBA_HEADS, HEAD_DIM), 1.0),
        "page_table": page_table,
        "state_pool": nrm(ks[5], (DEPTH, DEC_BATCH, POOL_BUF, GROUP_W), 1.0),
        "state_conv": nrm(ks[6], (DEPTH, DEC_BATCH, CONV_BUF, GROUP_W), 0.5),
        "state_hgrn": nrm(ks[7], (DEPTH, DEC_BATCH, HGRN_HEADS, HGRN_DK, HGRN_DV), 0.3),
        "ln_ffn1": gain(ks[8], (DEPTH, D_MODEL)),
        "ffn1_w_gate": nrm(ks[9], (DEPTH, D_MODEL, D_FF), D_MODEL ** -0.5),
        "ffn1_w_up": nrm(ks[10], (DEPTH, D_MODEL, D_FF), D_MODEL ** -0.5),
        "ffn1_w_down": nrm(ks[11], (DEPTH, D_FF, D_MODEL), D_FF ** -0.5),
        "ln_mix": gain(ks[12], (DEPTH, D_MODEL)),
        "w_in": nrm(ks[13], (DEPTH, D_MODEL, IN_COLS), D_MODEL ** -0.5),
        "w_out": nrm(ks[14], (DEPTH, D_MODEL, D_MODEL), D_MODEL ** -0.5),
        "pool_w": nrm(ks[15], (DEPTH, POOL_GROUPS, POOL_CH, POOL_CH), POOL_CH ** -0.5),
        "pool_scale": gain(ks[16], (DEPTH, GROUP_W)),
        "q_norm": gain(ks[17], (DEPTH, HEAD_DIM)),
        "k_norm": gain(ks[18], (DEPTH, HEAD_DIM)),
        "conv_w": nrm(ks[19], (DEPTH, CONV_W, GROUP_W), CONV_W ** -0.5),
        "conv_b": nrm(ks[20], (DEPTH, GROUP_W), 0.02),
        "conv_ln_g": gain(ks[21], (DEPTH, GROUP_W)),
        "conv_ln_b": nrm(ks[22], (DEPTH, GROUP_W), 0.02),
        "conv_pw": nrm(ks[23], (DEPTH, GROUP_W, GROUP_W), GROUP_W ** -0.5),
        "hgrn_lower_bounds": nrm(ks[24], (DEPTH, GROUP_W), 0.5),
        "hgrn_norm": gain(ks[25], (DEPTH, HGRN_DV)),
        "ln_ffn2": gain(ks[26], (DEPTH, D_MODEL)),
        "ffn2_w_gate": nrm(ks[27], (DEPTH, D_MODEL, D_FF), D_MODEL ** -0.5),
        "ffn2_w_up": nrm(ks[28], (DEPTH, D_MODEL, D_FF), D_MODEL ** -0.5),
        "ffn2_w_down": nrm(ks[29], (DEPTH, D_FF, D_MODEL), D_FF ** -0.5),
    }


def reference(x_prompt, x_sample, cache_k, cache_v, page_table, state_pool, state_conv, state_hgrn,
              ln_ffn1, ffn1_w_gate, ffn1_w_up, ffn1_w_down, ln_mix, w_in, w_out,
              pool_w, pool_scale, q_norm, k_norm, conv_w, conv_b, conv_ln_g, conv_ln_b, conv_pw,
              hgrn_lower_bounds, hgrn_norm, ln_ffn2, ffn2_w_gate, ffn2_w_up, ffn2_w_down):
    weights = (ln_ffn1, ffn1_w_gate, ffn1_w_up, ffn1_w_down, ln_mix, w_in, w_out,
               pool_w, pool_scale, q_norm, k_norm, conv_w, conv_b, conv_ln_g, conv_ln_b, conv_pw,
               hgrn_lower_bounds, hgrn_norm, ln_ffn2, ffn2_w_gate, ffn2_w_up, ffn2_w_down)
    bp = x_prompt.shape[0]
    dt = x_prompt.dtype
    zero_pool = jnp.zeros((DEPTH, bp, POOL_BUF, GROUP_W), dt)
    zero_conv = jnp.zeros((DEPTH, bp, CONV_BUF, GROUP_W), dt)
    zero_hgrn = jnp.zeros((DEPTH, bp, HGRN_HEADS, HGRN_DK, HGRN_DV), dt)
    y_prompt, k_prompt, v_prompt, pool_prompt, conv_prompt, hgrn_prompt = run_trunk(
        x_prompt, 0, zero_pool, zero_conv, zero_hgrn, None, None, None, *weights)
    past_len = page_table.shape[1] * cache_k.shape[2]
    y_sample, k_sample, v_sample, pool_sample, conv_sample, hgrn_sample = run_trunk(
        x_sample, past_len, state_pool, state_conv, state_hgrn, cache_k, cache_v, page_table, *weights)
    return (y_prompt, y_sample, k_prompt, v_prompt, k_sample, v_sample,
            pool_prompt, pool_sample, conv_prompt, conv_sample, hgrn_prompt, hgrn_sample)
```

```python
import numpy as np
from contextlib import ExitStack
import concourse.bass as bass
import concourse.mybir as mybir
from concourse.bass_utils import run_bass_kernel_spmd

F32 = mybir.dt.float32
BF16 = mybir.dt.bfloat16
AF = mybir.ActivationFunctionType
ALU = mybir.AluOpType
AX = mybir.AxisListType

D = 1024
DFF = 2816
GW = 256
NEG = -1.0e30
STAGE = 9
SUB = 9
DO_SAMPLE = True
EPS = 1e-6
SCALE = 64 ** -0.5


class Cell:
    __slots__ = ("w", "r")

    def __init__(self):
        self.w = None
        self.r = {}


class Buf:
    def __init__(self, t, ncells=1):
        self.t = t
        self.cells = [Cell() for _ in range(ncells)]

    def c(self, i):
        return [self.cells[i]]

    @property
    def all(self):
        return self.cells


def _cells(items):
    out = []
    for it in items:
        if isinstance(it, Buf):
            out.extend(it.cells)
        elif isinstance(it, Cell):
            out.append(it)
        else:
            out.extend(_cells(it))
    return out


class Sched:
    ENG = ("pe", "act", "dve", "pool", "sp")

    def __init__(self, nc, stack, n_dma_sems=(10, 8)):
        self.nc = nc
        self.stack = stack
        self.eng = {"pe": nc.tensor, "act": nc.scalar, "dve": nc.vector, "pool": nc.gpsimd, "sp": nc.sync}
        self.sem = {}
        self.cnt = {}
        self.seen = {e: {} for e in self.ENG}
        for e in self.ENG:
            self.sem[e] = stack.enter_context(nc.semaphore("s_" + e))
            self.cnt[e] = 0
        self.dq = {}
        for q, n in zip(("sp", "pool"), n_dma_sems):
            sems = []
            for i in range(n):
                k = "d_%s_%d" % (q, i)
                self.sem[k] = stack.enter_context(nc.semaphore(k))
                self.cnt[k] = 0
                sems.append(k)
            self.dq[q] = [sems, 0]
        self.nwaits = 0
        self.ninstr = 0

    def sbuf(self, name, shape, dt, ncells=1):
        t = self.stack.enter_context(self.nc.sbuf_tensor("sb_" + name, list(shape), dt))
        return Buf(t, ncells)

    def psum(self, name, shape, dt=F32, ncells=1):
        t = self.stack.enter_context(self.nc.psum_tensor(name, list(shape), dt))
        return Buf(t, ncells)

    def _wait(self, e, key, val):
        if self.seen[e].get(key, 0) >= val:
            return
        self.eng[e].wait_ge(self.sem[key], val)
        self.seen[e][key] = val
        self.nwaits += 1

    def _deps(self, e, R, W):
        deps = {}
        for c in R:
            if c.w is not None:
                k, v = c.w
                if deps.get(k, 0) < v:
                    deps[k] = v
        for c in W:
            if c.w is not None:
                k, v = c.w
                if deps.get(k, 0) < v:
                    deps[k] = v
            for k, v in c.r.items():
                if deps.get(k, 0) < v:
                    deps[k] = v
        for k, v in deps.items():
            if k == e and e == "pe":
                continue
            self._wait(e, k, v)

    def _mark(self, key, val, R, W):
        for c in R:
            if c.r.get(key, 0) < val:
                c.r[key] = val
        for c in W:
            c.w = (key, val)
            c.r = {}

    def op(self, e, fn, R=(), W=(), SS=()):
        R = _cells(R)
        W = _cells(W)
        self._deps(e, R, W)
        for c in _cells(SS):
            if c.w is not None and c.w[0] == e:
                self._wait(e, e, c.w[1])
        ins = fn(self.eng[e])
        self.cnt[e] += 1
        ins.then_inc(self.sem[e], 1)
        self._mark(e, self.cnt[e], R, W)
        self.ninstr += 1
        return ins

    def dma(self, q, out, in_, R=(), W=(), **kw):
        R = _cells(R)
        W = _cells(W)
        sems, idx = self.dq[q]
        key = sems[idx % len(sems)]
        self.dq[q][1] = idx + 1
        if self.cnt[key] > 0:
            self._wait(q, key, self.cnt[key])
        self._deps(q, R, W)
        ins = self.eng[q].dma_start(out=out, in_=in_, **kw)
        self.cnt[key] += 16
        ins.then_inc(self.sem[key], 16)
        self._mark(key, self.cnt[key], R, W)
        self.ninstr += 1
        return ins

    def idma(self, out, in_, idx_ap, R=(), W=()):
        q = "pool"
        R = _cells(R)
        W = _cells(W)
        sems, idx = self.dq[q]
        key = sems[idx % len(sems)]
        self.dq[q][1] = idx + 1
        if self.cnt[key] > 0:
            self._wait(q, key, self.cnt[key])
        self._deps(q, R, W)
        ins = self.nc.gpsimd.indirect_dma_start(out=out, out_offset=None, in_=in_, in_offset=bass.IndirectOffsetOnAxis(ap=idx_ap, axis=0))
        self.cnt[key] += 16
        ins.then_inc(self.sem[key], 16)
        self._mark(key, self.cnt[key], R, W)
        self.ninstr += 1
        return ins

    def cc(self, kind, ins_, outs, R=(), W=(), ncores=8):
        q = "pool"
        R = _cells(R)
        W = _cells(W)
        sems, idx = self.dq[q]
        key = sems[idx % len(sems)]
        self.dq[q][1] = idx + 1
        if self.cnt[key] > 0:
            self._wait(q, key, self.cnt[key])
        self._deps(q, R, W)
        ins = self.nc.gpsimd.collective_compute(kind, op=ALU.bypass, replica_groups=[list(range(ncores))], ins=ins_, outs=outs)
        self.cnt[key] += 16
        ins.then_inc(self.sem[key], 16)
        self._mark(key, self.cnt[key], R, W)
        self.ninstr += 1
        return ins

    def barrier(self, sp=False):
        for e in (("pe", "act", "dve", "pool", "sp") if sp else ("pe", "act", "dve", "pool")):
            for k, v in self.cnt.items():
                if v > 0 and k != e:
                    self._wait(e, k, v)

    def finish(self):
        for k, v in self.cnt.items():
            if v > 0 and k != "sp":
                self._wait("sp", k, v)


class Arena:
    def __init__(self, S, name, nbytes):
        self.S = S
        self.t = S.stack.enter_context(S.nc.sbuf_tensor("sb_" + name, [128, nbytes // 4], F32))
        self.n = nbytes // 4
        self.off = 0

    def reset(self):
        self.S.barrier()
        self.off = 0

    def alloc(self, shape, dt=F32, ncells=1):
        n = 1
        for s in shape:
            n *= s
        words = n if dt == F32 else (n + 1) // 2
        assert self.off + words <= self.n, ("arena overflow", self.off, words, self.n)
        ap = self.t[:, self.off:self.off + words]
        self.off += words
        if dt != F32:
            ap = ap.bitcast(dt)
        if len(shape) == 2:
            ap = ap.rearrange("p (a b) -> p a b", a=shape[0])
        elif len(shape) == 3:
            ap = ap.rearrange("p (a b c) -> p a b c", a=shape[0], b=shape[1])
        return Buf(ap, ncells)


def _const_tables():
    p = np.arange(128)
    ident = np.eye(128, dtype=np.float32)
    ones = np.ones((128, 128), np.float32)
    same = (p[:, None] // 64) == (p[None, :] // 64)
    tri2 = (same & (p[:, None] <= p[None, :])).astype(np.float32)
    trirev2 = (same & (p[:, None] > p[None, :])).astype(np.float32)
    caus = (p[:, None] <= p[None, :]).astype(np.float32)
    blk64 = same.astype(np.float32)
    same8 = (p[:, None] // 8) == (p[None, :] // 8)
    tri8 = (same8 & (p[:, None] <= p[None, :])).astype(np.float32)
    trev8 = (same8 & (p[:, None] > p[None, :])).astype(np.float32)
    n64 = np.arange(64)
    pairm = ((p[:, None] // 2) == n64[None, :]).astype(np.float32) / 256.0
    pairmt = np.zeros((128, 128), np.float32)
    pairmt[:64, :] = ((p[None, :] // 2) == n64[:, None]).astype(np.float32)
    rowm = ((p[:, None] // 8) == np.arange(16)[None, :]).astype(np.float32)
    q8 = np.arange(8)
    ownm = (((p[:, None, None] // 8) == np.arange(16)[None, :, None]) & ((p[:, None, None] % 8) <= q8[None, None, :])).astype(np.float32).reshape(128, 128)
    hm = ((p[:, None] // 8) == np.arange(4)[None, :]).astype(np.float32)
    qsel = ((p[:, None] % 8) == q8[None, :]).astype(np.float32)
    return np.concatenate([ident, ones, tri2, trirev2, caus, blk64, tri8, trev8, pairm, pairmt, rowm, ownm, hm, qsel], axis=1)


C_ID, C_ONES, C_TRI2, C_TREV, C_CAUS, C_BLK, C_TRI8, C_TREV8 = [i * 128 for i in range(8)]
C_PAIRM = 1024
C_PAIRMT = C_PAIRM + 64
C_ROWM = C_PAIRMT + 128
C_OWNM = C_ROWM + 16
C_HM = C_OWNM + 128
C_QSEL = C_HM + 4
C_TOT = C_QSEL + 8


def _fm(v):
    return np.ascontiguousarray(np.asarray(v, np.float32).reshape(-1, 128).T)


PAR_L = {}


def _par_layout():
    off = 0
    for name, w in (("g1", 8), ("gm", 8), ("g2", 8), ("pscale", 2), ("cw", 62), ("cb", 2), ("clg", 2), ("clb", 2),
                    ("hnorm", 2), ("lbfm", 2), ("gq", 256), ("gk", 256), ("lbtm", 256)):
        PAR_L[name] = (off, w)
        off += w
    return off


PAR_W = _par_layout()
PAR_G = 2 * PAR_W
PG_EPS, PG_INVW, PG_ONE, PG_INVC = PAR_G, PAR_G + 1, PAR_G + 3, PAR_G + 4
PAR_TOT = PAR_G + 4 + 32


def _params_table(inp):
    T = np.zeros((128, PAR_TOT), np.float32)
    for l in range(2):
        b = l * PAR_W

        def put(name, arr):
            o, w = PAR_L[name]
            T[:, b + o:b + o + w] = arr

        put("g1", _fm(inp["ln_ffn1"][l]))
        put("gm", _fm(inp["ln_mix"][l]))
        put("g2", _fm(inp["ln_ffn2"][l]))
        put("pscale", _fm(inp["pool_scale"][l]))
        cw = np.asarray(inp["conv_w"][l], np.float32)
        put("cw", np.ascontiguousarray(cw.T.reshape(2, 128, 31).transpose(1, 0, 2)).reshape(128, 62))
        put("cb", _fm(inp["conv_b"][l]))
        put("clg", _fm(inp["conv_ln_g"][l]))
        put("clb", _fm(inp["conv_ln_b"][l]))
        put("hnorm", np.tile(np.asarray(inp["hgrn_norm"][l], np.float32).reshape(1, 64), (2, 1)).reshape(128, 1).repeat(2, axis=1))
        put("lbfm", _fm(inp["hgrn_lower_bounds"][l]))
        put("gq", np.tile(np.asarray(inp["q_norm"][l], np.float32).reshape(1, 64), (128, 4)))
        put("gk", np.tile(np.asarray(inp["k_norm"][l], np.float32).reshape(1, 64), (128, 4)))
        put("lbtm", np.tile(np.asarray(inp["hgrn_lower_bounds"][l], np.float32).reshape(1, 256), (128, 1)))
    T[:, PG_EPS] = EPS
    w_of = np.array([[2, 8], [4, 16]], np.float32)
    pw = w_of[(np.arange(128) // 64)]
    T[:, PG_INVW:PG_INVW + 2] = 1.0 / pw
    T[:, PG_ONE] = 1.0
    t = np.arange(16, dtype=np.float32)
    invc = 1.0 / np.minimum(pw[:, :, None], t[None, None, :] + 1.0)
    T[:, PG_INVC:PG_INVC + 32] = invc.reshape(128, 32)
    return T


def _rope_tables(pos):
    half = 8
    inv = (500000.0 ** (-np.arange(half, dtype=np.float32) * 2.0 / 16)).astype(np.float32)
    ang = pos.astype(np.float32)[:, None] * inv[None, :]
    cos = np.cos(ang).astype(np.float32)
    sin = np.sin(ang).astype(np.float32)
    return np.tile(cos, (1, 4)), np.tile(sin, (1, 4))


class Builder:
    def __init__(self, SEQ, TT, NP=128, NPOOL=5120):
        self.SEQ, self.TT = SEQ, TT
        self.NP, self.NPOOL = NP, NPOOL
        self.NT = SEQ // TT
        self.NST = TT // 128
        self.NQT = SEQ // 128
        self.NBLK = max(1, SEQ // 256)

    def mm(self, out, lhsT, rhs, start, stop, R, W, **kw):
        self.S.op("pe", lambda e: e.matmul(out, lhsT=lhsT, rhs=rhs, start=start, stop=stop, **kw), R=R, W=W)

    def tr(self, out, in_, R, W, n=128):
        idn = self.cst.t[0:n, C_ID:C_ID + n]
        self.S.op("pe", lambda e: e.transpose(out=out, in_=in_, identity=idn), R=list(R) + [self.cst], W=W)

    def act(self, out, in_, func, R, W, bias=None, scale=1.0, accum_out=None):
        kw = {}
        if bias is not None:
            kw["bias"] = bias
        if accum_out is not None:
            kw["accum_out"] = accum_out
        self.S.op("act", lambda e: e.activation(out=out, in_=in_, func=func, scale=scale, **kw), R=R, W=W)

    def tt(self, eng, out, in0, in1, op, R, W):
        self.S.op(eng, lambda e: e.tensor_tensor(out=out, in0=in0, in1=in1, op=op), R=R, W=W)

    def ts(self, eng, out, in0, s1, s2, op0, op1, R, W, SS=()):
        if op1 is None:
            self.S.op(eng, lambda e: e.tensor_scalar(out=out, in0=in0, scalar1=s1, scalar2=None, op0=op0), R=R, W=W, SS=SS)
        else:
            self.S.op(eng, lambda e: e.tensor_scalar(out=out, in0=in0, scalar1=s1, scalar2=s2, op0=op0, op1=op1), R=R, W=W, SS=SS)

    def stt(self, eng, out, in0, scalar, in1, op0, op1, R, W, SS=()):
        self.S.op(eng, lambda e: e.scalar_tensor_tensor(out=out, in0=in0, scalar=scalar, in1=in1, op0=op0, op1=op1), R=R, W=W, SS=SS)

    def cp(self, eng, out, in_, R, W):
        if eng == "act":
            self.S.op("act", lambda e: e.copy(out=out, in_=in_), R=R, W=W)
        else:
            self.S.op(eng, lambda e: e.tensor_copy(out=out, in_=in_), R=R, W=W)

    def ps(self):
        b = self.psb[self.psi % 6]
        self.psi += 1
        return b

    def par(self, l, name, j=0, n=1):
        o, w = PAR_L[name]
        c = l * PAR_W + o + j
        return self.prm.t[:, c:c + n]

    def wload(self, src, a, b, key):
        n = a * b
        if key not in self.wmap:
            idx = len(self.wmap)
            cell = Buf(None)
            self.wmap[key] = (idx, cell)
            i = self.wi
            self.wi += 1
            stg = self.wstg[i % len(self.wstg)]
            wb = self.wbf[i % len(self.wbf)]
            self.S.dma("sp", stg.t[:, 0:n].rearrange("p (a b) -> p a b", a=a), src, W=[stg])
            self.S.op("pool", lambda e: e.tensor_copy(out=wb.t[:, 0:n], in_=stg.t[:, 0:n]), R=[stg], W=[wb])
            self.S.dma("pool", self.wsc[idx, :, 0:n], wb.t[:, 0:n], R=[wb], W=[cell])
        else:
            idx, cell = self.wmap[key]
            wb = self.wring[self.wj % len(self.wring)]
            self.wj += 1
            self.S.dma("sp", wb.t[:, 0:n], self.wsc[idx, :, 0:n], R=[cell], W=[wb])
        return wb, wb.t[:, 0:n].rearrange("p (a b) -> p a b", a=a)

    def end_first_pass(self):
        if self.first_pass_done:
            return
        self.first_pass_done = True
        self.S.barrier(sp=True)
        extra = []
        for st in self.wstg:
            for hhalf in range(2):
                extra.append(Buf(st.t[:, hhalf * 1024:(hhalf + 1) * 1024].bitcast(BF16)))
        self.wring = list(self.wbf) + extra

    def rmsnorm(self, l, gname):
        S, TT = self.S, self.TT
        xT, hT, sq = self.xT, self.hT, self.sqb
        for c in range(8):
            self.tt("pool", sq.t[:, c, :], xT.t[:, c, :], xT.t[:, c, :], ALU.mult, R=xT.c(c), W=sq.c(c))
        ps = self.ps()
        for c in range(8):
            self.mm(ps.t[:, 0:TT], self.ones_bf.t[:, :], sq.t[:, c, :], c == 0, c == 7, R=[sq.c(c), self.ones_bf], W=[ps])
        rs = self.rstd
        self.act(rs.t[:, 0:TT], ps.t[:, 0:TT], AF.Sqrt, R=[ps, self.prm], W=[rs], bias=self.prm.t[:, PG_EPS:PG_EPS + 1], scale=1.0 / D)
        self.S.op("dve", lambda e: e.reciprocal(out=rs.t[:, 0:TT], in_=rs.t[:, 0:TT]), R=[rs], W=[rs])
        for c in range(8):
            self.stt("dve", hT.t[:, c, :], xT.t[:, c, :], self.par(l, gname, c), rs.t[:, 0:TT], ALU.mult, ALU.mult,
                     R=[xT.c(c), rs, self.prm], W=hT.c(c))

    def ffn(self, l, Wg, Wu, Wd):
        S, TT = self.S, self.TT
        xT, hT = self.xT, self.hT
        self.ar.reset()
        aT = self.ar.alloc([22, TT], BF16, ncells=22)
        sg = [self.ar.alloc([TT], F32), self.ar.alloc([TT], F32)]
        for j2 in range(11):
            wgb, wg = self.wload(Wg[l][:, j2 * 256:(j2 + 1) * 256].rearrange("(c p) n -> p c n", p=128), 8, 256, (id(Wg), l, "g", j2))
            wub, wu = self.wload(Wu[l][:, j2 * 256:(j2 + 1) * 256].rearrange("(c p) n -> p c n", p=128), 8, 256, (id(Wu), l, "u", j2))
            for jj in range(2):
                j = 2 * j2 + jj
                pg, pu = self.ps(), self.ps()
                for c in range(8):
                    self.mm(pg.t[:, 0:TT], wg[:, c, jj * 128:(jj + 1) * 128], hT.t[:, c, :], c == 0, c == 7, R=[wgb, hT.c(c)], W=[pg])
                for c in range(8):
                    self.mm(pu.t[:, 0:TT], wu[:, c, jj * 128:(jj + 1) * 128], hT.t[:, c, :], c == 0, c == 7, R=[wub, hT.c(c)], W=[pu])
                s = sg[j % 2]
                self.act(s.t[:, :], pg.t[:, 0:TT], AF.Silu, R=[pg], W=[s])
                self.tt("dve", aT.t[:, j, :], s.t[:, :], pu.t[:, 0:TT], ALU.mult, R=[s, pu], W=aT.c(j))
        for m in range(8):
            py = self.ps()
            for hf in range(2):
                wdb, wd = self.wload(Wd[l][hf * 1408:(hf + 1) * 1408, m * 128:(m + 1) * 128].rearrange("(j p) n -> p j n", p=128), 11, 128, (id(Wd), l, "d", m, hf))
                for jj in range(11):
                    j = hf * 11 + jj
                    self.mm(py.t[:, 0:TT], wd[:, jj, :], aT.t[:, j, :], j == 0, j == 21, R=[wdb, aT.c(j)], W=[py])
            self.stt("dve", xT.t[:, m, :], py.t[:, 0:TT], 0.5, xT.t[:, m, :], ALU.mult, ALU.add, R=[py, xT.c(m)], W=xT.c(m))

    def proj_fm(self, l, col0, dst, dcol0=0, post=None):
        TT, hT = self.TT, self.hT
        wb, w = self.wload(self.w_in[l][:, col0:col0 + 256].rearrange("(c p) n -> p c n", p=128), 8, 256, ("in", l, col0))
        for cc in range(2):
            ps = self.ps()
            for c in range(8):
                self.mm(ps.t[:, 0:TT], w[:, c, cc * 128:(cc + 1) * 128], hT.t[:, c, :], c == 0, c == 7, R=[wb, hT.c(c)], W=[ps])
            if post is not None:
                post(cc, ps)
            else:
                self.cp("act", dst.t[:, cc, dcol0:dcol0 + TT], ps.t[:, 0:TT], R=[ps], W=[dst])

    def proj_tm(self, l, col0, dst):
        hT = self.hT
        wb, w = self.wload(self.w_in[l][:, col0:col0 + 256].rearrange("(c p) n -> p c n", p=128), 8, 256, ("in", l, col0))
        for st in range(self.NST):
            ps = self.ps()
            for c in range(8):
                self.mm(ps.t[:, 0:256], hT.t[:, c, st * 128:(st + 1) * 128], w[:, c, :], c == 0, c == 7, R=[wb, hT.c(c)], W=[ps])
            self.cp("act", dst.t[:, st, :], ps.t[:, 0:256], R=[ps], W=dst.c(st))

    def store_rows_fm(self, src_ap_fn, src_buf, n, dram_ap):
        ps = self.ps()
        for c in range(2):
            self.tr(ps.t[0:n, c * 128:(c + 1) * 128], src_ap_fn(c), R=[src_buf], W=[ps])
        ob = self.ar.alloc([256], F32)
        self.cp("act", ob.t[0:n, :], ps.t[0:n, 0:256], R=[ps], W=[ob])
        self.S.dma("pool", dram_ap, ob.t[0:n, :], R=[ob])

    def pool_mix(self, l, tile, last):
        TT = self.TT
        UP = self.UP[l]
        W_ = TT + 15
        s2 = self.ar.alloc([2, W_], F32)
        s4 = self.ar.alloc([2, W_], F32)
        s8 = self.ar.alloc([2, W_], F32)
        s16 = self.ar.alloc([2, W_], F32)
        pooled = self.ar.alloc([2, TT], BF16)
        self.tt("dve", s2.t[:, :, 0:W_ - 1], UP.t[:, :, 1:W_], UP.t[:, :, 0:W_ - 1], ALU.add, R=[UP], W=[s2])
        self.tt("dve", s4.t[:, :, 0:W_ - 3], s2.t[:, :, 2:W_ - 1], s2.t[:, :, 0:W_ - 3], ALU.add, R=[s2], W=[s4])
        self.tt("dve", s8.t[:, :, 0:W_ - 7], s4.t[:, :, 4:W_ - 3], s4.t[:, :, 0:W_ - 7], ALU.add, R=[s4], W=[s8])
        self.tt("dve", s16.t[:, :, 0:W_ - 15], s8.t[:, :, 8:W_ - 7], s8.t[:, :, 0:W_ - 15], ALU.add, R=[s8], W=[s16])
        srcs = {(0, 0): (s2, 14), (1, 0): (s4, 12), (0, 1): (s8, 8), (1, 1): (s16, 0)}
        for (hf, c), (sb, o) in srcs.items():
            r = slice(hf * 64, hf * 64 + 64)
            self.stt("dve", pooled.t[r, c, :], sb.t[r, c, o:o + TT], self.prm.t[r, PG_INVW + c:PG_INVW + c + 1], UP.t[r, c, 15:15 + TT],
                     ALU.mult, ALU.subtract, R=[sb, UP, self.prm], W=[pooled])
            if tile == 0:
                tmp = self.ar.alloc([16], F32)
                self.tt("dve", tmp.t[r, :], sb.t[r, c, o:o + 16], self.prm.t[r, PG_INVC + c * 16:PG_INVC + c * 16 + 16], ALU.mult, R=[sb, self.prm], W=[tmp])
                self.tt("dve", pooled.t[r, c, 0:16], tmp.t[r, :], UP.t[r, c, 15:31], ALU.subtract, R=[tmp, UP], W=[pooled])
        for c in range(2):
            ps = self.ps()
            self.mm(ps.t[:, 0:TT], self.pwbd.t[:, l, c, :], pooled.t[:, c, :], True, True, R=[self.pwbd, pooled], W=[ps])
            self.ts("dve", self.ycat.t[:, c, :], ps.t[:, 0:TT], self.par(l, "pscale", c), None, ALU.mult, None, R=[ps, self.prm], W=self.ycat.c(c))
        if last:
            self.store_rows_fm(lambda c: UP.t[:, c, TT:TT + 15], UP, 15, self.o_pool[l])
        self.cp("pool", UP.t[:, :, 0:15], UP.t[:, :, TT:TT + 15], R=[UP], W=[UP])

    def conv_mix(self, l, tile, last):
        TT = self.TT
        G = self.G[l]
        acc = self.ar.alloc([2, TT], F32, ncells=2)
        sq = self.ar.alloc([2, TT], F32, ncells=2)
        for c, eng in ((0, "dve"), (1, "dve")):
            self.ts(eng, acc.t[:, c, :], G.t[:, c, 0:TT], self.par(l, "cw", c * 31), self.par(l, "cb", c), ALU.mult, ALU.add, R=[G, self.prm], W=acc.c(c))
            for j in range(1, 31):
                self.stt(eng, acc.t[:, c, :], G.t[:, c, j:j + TT], self.par(l, "cw", c * 31 + j), acc.t[:, c, :], ALU.mult, ALU.add,
                         R=[G, self.prm, acc.c(c)], W=acc.c(c))
            self.tt(eng, sq.t[:, c, :], acc.t[:, c, :], acc.t[:, c, :], ALU.mult, R=acc.c(c), W=sq.c(c))
        self.conv_tail(l, acc, sq)
        if last:
            self.store_rows_fm(lambda c: G.t[:, c, TT:TT + 30], G, 30, self.o_conv[l])
        self.cp("pool", G.t[:, :, 0:30], G.t[:, :, TT:TT + 30], R=[G], W=[G])

    def conv_tail(self, l, acc, sq):
        TT = self.TT
        psm, psq = self.ps(), self.ps()
        onesf = self.cst.t[:, C_ONES:C_ONES + 128]
        for c in range(2):
            self.mm(psm.t[:, 0:TT], onesf, acc.t[:, c, :], c == 0, c == 1, R=[self.cst, acc.c(c)], W=[psm])
        for c in range(2):
            self.mm(psq.t[:, 0:TT], onesf, sq.t[:, c, :], c == 0, c == 1, R=[self.cst, sq.c(c)], W=[psq])
        mean = self.ar.alloc([TT], F32)
        m2 = self.ar.alloc([TT], F32)
        rstd = self.ar.alloc([TT], F32)
        self.S.op("act", lambda e: e.mul(out=mean.t[:, :], in_=psm.t[:, 0:TT], mul=1.0 / 256), R=[psm], W=[mean])
        self.tt("dve", m2.t[:, :], mean.t[:, :], mean.t[:, :], ALU.mult, R=[mean], W=[m2])
        self.stt("dve", rstd.t[:, :], psq.t[:, 0:TT], 1.0 / 256, m2.t[:, :], ALU.mult, ALU.subtract, R=[psq, m2], W=[rstd])
        self.act(rstd.t[:, :], rstd.t[:, :], AF.Sqrt, R=[rstd, self.prm], W=[rstd], bias=self.prm.t[:, PG_EPS:PG_EPS + 1])
        self.S.op("dve", lambda e: e.reciprocal(out=rstd.t[:, :], in_=rstd.t[:, :]), R=[rstd], W=[rstd])
        zs = self.ar.alloc([2, TT], BF16, ncells=2)
        for c in range(2):
            self.tt("dve", acc.t[:, c, :], acc.t[:, c, :], mean.t[:, :], ALU.subtract, R=[acc.c(c), mean], W=acc.c(c))
            self.tt("dve", acc.t[:, c, :], acc.t[:, c, :], rstd.t[:, :], ALU.mult, R=[acc.c(c), rstd], W=acc.c(c))
            self.act(zs.t[:, c, :], acc.t[:, c, :], AF.Silu, R=[acc.c(c), self.prm], W=zs.c(c), bias=self.par(l, "clb", c), scale=self.par(l, "clg", c))
        for co in range(2):
            ps = self.ps()
            for c in range(2):
                self.mm(ps.t[:, 0:TT], self.pw.t[:, l, c, co * 128:(co + 1) * 128], zs.t[:, c, :], c == 0, c == 1, R=[self.pw, zs.c(c)], W=[ps])
            self.cp("act", self.ycat.t[:, 4 + co, :], ps.t[:, 0:TT], R=[ps], W=self.ycat.c(4 + co))

    def hgrn_mix(self, l, tile, last):
        TT = self.TT
        A = self.ar
        HQ, HF, HG, HFt, HIt = self.HQ, self.HF, self.HG, self.HFt, self.HIt
        Sm = self.Sst[l]
        tri2 = self.cst.t[:, C_TRI2:C_TRI2 + 128]
        trev = self.cst.t[:, C_TREV:C_TREV + 128]
        one = self.prm.t[:, PG_ONE:PG_ONE + 1]
        for st in range(self.NST):
            cs = slice(st * 128, (st + 1) * 128)
            mark = A.off
            sig = A.alloc([256], F32)
            logf = A.alloc([256], F32)
            kin = A.alloc([256], F32)
            self.act(sig.t[:, :], HFt.t[:, st, :], AF.Sigmoid, R=HFt.c(st), W=[sig])
            self.tt("dve", sig.t[:, :], sig.t[:, :], self.omltm.t[:, l, :], ALU.mult, R=[sig, self.omltm], W=[sig])
            self.tt("dve", sig.t[:, :], sig.t[:, :], self.lbtm.t[:, l, :], ALU.add, R=[sig, self.lbtm], W=[sig])
            self.act(logf.t[:, :], sig.t[:, :], AF.Ln, R=[sig], W=[logf])
            self.ts("dve", kin.t[:, :], sig.t[:, :], -1.0, 1.0, ALU.mult, ALU.add, R=[sig], W=[kin])
            prev = self.ps()
            self.mm(prev.t[:, 0:256], trev, logf.t[:, :], True, True, R=[self.cst, logf], W=[prev])
            pbc = self.ps()
            for kc in range(2):
                self.mm(pbc.t[:, kc * 128:(kc + 1) * 128], logf.t[:, kc * 128:(kc + 1) * 128], tri2, True, True, R=[self.cst, logf], W=[pbc])
            er = A.alloc([256], F32)
            kh = A.alloc([256], BF16)
            vb = A.alloc([256], BF16)
            self.act(er.t[:, :], prev.t[:, 0:256], AF.Exp, R=[prev], W=[er])
            self.tt("dve", kh.t[:, :], kin.t[:, :], er.t[:, :], ALU.mult, R=[kin, er], W=[kh])
            self.cp("pool", vb.t[:, :], HIt.t[:, st, :], R=HIt.c(st), W=[vb])
            E = A.alloc([2, 128], F32)
            Ei = A.alloc([2, 128], F32)
            self.act(E.t[:, :, :], pbc.t[:, 0:256].rearrange("p (a b) -> p a b", a=2), AF.Exp, R=[pbc], W=[E])
            self.act(Ei.t[:, :, :], pbc.t[:, 0:256].rearrange("p (a b) -> p a b", a=2), AF.Exp, R=[pbc], W=[Ei], scale=-1.0)
            sT = A.alloc([2, 128], F32)
            self.act(sT.t[:, :, :], HF.t[:, :, cs], AF.Sigmoid, R=[HF], W=[sT])
            for c in range(2):
                self.ts("dve", sT.t[:, c, :], sT.t[:, c, :], self.nomlfm.t[:, l, c:c + 1], self.omlfm.t[:, l, c:c + 1], ALU.mult, ALU.add,
                        R=[sT, self.nomlfm, self.omlfm], W=[sT])
            qt_ = A.alloc([2, 128], BF16)
            kt_ = A.alloc([2, 128], BF16)
            self.tt("dve", qt_.t[:, :, :], HQ.t[:, :, cs], E.t[:, :, :], ALU.mult, R=[HQ, E], W=[qt_])
            self.tt("dve", kt_.t[:, :, :], sT.t[:, :, :], Ei.t[:, :, :], ALU.mult, R=[sT, Ei], W=[kt_])
            patt = [self.ps(), self.ps()]
            for h in range(4):
                pr, hf, r0 = h // 2, h % 2, (h % 2) * 64
                self.mm(patt[hf].t[:, pr * 128:(pr + 1) * 128], kt_.t[r0:r0 + 64, pr, :], qt_.t[r0:r0 + 64, pr, :], True, True, R=[kt_, qt_], W=[patt[hf]])
            att = A.alloc([4, 128], BF16)
            for h in range(4):
                pr, hf = h // 2, h % 2
                self.tt("dve", att.t[:, h, :], patt[hf].t[:, pr * 128:(pr + 1) * 128], tri2, ALU.mult, R=[patt[hf], self.cst], W=[att])
            pU = [self.ps(), self.ps()]
            for ci in range(2):
                for pr in range(2):
                    k0 = pr * 128
                    self.mm(pU[ci].t[:, k0:k0 + 128], kh.t[ci * 64:(ci + 1) * 64, pr * 128:(pr + 1) * 128], vb.t[ci * 64:(ci + 1) * 64, pr * 128:(pr + 1) * 128],
                            True, True, R=[kh, vb], W=[pU[ci]])
            sbf = [A.alloc([2, 64], BF16), A.alloc([2, 64], BF16)]
            self.cp("act", sbf[0].t[:, :, :], Sm.t[:, :, :], R=[Sm], W=[sbf[0]])
            for ci in range(2):
                for pr in range(2):
                    for hf in range(2):
                        r = slice(hf * 64, hf * 64 + 64)
                        k0 = pr * 128 + hf * 64
                        self.stt("dve", Sm.t[r, pr, :], Sm.t[r, pr, :], E.t[r, pr, ci * 64 + 63:ci * 64 + 64], pU[ci].t[r, k0:k0 + 64], ALU.mult, ALU.add,
                                 R=[Sm, E, pU[ci]], W=[Sm], SS=[E])
                if ci == 0:
                    self.cp("act", sbf[1].t[:, :, :], Sm.t[:, :, :], R=[Sm], W=[sbf[1]])
            po = [self.ps(), self.ps()]
            for h in range(4):
                pr, hf, r0 = h // 2, h % 2, (h % 2) * 64
                self.mm(po[hf].t[r0:r0 + 64, pr * 128:(pr + 1) * 128], vb.t[:, h * 64:(h + 1) * 64], att.t[:, h, :], True, False, R=[vb, att], W=[po[hf]])
                for ci in range(2):
                    self.mm(po[hf].t[r0:r0 + 64, pr * 128 + ci * 64:pr * 128 + ci * 64 + 64], sbf[ci].t[r0:r0 + 64, pr, :], qt_.t[r0:r0 + 64, pr, ci * 64:(ci + 1) * 64],
                            False, ci == 1, R=[sbf[ci], qt_], W=[po[hf]], skip_group_check=True)
            O = A.alloc([2, 128], F32)
            sqo = A.alloc([2, 128], BF16)
            for hf in range(2):
                r = slice(hf * 64, hf * 64 + 64)
                self.cp("act", O.t[r, :, :], po[hf].t[r, 0:256].rearrange("p (a b) -> p a b", a=2), R=[po[hf]], W=[O])
            self.tt("pool", sqo.t[:, :, :], O.t[:, :, :], O.t[:, :, :], ALU.mult, R=[O], W=[sqo])
            pss = self.ps()
            for pr in range(2):
                self.mm(pss.t[:, pr * 128:(pr + 1) * 128], self.blk_bf.t[:, :], sqo.t[:, pr, :], True, True, R=[self.blk_bf, sqo], W=[pss])
            rs = A.alloc([2, 128], F32)
            self.act(rs.t[:, :, :], pss.t[:, 0:256].rearrange("p (a b) -> p a b", a=2), AF.Sqrt, R=[pss, self.prm], W=[rs],
                     bias=self.prm.t[:, PG_EPS:PG_EPS + 1], scale=1.0 / 64)
            self.S.op("dve", lambda e: e.reciprocal(out=rs.t[:, :, :], in_=rs.t[:, :, :]), R=[rs], W=[rs])
            sgt = A.alloc([2, 128], F32)
            self.act(sgt.t[:, :, :], HG.t[:, :, cs], AF.Silu, R=[HG], W=[sgt])
            self.tt("dve", O.t[:, :, :], O.t[:, :, :], rs.t[:, :, :], ALU.mult, R=[O, rs], W=[O])
            for pr in range(2):
                self.stt("dve", self.ycat.t[:, 6 + pr, cs], O.t[:, pr, :], self.par(l, "hnorm", pr), sgt.t[:, pr, :], ALU.mult, ALU.mult,
                         R=[O, sgt, self.prm], W=self.ycat.c(6 + pr))
            self.S.barrier()
            A.off = mark
        if last:
            self.S.dma("pool", self.o_hgrn[l].rearrange("(pr hf) k v -> (hf k) pr v", hf=2), Sm.t[:, :, :], R=[Sm])

    def qk_norm_rope(self, l, st, cos_ap, sin_ap, csbufs):
        A = self.ar
        QN = A.alloc([256], F32)
        KN = A.alloc([256], F32)
        for src, dst, gname in ((self.Qt, QN, "gq"), (self.Kt, KN, "gk")):
            sq = A.alloc([256], F32)
            ss = A.alloc([4], F32)
            self.tt("pool", sq.t[:, :], src.t[:, st, :], src.t[:, st, :], ALU.mult, R=src.c(st), W=[sq])
            for h in range(4):
                self.S.op("dve", lambda e: e.tensor_reduce(out=ss.t[:, h:h + 1], in_=sq.t[:, h * 64:(h + 1) * 64], axis=AX.X, op=ALU.add), R=[sq], W=[ss])
            self.act(ss.t[:, :], ss.t[:, :], AF.Sqrt, R=[ss, self.prm], W=[ss], bias=self.prm.t[:, PG_EPS:PG_EPS + 1], scale=1.0 / 64)
            self.S.op("dve", lambda e: e.reciprocal(out=ss.t[:, :], in_=ss.t[:, :]), R=[ss], W=[ss])
            for h in range(4):
                hs = slice(h * 64, (h + 1) * 64)
                self.stt("dve", dst.t[:, hs], src.t[:, st, hs], ss.t[:, h:h + 1], self.par(l, gname, h * 64, 64), ALU.mult, ALU.mult,
                         R=[src.c(st), ss, self.prm], W=[dst], SS=[ss])
            v3 = dst.t[:, :].rearrange("p (h d) -> p h d", h=4)
            x1, x2 = v3[:, :, 0:8], v3[:, :, 8:16]
            cos = cos_ap.rearrange("p (h d) -> p h d", h=4)
            sin = sin_ap.rearrange("p (h d) -> p h d", h=4)
            tmp = [A.alloc([4, 8], F32) for _ in range(4)]
            self.tt("dve", tmp[0].t[:, :, :], x1, cos, ALU.mult, R=[dst] + csbufs, W=[tmp[0]])
            self.tt("dve", tmp[1].t[:, :, :], x2, sin, ALU.mult, R=[dst] + csbufs, W=[tmp[1]])
            self.tt("dve", tmp[2].t[:, :, :], x2, cos, ALU.mult, R=[dst] + csbufs, W=[tmp[2]])
            self.tt("dve", tmp[3].t[:, :, :], x1, sin, ALU.mult, R=[dst] + csbufs, W=[tmp[3]])
            self.tt("dve", x1, tmp[0].t[:, :, :], tmp[1].t[:, :, :], ALU.subtract, R=[tmp[0], tmp[1]], W=[dst])
            self.tt("dve", x2, tmp[2].t[:, :, :], tmp[3].t[:, :, :], ALU.add, R=[tmp[2], tmp[3]], W=[dst])
        return QN, KN

    def moba_mix(self, l, tile, last):
        TT = self.TT
        A = self.ar
        KT, VX, KP, KM = self.KT[l], self.VX[l], self.KP[l], self.KM[l]
        caus = self.caus_bf.t[:, :]
        for st in range(self.NST):
            qt = tile * self.NST + st
            cs = slice(st * 128, (st + 1) * 128)
            mark = A.off
            QN, KN = self.qk_norm_rope(l, st, self.cos.t[:, qt, :], self.sin.t[:, qt, :], [self.cos, self.sin])
            self.S.dma("pool", self.o_k[l, qt * 128:(qt + 1) * 128, :], KN.t[:, :], R=[KN])
            self.S.dma("pool", self.o_v[l, qt * 128:(qt + 1) * 128, :], self.Vt.t[:, st, :], R=self.Vt.c(st))
            if SUB < 2:
                self.S.barrier()
                A.off = mark
                continue
            QTb = A.alloc([2, 128], BF16)
            QTf = A.alloc([2, 128], F32)
            pq = self.ps()
            pk = self.ps()
            for pr in range(2):
                self.tr(pq.t[:, pr * 128:(pr + 1) * 128], QN.t[:, pr * 128:(pr + 1) * 128], R=[QN], W=[pq])
                self.tr(pk.t[:, pr * 128:(pr + 1) * 128], KN.t[:, pr * 128:(pr + 1) * 128], R=[KN], W=[pk])
            for pr in range(2):
                self.cp("dve", QTb.t[:, pr, :], pq.t[:, pr * 128:(pr + 1) * 128], R=[pq], W=[QTb])
                self.cp("dve", QTf.t[:, pr, :], pq.t[:, pr * 128:(pr + 1) * 128], R=[pq], W=[QTf])
                self.cp("dve", KT.t[:, pr, qt * 128:(qt + 1) * 128], pk.t[:, pr * 128:(pr + 1) * 128], R=[pk], W=[KT.c(qt)])
                self.S.op("dve", lambda e: e.tensor_reduce(out=KP.t[:, pr, qt:qt + 1], in_=pk.t[:, pr * 128:(pr + 1) * 128], axis=AX.X, op=ALU.add),
                          R=[pk], W=[KP])
            for h in range(4):
                self.cp("dve", VX.t[:, qt, h, 0:64], self.Vt.t[:, st, h * 64:(h + 1) * 64], R=self.Vt.c(st), W=VX.c(qt))
            if qt % 2 == 1:
                n = qt // 2
                self.tt("dve", KM.t[:, :, n:n + 1], KP.t[:, :, qt - 1:qt], KP.t[:, :, qt:qt + 1], ALU.add, R=[KP], W=[KM])
                self.ts("dve", KM.t[:, :, n:n + 1], KM.t[:, :, n:n + 1], 1.0 / 256, None, ALU.mult, None, R=[KM], W=[KM])
            if SUB < 3:
                self.S.barrier()
                A.off = mark
                continue
            ob = qt // 2
            dense = ob <= 3
            SEL = None
            if not dense:
                pg = [self.ps(), self.ps()]
                for h in range(4):
                    pr, hf, r0 = h // 2, h % 2, (h % 2) * 64
                    self.mm(pg[hf].t[:, pr * 16:pr * 16 + ob], QTf.t[r0:r0 + 64, pr, :], KM.t[r0:r0 + 64, pr, 0:ob], True, True, R=[QTf, KM], W=[pg[hf]])
                GATE = A.alloc([4, 16], F32)
                SEL = A.alloc([4, 16], F32)
                mx = A.alloc([4, 8], F32)
                self.S.op("pool", lambda e: e.memset(GATE.t[:, :, :], NEG), W=[GATE])
                for h in range(4):
                    pr, hf = h // 2, h % 2
                    self.cp("dve", GATE.t[:, h, 0:ob], pg[hf].t[:, pr * 16:pr * 16 + ob], R=[pg[hf]], W=[GATE])
                for h in range(4):
                    self.S.op("dve", lambda e: e.max(out=mx.t[:, h, :], in_=GATE.t[:, h, :]), R=[GATE], W=[mx])
                    self.ts("dve", SEL.t[:, h, :], GATE.t[:, h, :], mx.t[:, h, 2:3], None, ALU.is_ge, None, R=[GATE, mx], W=[SEL], SS=[mx])
            ACC = A.alloc([4, 65], F32)

            def group(kts, pacc):
                self.mm(pacc.t[:, 0:260], self.zero_bf.t[:, 0:128], self.zero_bf.t[:, 0:260], True, False, R=[self.zero_bf], W=[pacc])
                for i, kt in enumerate(kts):
                    ps_s = [self.ps(), self.ps()]
                    for h in range(4):
                        pr, hf, r0 = h // 2, h % 2, (h % 2) * 64
                        self.mm(ps_s[hf].t[:, pr * 128:(pr + 1) * 128], KT.t[r0:r0 + 64, pr, kt * 128:(kt + 1) * 128], QTb.t[r0:r0 + 64, pr, :], True, True,
                                R=[KT.c(kt), QTb], W=[ps_s[hf]])
                    PT = self.PT[self.pti % 2]
                    self.pti += 1
                    for hf in range(2):
                        self.act(PT.t[:, :].rearrange("p (pr hf q) -> p hf pr q", pr=2, hf=2)[:, hf], ps_s[hf].t[:, 0:256].rearrange("p (a b) -> p a b", a=2),
                                 AF.Exp, R=[ps_s[hf]], W=[PT], scale=SCALE)
                    if kt == qt:
                        for h in range(4):
                            self.tt("dve", PT.t[:, h * 128:(h + 1) * 128], PT.t[:, h * 128:(h + 1) * 128], caus, ALU.mult, R=[PT, self.caus_bf], W=[PT])
                    for h in range(4):
                        self.mm(pacc.t[:, h * 65:(h + 1) * 65], PT.t[:, h * 128:(h + 1) * 128], VX.t[:, kt, h, 0:65], False, (i == len(kts) - 1 and h == 3),
                                R=[PT, VX.c(kt)], W=[pacc], skip_group_check=True)

            pown = self.psb[6]
            own_kts = list(range(0 if dense else 2 * ob, qt + 1))
            group(own_kts, pown)
            self.cp("act", ACC.t[:, :, :], pown.t[:, 0:260].rearrange("p (h d) -> p h d", h=4), R=[pown], W=[ACC])
            if not dense:
                for n in range(ob):
                    pb = self.psb[7]
                    group([2 * n, 2 * n + 1], pb)
                    for h in range(4):
                        self.stt("dve", ACC.t[:, h, :], pb.t[:, h * 65:(h + 1) * 65], SEL.t[:, h, n:n + 1], ACC.t[:, h, :], ALU.mult, ALU.add,
                                 R=[pb, SEL, ACC], W=[ACC], SS=[SEL])
            rden = A.alloc([4], F32)
            Ot = A.alloc([256], F32)
            self.S.op("dve", lambda e: e.reciprocal(out=rden.t[:, :], in_=ACC.t[:, :, 64]), R=[ACC], W=[rden])
            for h in range(4):
                self.ts("dve", Ot.t[:, h * 64:(h + 1) * 64], ACC.t[:, h, 0:64], rden.t[:, h:h + 1], None, ALU.mult, None, R=[ACC, rden], W=[Ot], SS=[rden])
            po = self.ps()
            for pr in range(2):
                self.tr(po.t[:, pr * 128:(pr + 1) * 128], Ot.t[:, pr * 128:(pr + 1) * 128], R=[Ot], W=[po])
            for pr in range(2):
                self.cp("act", self.ycat.t[:, 2 + pr, cs], po.t[:, pr * 128:(pr + 1) * 128], R=[po], W=self.ycat.c(2 + pr))
            self.S.barrier()
            A.off = mark

    def mixer(self, l, tile, last):
        TT = self.TT
        A = self.ar
        A.reset()
        self.HQ = A.alloc([2, TT], F32)
        self.HF = A.alloc([2, TT], F32)
        self.HG = A.alloc([2, TT], F32)
        self.Qt = A.alloc([self.NST, 256], F32, ncells=self.NST)
        self.Kt = A.alloc([self.NST, 256], F32, ncells=self.NST)
        self.Vt = A.alloc([self.NST, 256], F32, ncells=self.NST)
        self.HFt = A.alloc([self.NST, 256], F32, ncells=self.NST)
        self.HIt = A.alloc([self.NST, 256], F32, ncells=self.NST)
        CA = A.alloc([2, TT], F32)
        UP, G = self.UP[l], self.G[l]
        self.proj_fm(l, 0, UP, 15)
        self.proj_tm(l, 256, self.Qt)
        self.proj_tm(l, 512, self.Kt)
        self.proj_tm(l, 768, self.Vt)
        self.proj_fm(l, 1024, CA)

        def glu(cc, ps):
            sb = A.alloc([TT], F32)
            self.act(sb.t[:, :], ps.t[:, 0:TT], AF.Sigmoid, R=[ps], W=[sb])
            self.tt("dve", G.t[:, cc, 30:30 + TT], CA.t[:, cc, :], sb.t[:, :], ALU.mult, R=[CA, sb], W=[G])
        self.proj_fm(l, 1280, None, post=glu)
        self.proj_fm(l, 1536, self.HQ)
        self.proj_fm(l, 1792, self.HF)
        self.proj_tm(l, 1792, self.HFt)
        self.proj_tm(l, 2048, self.HIt)
        self.proj_fm(l, 2304, self.HG)
        base = A.off
        for i, fn in enumerate((self.pool_mix, self.conv_mix, self.hgrn_mix, self.moba_mix)):
            if STAGE < 4 + i:
                continue
            fn(l, tile, last)
            self.S.barrier()
            A.off = base
        self.outproj(l)

    def outproj(self, l):
        TT = self.TT
        for mu in range(4):
            wb, w = self.wload(self.w_out[l][:, mu * 256:(mu + 1) * 256].rearrange("(c p) n -> p c n", p=128), 8, 256, ("out", l, mu))
            for mm_ in range(2):
                m = mu * 2 + mm_
                ps = self.ps()
                for c in range(8):
                    self.mm(ps.t[:, 0:TT], w[:, c, mm_ * 128:(mm_ + 1) * 128], self.ycat.t[:, c, :], c == 0, c == 7, R=[wb, self.ycat.c(c)], W=[ps])
                self.tt("dve", self.xT.t[:, m, :], self.xT.t[:, m, :], ps.t[:, 0:TT], ALU.add, R=[ps, self.xT.c(m)], W=self.xT.c(m))


    def pool_s(self, l, Us):
        A = self.ar
        UPs = A.alloc([2, 4 * 23], F32)
        v = [UPs.t[:, c, :].rearrange("p (b i) -> p b i", b=4) for c in range(2)]
        self.S.op("pool", lambda e: e.memset(UPs.t[:, :, :], 0.0), W=[UPs])
        stg = A.alloc([256], F32)
        self.S.dma("pool", stg.t[0:60, :], self.sp_d[l].rearrange("s i c -> (s i) c"), W=[stg])
        ps = self.ps()
        for c in range(2):
            self.tr(ps.t[:, c * 64:c * 64 + 60], stg.t[0:60, c * 128:(c + 1) * 128], R=[stg], W=[ps], n=60)
        for c in range(2):
            self.cp("dve", v[c][:, 0:4, 0:15], ps.t[:, c * 64:c * 64 + 60].rearrange("p (s i) -> p s i", s=4), R=[ps], W=[UPs])
            self.cp("dve", v[c][:, :, 15:23], Us.t[:, c, 0:32].rearrange("p (b t) -> p b t", b=4), R=[Us], W=[UPs])
        sb = [A.alloc([2, 4 * 23], F32) for _ in range(4)]
        sv = [[b_.t[:, c, :].rearrange("p (b i) -> p b i", b=4) for c in range(2)] for b_ in sb]
        for c in range(2):
            self.tt("dve", sv[0][c][:, :, 0:22], v[c][:, :, 1:23], v[c][:, :, 0:22], ALU.add, R=[UPs], W=[sb[0]])
            self.tt("dve", sv[1][c][:, :, 0:20], sv[0][c][:, :, 2:22], sv[0][c][:, :, 0:20], ALU.add, R=[sb[0]], W=[sb[1]])
            self.tt("dve", sv[2][c][:, :, 0:16], sv[1][c][:, :, 4:20], sv[1][c][:, :, 0:16], ALU.add, R=[sb[1]], W=[sb[2]])
            self.tt("dve", sv[3][c][:, :, 0:8], sv[2][c][:, :, 8:16], sv[2][c][:, :, 0:8], ALU.add, R=[sb[2]], W=[sb[3]])
        pooled = A.alloc([2, 256], BF16)
        self.S.op("pool", lambda e: e.memset(pooled.t[:, :, :], 0.0), W=[pooled])
        srcs = {(0, 0): (0, 14), (1, 0): (1, 12), (0, 1): (2, 8), (1, 1): (3, 0)}
        for (hf, c), (si, o) in srcs.items():
            r = slice(hf * 64, hf * 64 + 64)
            self.stt("dve", pooled.t[r, c, 0:32].rearrange("p (b t) -> p b t", b=4), sv[si][c][r, :, o:o + 8], self.prm.t[r, PG_INVW + c:PG_INVW + c + 1],
                     v[c][r, :, 15:23], ALU.mult, ALU.subtract, R=[sb[si], UPs, self.prm], W=[pooled])
        for c in range(2):
            ps2 = self.ps()
            self.mm(ps2.t[:, 0:256], self.pwbd.t[:, l, c, :], pooled.t[:, c, :], True, True, R=[self.pwbd, pooled], W=[ps2])
            self.ts("dve", self.ycat.t[:, c, :], ps2.t[:, 0:256], self.par(l, "pscale", c), None, ALU.mult, None, R=[ps2, self.prm], W=self.ycat.c(c))
        tmp = A.alloc([2, 60], F32)
        pso = self.ps()
        for c in range(2):
            self.cp("dve", tmp.t[:, c, :].rearrange("p (s i) -> p s i", s=4), v[c][:, 0:4, 8:23], R=[UPs], W=[tmp])
            self.tr(pso.t[0:60, c * 128:(c + 1) * 128], tmp.t[:, c, :], R=[tmp], W=[pso])
        ob = A.alloc([256], F32)
        self.cp("act", ob.t[0:60, :], pso.t[0:60, 0:256], R=[pso], W=[ob])
        self.S.dma("pool", self.o_pools[l].rearrange("s i c -> (s i) c"), ob.t[0:60, :], R=[ob])

    def conv_s(self, l, Gn):
        A = self.ar
        Gs = A.alloc([2, 4 * 38], F32)
        v = [Gs.t[:, c, :].rearrange("p (b i) -> p b i", b=4) for c in range(2)]
        self.S.op("pool", lambda e: e.memset(Gs.t[:, :, :], 0.0), W=[Gs])
        stg = A.alloc([256], F32)
        self.S.dma("pool", stg.t[0:120, :], self.sc_d[l].rearrange("s i c -> (s i) c"), W=[stg])
        ps = self.ps()
        for c in range(2):
            self.tr(ps.t[:, c * 128:c * 128 + 120], stg.t[0:120, c * 128:(c + 1) * 128], R=[stg], W=[ps], n=120)
        for c in range(2):
            self.cp("dve", v[c][:, 0:4, 0:30], ps.t[:, c * 128:c * 128 + 120].rearrange("p (s i) -> p s i", s=4), R=[ps], W=[Gs])
            self.cp("dve", v[c][:, :, 30:38], Gn.t[:, c, 0:32].rearrange("p (b t) -> p b t", b=4), R=[Gn], W=[Gs])
        acc = A.alloc([2, 256], F32, ncells=2)
        sq = A.alloc([2, 256], F32, ncells=2)
        self.S.op("pool", lambda e: e.memset(acc.t[:, :, :], 0.0), W=[acc])
        for c in range(2):
            av = acc.t[:, c, 0:32].rearrange("p (b t) -> p b t", b=4)
            self.ts("dve", av, v[c][:, :, 0:8], self.par(l, "cw", c * 31), self.par(l, "cb", c), ALU.mult, ALU.add, R=[Gs, self.prm], W=acc.c(c))
            for j in range(1, 31):
                self.stt("dve", av, v[c][:, :, j:j + 8], self.par(l, "cw", c * 31 + j), av, ALU.mult, ALU.add, R=[Gs, self.prm, acc.c(c)], W=acc.c(c))
            self.tt("dve", sq.t[:, c, :], acc.t[:, c, :], acc.t[:, c, :], ALU.mult, R=acc.c(c), W=sq.c(c))
        self.conv_tail(l, acc, sq)
        tmp = A.alloc([2, 120], F32)
        pso = self.ps()
        for c in range(2):
            self.cp("dve", tmp.t[:, c, :].rearrange("p (s i) -> p s i", s=4), v[c][:, 0:4, 8:38], R=[Gs], W=[tmp])
            self.tr(pso.t[0:120, c * 128:(c + 1) * 128], tmp.t[:, c, :], R=[tmp], W=[pso])
        ob = A.alloc([256], F32)
        self.cp("act", ob.t[0:120, :], pso.t[0:120, 0:256], R=[pso], W=[ob])
        self.S.dma("pool", self.o_convs[l].rearrange("s i c -> (s i) c"), ob.t[0:120, :], R=[ob])

    def hgrn_s(self, l):
        A = self.ar
        HQ, HF, HG, HFt, HIt = self.HQ, self.HF, self.HG, self.HFt, self.HIt
        tri = self.cst.t[:, C_TRI8:C_TRI8 + 128]
        trev = self.cst.t[:, C_TREV8:C_TREV8 + 128]
        st = 0
        cs = slice(0, 128)
        Ss = A.alloc([4, 2, 64], F32)
        self.S.dma("pool", Ss.t[:, :, :, :], self.sh_d[l].rearrange("s (pr hf) k v -> (hf k) s pr v", hf=2), W=[Ss])
        sig = A.alloc([256], F32)
        logf = A.alloc([256], F32)
        kin = A.alloc([256], F32)
        self.act(sig.t[:, :], HFt.t[:, st, :], AF.Sigmoid, R=HFt.c(st), W=[sig])
        self.tt("dve", sig.t[:, :], sig.t[:, :], self.omltm.t[:, l, :], ALU.mult, R=[sig, self.omltm], W=[sig])
        self.tt("dve", sig.t[:, :], sig.t[:, :], self.lbtm.t[:, l, :], ALU.add, R=[sig, self.lbtm], W=[sig])
        self.act(logf.t[:, :], sig.t[:, :], AF.Ln, R=[sig], W=[logf])
        self.ts("dve", kin.t[:, :], sig.t[:, :], -1.0, 1.0, ALU.mult, ALU.add, R=[sig], W=[kin])
        prev = self.ps()
        self.mm(prev.t[:, 0:256], trev, logf.t[:, :], True, True, R=[self.cst, logf], W=[prev])
        pbc = self.ps()
        for kc in range(2):
            self.mm(pbc.t[:, kc * 128:(kc + 1) * 128], logf.t[:, kc * 128:(kc + 1) * 128], tri, True, True, R=[self.cst, logf], W=[pbc])
        er = A.alloc([256], F32)
        kh = A.alloc([256], F32)
        vb = A.alloc([256], BF16)
        self.act(er.t[:, :], prev.t[:, 0:256], AF.Exp, R=[prev], W=[er])
        self.tt("dve", kh.t[:, :], kin.t[:, :], er.t[:, :], ALU.mult, R=[kin, er], W=[kh])
        self.cp("pool", vb.t[:, :], HIt.t[:, st, :], R=HIt.c(st), W=[vb])
        E = A.alloc([2, 128], F32)
        Ei = A.alloc([2, 128], F32)
        for kc in range(2):
            self.act(E.t[:, kc, :], pbc.t[:, kc * 128:(kc + 1) * 128], AF.Exp, R=[pbc], W=[E])
            self.act(Ei.t[:, kc, :], pbc.t[:, kc * 128:(kc + 1) * 128], AF.Exp, R=[pbc], W=[Ei], scale=-1.0)
        sT = A.alloc([2, 128], F32)
        self.act(sT.t[:, :, :], HF.t[:, :, cs], AF.Sigmoid, R=[HF], W=[sT])
        for c in range(2):
            self.ts("dve", sT.t[:, c, :], sT.t[:, c, :], self.nomlfm.t[:, l, c:c + 1], self.omlfm.t[:, l, c:c + 1], ALU.mult, ALU.add,
                    R=[sT, self.nomlfm, self.omlfm], W=[sT])
        qt_ = A.alloc([2, 128], BF16)
        kt_ = A.alloc([2, 128], BF16)
        self.tt("dve", qt_.t[:, :, :], HQ.t[:, :, cs], E.t[:, :, :], ALU.mult, R=[HQ, E], W=[qt_])
        self.tt("dve", kt_.t[:, :, :], sT.t[:, :, :], Ei.t[:, :, :], ALU.mult, R=[sT, Ei], W=[kt_])
        patt = [self.ps(), self.ps()]
        for h in range(4):
            pr, hf, r0 = h // 2, h % 2, (h % 2) * 64
            self.mm(patt[hf].t[:, pr * 128:(pr + 1) * 128], kt_.t[r0:r0 + 64, pr, :], qt_.t[r0:r0 + 64, pr, :], True, True, R=[kt_, qt_], W=[patt[hf]])
        att = A.alloc([4, 128], BF16)
        for h in range(4):
            pr, hf = h // 2, h % 2
            self.tt("dve", att.t[:, h, :], patt[hf].t[:, pr * 128:(pr + 1) * 128], tri, ALU.mult, R=[patt[hf], self.cst], W=[att])
        sbf = A.alloc([4, 2, 64], BF16)
        self.cp("act", sbf.t[:, :, :, :], Ss.t[:, :, :, :], R=[Ss], W=[sbf])
        po = [self.ps(), self.ps()]
        for h in range(4):
            pr, hf, r0 = h // 2, h % 2, (h % 2) * 64
            self.mm(po[hf].t[r0:r0 + 64, pr * 128:(pr + 1) * 128], vb.t[:, h * 64:(h + 1) * 64], att.t[:, h, :], True, False, R=[vb, att], W=[po[hf]])
            for sl in range(4):
                self.mm(po[hf].t[r0:r0 + 64, pr * 128 + sl * 8:pr * 128 + sl * 8 + 8], sbf.t[r0:r0 + 64, sl, pr, :], qt_.t[r0:r0 + 64, pr, sl * 8:(sl + 1) * 8],
                        False, sl == 3, R=[sbf, qt_], W=[po[hf]], skip_group_check=True)
        khm = A.alloc([256], BF16)
        for sl in range(4):
            self.ts("dve", khm.t[:, :], kh.t[:, :], self.cst.t[:, C_ROWM + sl:C_ROWM + sl + 1], None, ALU.mult, None, R=[kh, self.cst], W=[khm])
            pU = self.ps()
            for pr in range(2):
                self.mm(pU.t[:, pr * 128:(pr + 1) * 128], khm.t[:, pr * 128:(pr + 1) * 128], vb.t[:, pr * 128:(pr + 1) * 128], True, True, R=[khm, vb], W=[pU])
            for pr in range(2):
                for hf in range(2):
                    r = slice(hf * 64, hf * 64 + 64)
                    k0 = pr * 128 + hf * 64
                    self.stt("dve", Ss.t[r, sl, pr, :], Ss.t[r, sl, pr, :], E.t[r, pr, sl * 8 + 7:sl * 8 + 8], pU.t[r, k0:k0 + 64], ALU.mult, ALU.add,
                             R=[Ss, E, pU], W=[Ss], SS=[E])
        self.S.dma("pool", self.o_hgrns[l].rearrange("s (pr hf) k v -> (hf k) s pr v", hf=2), Ss.t[:, :, :, :], R=[Ss])
        O = A.alloc([2, 128], F32)
        sqo = A.alloc([2, 128], BF16)
        for hf in range(2):
            r = slice(hf * 64, hf * 64 + 64)
            self.cp("act", O.t[r, :, :], po[hf].t[r, 0:256].rearrange("p (a b) -> p a b", a=2), R=[po[hf]], W=[O])
        self.tt("pool", sqo.t[:, :, :], O.t[:, :, :], O.t[:, :, :], ALU.mult, R=[O], W=[sqo])
        pss = self.ps()
        for pr in range(2):
            self.mm(pss.t[:, pr * 128:(pr + 1) * 128], self.blk_bf.t[:, :], sqo.t[:, pr, :], True, True, R=[self.blk_bf, sqo], W=[pss])
        rs = A.alloc([2, 128], F32)
        self.act(rs.t[:, :, :], pss.t[:, 0:256].rearrange("p (a b) -> p a b", a=2), AF.Sqrt, R=[pss, self.prm], W=[rs],
                 bias=self.prm.t[:, PG_EPS:PG_EPS + 1], scale=1.0 / 64)
        self.S.op("dve", lambda e: e.reciprocal(out=rs.t[:, :, :], in_=rs.t[:, :, :]), R=[rs], W=[rs])
        sgt = A.alloc([2, 128], F32)
        self.act(sgt.t[:, :, :], HG.t[:, :, cs], AF.Silu, R=[HG], W=[sgt])
        self.tt("dve", O.t[:, :, :], O.t[:, :, :], rs.t[:, :, :], ALU.mult, R=[O, rs], W=[O])
        for pr in range(2):
            self.stt("dve", self.ycat.t[:, 6 + pr, cs], O.t[:, pr, :], self.par(l, "hnorm", pr), sgt.t[:, pr, :], ALU.mult, ALU.mult,
                     R=[O, sgt, self.prm], W=self.ycat.c(6 + pr))
            self.S.op("pool", lambda e: e.memset(self.ycat.t[:, 6 + pr, 128:256], 0.0), W=self.ycat.c(6 + pr))

    def moba_s(self, l):
        A = self.ar
        S = self.S
        NP, NB = self.NP, self.NP // 2
        st = 0
        QTb = A.alloc([2, 128], BF16)
        KTn = A.alloc([2, 128], BF16)
        Vxn = A.alloc([4, 66], BF16)
        mark2 = A.off
        QN, KN = self.qk_norm_rope(l, st, self.coss.t[:, :], self.sins.t[:, :], [self.coss, self.sins])
        S.dma("pool", self.o_ks[l], KN.t[0:32, :], R=[KN])
        S.dma("pool", self.o_vs[l], self.Vt.t[0:32, st, :], R=self.Vt.c(st))
        pq, pk = self.ps(), self.ps()
        for pr in range(2):
            self.tr(pq.t[:, pr * 128:(pr + 1) * 128], QN.t[:, pr * 128:(pr + 1) * 128], R=[QN], W=[pq])
            self.tr(pk.t[:, pr * 128:(pr + 1) * 128], KN.t[:, pr * 128:(pr + 1) * 128], R=[KN], W=[pk])
        for pr in range(2):
            self.cp("dve", QTb.t[:, pr, :], pq.t[:, pr * 128:(pr + 1) * 128], R=[pq], W=[QTb])
            self.cp("dve", KTn.t[:, pr, :], pk.t[:, pr * 128:(pr + 1) * 128], R=[pk], W=[KTn])
        S.op("pool", lambda e: e.memset(Vxn.t[:, :, :], 1.0), W=[Vxn])
        for h in range(4):
            self.cp("dve", Vxn.t[:, h, 0:64], self.Vt.t[:, st, h * 64:(h + 1) * 64], R=self.Vt.c(st), W=[Vxn])
        S.barrier()
        A.off = mark2
        self.KTc = [A.alloc([2, 4, 128], BF16)]
        self.Pof = A.alloc([32], F32)
        Sall = A.alloc([128, 32], F32)
        Kc = A.alloc([8, 256], F32)
        Vcb = A.alloc([8, 4, 66], BF16)
        S.op("pool", lambda e: e.memset(Vcb.t[:, :, :, :], 1.0), W=[Vcb])
        P = [self.hT, self.sqb]
        Pv = [b_.t[:, :, :].rearrange("p a b -> p (a b)").rearrange("p (r c) -> p r c", c=32) for b_ in P]
        gT = A.alloc([32], F32)
        GATE = A.alloc([64], F32)
        SEL = A.alloc([64], F32)
        mx = A.alloc([8], F32)
        selTs = A.alloc([32], F32)
        selT2 = A.alloc([32], F32)
        Po = A.alloc([32], BF16)
        ACC = A.alloc([264], F32)
        Osel = A.alloc([64], F32)
        den = A.alloc([1], F32)
        Oh = A.alloc([2, 128], F32)
        hm = self.cst.t[:, C_HM:C_HM + 4]
        for sl in range(4):
            idx = self.ptab.t[0:NP, sl:sl + 1]
            for rc in range(16):
                S.idma(Kc.t[0:NP, :, :].rearrange("p a b -> p (a b)"), self.ck_d[l][rc], idx, R=[self.ptab], W=[Kc])
                pS = [self.psb[6], self.psb[7]]
                for r4 in range(2):
                    KTc = self.KTc[0]
                    for pr in range(2):
                        pt_ = self.ps()
                        for ri in range(4):
                            r = r4 * 4 + ri
                            self.tr(pt_.t[:, ri * 128:ri * 128 + NP], Kc.t[0:NP, r, pr * 128:(pr + 1) * 128], R=[Kc], W=[pt_], n=NP)
                        self.cp("act" if pr == 0 else "dve", KTc.t[:, pr, :, 0:NP], pt_.t[:, :].rearrange("p (a b) -> p a b", a=4)[:, :, 0:NP], R=[pt_], W=[KTc])
                    for ri in range(4):
                        r = r4 * 4 + ri
                        for h in range(4):
                            pr, hf, r0 = h // 2, h % 2, (h % 2) * 64
                            c0 = (r * 2 + pr) * 8
                            self.mm(pS[hf].t[0:NP, c0:c0 + 8], KTc.t[r0:r0 + 64, pr, ri, 0:NP], QTb.t[r0:r0 + 64, pr, sl * 8:(sl + 1) * 8], True, True,
                                    R=[KTc, QTb], W=[pS[hf]])
                for hf in range(2):
                    dst = Sall.t[0:NP, rc * 8:(rc + 1) * 8, :].rearrange("p r (pr hf q) -> p hf r pr q", pr=2, hf=2)[:, hf]
                    self.cp("act" if hf == 0 else "dve", dst, pS[hf].t[0:NP, 0:128].rearrange("p (r pr q) -> p r pr q", r=8, pr=2), R=[pS[hf]], W=[Sall])
            if NB > 3:
                S.op("dve", lambda e: e.tensor_reduce(out=gT.t[0:NP, :], in_=Sall.t[0:NP, :, :].rearrange("p r c -> p c r"), axis=AX.X, op=ALU.add), R=[Sall], W=[gT])
                pg = self.ps()
                self.mm(pg.t[0:32, 0:NB], gT.t[0:NP, :], self.cst.t[0:NP, C_PAIRM:C_PAIRM + NB], True, True, R=[gT, self.cst], W=[pg])
                S.op("pool", lambda e: e.memset(GATE.t[0:32, :], NEG), W=[GATE])
                self.cp("dve", GATE.t[0:32, 0:NB], pg.t[0:32, 0:NB], R=[pg], W=[GATE])
                S.op("dve", lambda e: e.max(out=mx.t[0:32, :], in_=GATE.t[0:32, :]), R=[GATE], W=[mx])
                self.ts("dve", SEL.t[0:32, :], GATE.t[0:32, :], mx.t[0:32, 2:3], None, ALU.is_ge, None, R=[GATE, mx], W=[SEL], SS=[mx])
                pst = self.ps()
                self.tr(pst.t[0:64, 0:32], SEL.t[0:32, :], R=[SEL], W=[pst], n=32)
                self.cp("dve", selTs.t[0:64, :], pst.t[0:64, 0:32], R=[pst], W=[selTs])
                pe_ = self.ps()
                self.mm(pe_.t[0:NP, 0:32], self.cst.t[0:NB, C_PAIRMT:C_PAIRMT + NP], selTs.t[0:NB, :], True, True, R=[self.cst, selTs], W=[pe_])
                self.cp("dve", selT2.t[0:NP, :], pe_.t[0:NP, 0:32], R=[pe_], W=[selT2])
            else:
                S.op("pool", lambda e: e.memset(selT2.t[:, :], 1.0), W=[selT2])
            self.act(Sall.t[0:NP, :, :], Sall.t[0:NP, :, :], AF.Exp, R=[Sall], W=[Sall], scale=SCALE)
            for half in range(2):
                for c in range(32):
                    self.ts("dve", Pv[half][0:NP, :, c], Sall.t[0:NP, half * 64:(half + 1) * 64, c], selT2.t[0:NP, c:c + 1], None, ALU.mult, None,
                            R=[Sall, selT2], W=P[half].all, SS=[selT2])
            pacc = self.psb[6]
            self.mm(pacc.t[0:32, 0:264], self.zero_bf.t[:, 0:32], self.zero_bf.t[:, 0:264], True, False, R=[self.zero_bf], W=[pacc])
            for rc in range(16):
                S.idma(Kc.t[0:NP, :, :].rearrange("p a b -> p (a b)"), self.cv_d[l][rc], idx, R=[self.ptab], W=[Kc])
                for h in range(4):
                    self.cp("dve" if h % 2 == 0 else "pool", Vcb.t[0:NP, :, h, 0:64], Kc.t[0:NP, :, h * 64:(h + 1) * 64], R=[Kc], W=[Vcb])
                for ri in range(8):
                    r = rc * 8 + ri
                    self.mm(pacc.t[0:32, 0:264], Pv[r // 64][0:NP, r % 64, :], Vcb.t[0:NP, ri, :, :].rearrange("p h d -> p (h d)"), False, False,
                            R=[P[r // 64].all, Vcb], W=[pacc], skip_group_check=True)
            pso = [self.ps(), self.ps()]
            for h in range(4):
                pr, hf, r0 = h // 2, h % 2, (h % 2) * 64
                self.mm(pso[hf].t[:, pr * 8:(pr + 1) * 8], KTn.t[r0:r0 + 64, pr, :], QTb.t[r0:r0 + 64, pr, sl * 8:(sl + 1) * 8], True, True, R=[KTn, QTb], W=[pso[hf]])
            Pof = self.Pof
            for hf in range(2):
                self.act(Pof.t[:, :].rearrange("p (pr hf q) -> p hf pr q", pr=2, hf=2)[:, hf], pso[hf].t[:, 0:16].rearrange("p (a b) -> p a b", a=2), AF.Exp,
                         R=[pso[hf]], W=[Pof], scale=SCALE)
            for h in range(4):
                self.tt("dve", Po.t[:, h * 8:(h + 1) * 8], Pof.t[:, h * 8:(h + 1) * 8], self.cst.t[:, C_OWNM + sl * 8:C_OWNM + sl * 8 + 8], ALU.mult,
                        R=[Pof, self.cst], W=[Po])
            self.mm(pacc.t[0:32, 0:264], Po.t[:, :], Vxn.t[:, :, :].rearrange("p h d -> p (h d)"), False, True, R=[Po, Vxn], W=[pacc], skip_group_check=True)
            self.cp("act", ACC.t[0:32, :], pacc.t[0:32, 0:264], R=[pacc], W=[ACC])
            self.ts("dve", Osel.t[0:32, :], ACC.t[0:32, 0:64], hm[0:32, 0:1], None, ALU.mult, None, R=[ACC, self.cst], W=[Osel])
            self.ts("dve", den.t[0:32, :], ACC.t[0:32, 64:65], hm[0:32, 0:1], None, ALU.mult, None, R=[ACC, self.cst], W=[den])
            for h in range(1, 4):
                self.stt("dve", Osel.t[0:32, :], ACC.t[0:32, h * 66:h * 66 + 64], hm[0:32, h:h + 1], Osel.t[0:32, :], ALU.mult, ALU.add, R=[ACC, self.cst, Osel], W=[Osel])
                self.stt("dve", den.t[0:32, :], ACC.t[0:32, h * 66 + 64:h * 66 + 65], hm[0:32, h:h + 1], den.t[0:32, :], ALU.mult, ALU.add, R=[ACC, self.cst, den], W=[den])
            S.op("dve", lambda e: e.reciprocal(out=den.t[0:32, :], in_=den.t[0:32, :]), R=[den], W=[den])
            self.ts("dve", Osel.t[0:32, :], Osel.t[0:32, :], den.t[0:32, 0:1], None, ALU.mult, None, R=[Osel, den], W=[Osel], SS=[den])
            for pr in range(2):
                for hf in range(2):
                    self.ts("dve", Oh.t[0:32, pr, hf * 64:(hf + 1) * 64], Osel.t[0:32, :], hm[0:32, pr * 2 + hf:pr * 2 + hf + 1], None, ALU.mult, None,
                            R=[Osel, self.cst], W=[Oh])
            pf = self.ps()
            for pr in range(2):
                self.mm(pf.t[:, pr * 8:(pr + 1) * 8], Oh.t[0:32, pr, :], self.cst.t[0:32, C_QSEL:C_QSEL + 8], True, True, R=[Oh, self.cst], W=[pf])
            for pr in range(2):
                self.cp("dve", self.ycat.t[:, 2 + pr, sl * 8:(sl + 1) * 8], pf.t[:, pr * 8:(pr + 1) * 8], R=[pf], W=self.ycat.c(2 + pr))

    def mixer_s(self, l):
        TT = self.TT
        A = self.ar
        A.reset()
        self.Qt = A.alloc([self.NST, 256], F32, ncells=self.NST)
        self.Kt = A.alloc([self.NST, 256], F32, ncells=self.NST)
        self.Vt = A.alloc([self.NST, 256], F32, ncells=self.NST)
        mark_qkv = A.off
        self.HQ = A.alloc([2, TT], F32)
        self.HF = A.alloc([2, TT], F32)
        self.HG = A.alloc([2, TT], F32)
        self.HFt = A.alloc([self.NST, 256], F32, ncells=self.NST)
        self.HIt = A.alloc([self.NST, 256], F32, ncells=self.NST)
        CA = A.alloc([2, TT], F32)
        Us = A.alloc([2, TT], F32)
        Gn = A.alloc([2, TT], F32)
        self.proj_fm(l, 0, Us)
        self.proj_tm(l, 256, self.Qt)
        self.proj_tm(l, 512, self.Kt)
        self.proj_tm(l, 768, self.Vt)
        self.proj_fm(l, 1024, CA)

        def glu(cc, ps):
            sb = A.alloc([TT], F32)
            self.act(sb.t[:, :], ps.t[:, 0:TT], AF.Sigmoid, R=[ps], W=[sb])
            self.tt("dve", Gn.t[:, cc, :], CA.t[:, cc, :], sb.t[:, :], ALU.mult, R=[CA, sb], W=[Gn])
        self.proj_fm(l, 1280, None, post=glu)
        self.proj_fm(l, 1536, self.HQ)
        self.proj_fm(l, 1792, self.HF)
        self.proj_tm(l, 1792, self.HFt)
        self.proj_tm(l, 2048, self.HIt)
        self.proj_fm(l, 2304, self.HG)
        base = A.off
        self.pool_s(l, Us)
        self.S.barrier()
        A.off = base
        self.conv_s(l, Gn)
        self.S.barrier()
        A.off = base
        self.hgrn_s(l)
        self.S.barrier()
        A.off = mark_qkv
        for pr in range(2):
            self.S.op("pool", lambda e: e.memset(self.ycat.t[:, 2 + pr, :], 0.0), W=self.ycat.c(2 + pr))
        self.moba_s(l)
        self.S.barrier()
        self.outproj(l)

    def sample_tile(self):
        S, A = self.S, self.ar
        A.reset()
        for st in range(2):
            xin = A.alloc([D], F32)
            S.dma("pool", xin.t[:, :], self.xs_d[st * 128:(st + 1) * 128, :], W=[xin])
            for g in range(2):
                ps = self.ps()
                for c4 in range(4):
                    c = g * 4 + c4
                    self.tr(ps.t[:, c4 * 128:(c4 + 1) * 128], xin.t[:, c * 128:(c + 1) * 128], R=[xin], W=[ps])
                self.cp("act" if g == 0 else "dve", self.xT.t[:, g * 4:(g + 1) * 4, st * 128:(st + 1) * 128],
                        ps.t[:, :].rearrange("p (a b) -> p a b", a=4), R=[ps], W=[self.xT.c(g * 4 + i) for i in range(4)])
        for l in range(2):
            self.rmsnorm(l, "g1")
            self.ffn(l, self.W1g, self.W1u, self.W1d)
            self.rmsnorm(l, "gm")
            self.mixer_s(l)
            self.rmsnorm(l, "g2")
            self.ffn(l, self.W2g, self.W2u, self.W2d)
        A.reset()
        yo = A.alloc([D], F32)
        for g in range(2):
            ps = self.ps()
            for c4 in range(4):
                c = g * 4 + c4
                self.tr(ps.t[0:32, c4 * 128:(c4 + 1) * 128], self.xT.t[:, c, 0:32], R=self.xT.c(c), W=[ps])
            self.cp("act" if g == 0 else "dve", yo.t[0:32, g * 512:(g + 1) * 512], ps.t[0:32, :], R=[ps], W=[yo])
        S.dma("pool", self.ys_d, yo.t[0:32, :], R=[yo])

    def build(self):
        SEQ, TT, NT, NST, NQT = self.SEQ, self.TT, self.NT, self.NST, self.NQT
        nc = bass.Bass("TRN2", target_bir_lowering=False)
        self.nc = nc

        def din(name, shape):
            return nc.dram_tensor(name, list(shape), F32, kind="ExternalInput").ap()

        def dout(name, shape):
            return nc.dram_tensor(name, list(shape), F32, kind="ExternalOutput").ap()

        x = din("xp", [SEQ, D])
        cst_d = din("cst", [128, C_TOT])
        prm_d = din("prm", [128, PAR_TOT])
        cos_d = din("cos", [SEQ, 32])
        sin_d = din("sin", [SEQ, 32])
        pwbd_d = din("pwbd", [2, 2, 128, 128])
        pw_d = din("cpw", [2, 256, 256])
        self.W1g, self.W1u, self.W1d = din("w1g", [2, D, DFF]), din("w1u", [2, D, DFF]), din("w1d", [2, DFF, D])
        self.W2g, self.W2u, self.W2d = din("w2g", [2, D, DFF]), din("w2u", [2, D, DFF]), din("w2d", [2, DFF, D])
        self.w_in, self.w_out = din("w_in", [2, D, 2560]), din("w_out", [2, D, D])
        NP, NPOOL = self.NP, self.NPOOL
        self.xs_d = din("xs", [256, D])
        self.sp_d = din("sp", [2, 4, 15, 256])
        self.sc_d = din("sc", [2, 4, 30, 256])
        self.sh_d = din("sh", [2, 4, 4, 64, 64])
        pt_d = nc.dram_tensor("pt", [4, NP], mybir.dt.int32, kind="ExternalInput").ap()
        coss_d, sins_d = din("coss", [128, 32]), din("sins", [128, 32])
        self.ck_d = [[din("ck%d_%d" % (l, rc), [NPOOL, 2048]) for rc in range(16)] for l in range(2)]
        self.cv_d = [[din("cv%d_%d" % (l, rc), [NPOOL, 2048]) for rc in range(16)] for l in range(2)]
        self.ys_d = dout("ys", [32, D])
        self.o_ks, self.o_vs = dout("oks", [2, 32, 256]), dout("ovs", [2, 32, 256])
        self.o_pools, self.o_convs = dout("opools", [2, 4, 15, 256]), dout("oconvs", [2, 4, 30, 256])
        self.o_hgrns = dout("ohgrns", [2, 4, 4, 64, 64])
        y = dout("y", [SEQ, D])
        self.o_k, self.o_v = dout("ok", [2, SEQ, 256]), dout("ov", [2, SEQ, 256])
        self.o_pool, self.o_conv = dout("opool", [2, 15, 256]), dout("oconv", [2, 30, 256])
        self.o_hgrn = dout("ohgrn", [2, 4, 64, 64])

        with ExitStack() as stack:
            S = Sched(nc, stack)
            self.S = S
            self.psb = [S.psum("ps%d" % i, [128, 512]) for i in range(8)]
            self.psi = 0
            self.pti = 0
            self.wi = 0
            self.wj = 0
            self.wmap = {}
            self.first_pass_done = False
            self.wsc = nc.dram_tensor("wsc", [192, 128, 2048], BF16, kind="Internal").ap()
            self.wstg = [S.sbuf("wstg%d" % i, [128, 2048], F32) for i in range(2)]
            self.wbf = [S.sbuf("wbf%d" % i, [128, 2048], BF16) for i in range(2)]
            self.wring = list(self.wbf)
            self.cst = S.sbuf("cst", [128, C_TOT], F32)
            self.prm = S.sbuf("prm", [128, PAR_TOT], F32)
            self.ones_bf = S.sbuf("ones_bf", [128, 128], BF16)
            self.blk_bf = S.sbuf("blk_bf", [128, 128], BF16)
            self.caus_bf = S.sbuf("caus_bf", [128, 128], BF16)
            self.zero_bf = S.sbuf("zero_bf", [128, 264], BF16)
            self.cos = S.sbuf("cos", [128, NQT, 32], F32)
            self.sin = S.sbuf("sin", [128, NQT, 32], F32)
            self.pwbd = S.sbuf("pwbd", [128, 2, 2, 128], BF16)
            self.pw = S.sbuf("pw", [128, 2, 2, 256], BF16)
            self.lbtm = S.sbuf("lbtm", [128, 2, 256], F32)
            self.omltm = S.sbuf("omltm", [128, 2, 256], F32)
            self.omlfm = S.sbuf("omlfm", [128, 2, 2], F32)
            self.nomlfm = S.sbuf("nomlfm", [128, 2, 2], F32)
            self.xT = S.sbuf("xT", [128, 8, TT], F32, ncells=8)
            self.hT = S.sbuf("hT", [128, 8, TT], BF16, ncells=8)
            self.sqb = S.sbuf("sqb", [128, 8, TT], BF16, ncells=8)
            self.rstd = S.sbuf("rstd", [128, TT], F32)
            self.ycat = S.sbuf("ycat", [128, 8, TT], BF16, ncells=8)
            self.UP = [S.sbuf("UP%d" % l, [128, 2, 15 + TT], F32) for l in range(2)]
            self.G = [S.sbuf("G%d" % l, [128, 2, 30 + TT], F32) for l in range(2)]
            self.Sst = [S.sbuf("Sst%d" % l, [128, 2, 64], F32) for l in range(2)]
            self.KT = [S.sbuf("KT%d" % l, [128, 2, SEQ], BF16, ncells=NQT) for l in range(2)]
            self.VX = [S.sbuf("VX%d" % l, [128, NQT, 4, 66], BF16, ncells=NQT) for l in range(2)]
            self.KP = [S.sbuf("KP%d" % l, [128, 2, NQT], F32) for l in range(2)]
            self.KM = [S.sbuf("KM%d" % l, [128, 2, 16], F32) for l in range(2)]
            self.PT = [S.sbuf("PT%d" % i, [128, 512], BF16) for i in range(2)]
            self.ar = Arena(S, "arena", self.arena_bytes())
            A = self.ar

            S.dma("sp", self.cst.t[:, :], cst_d, W=[self.cst])
            S.dma("sp", self.prm.t[:, :], prm_d, W=[self.prm])
            S.dma("sp", self.cos.t[:, :, :], cos_d.rearrange("(q p) c -> p q c", p=128), W=[self.cos])
            S.dma("sp", self.sin.t[:, :, :], sin_d.rearrange("(q p) c -> p q c", p=128), W=[self.sin])
            self.cp("dve", self.ones_bf.t[:, :], self.cst.t[:, C_ONES:C_ONES + 128], R=[self.cst], W=[self.ones_bf])
            self.cp("dve", self.blk_bf.t[:, :], self.cst.t[:, C_BLK:C_BLK + 128], R=[self.cst], W=[self.blk_bf])
            self.cp("dve", self.caus_bf.t[:, :], self.cst.t[:, C_CAUS:C_CAUS + 128], R=[self.cst], W=[self.caus_bf])
            S.op("pool", lambda e: e.memset(self.zero_bf.t[:, :], 0.0), W=[self.zero_bf])
            t0 = A.alloc([2, 2, 128], F32)
            S.dma("pool", t0.t[:, :, :, :], pwbd_d.rearrange("l c p n -> p l c n"), W=[t0])
            self.cp("dve", self.pwbd.t[:, :, :, :], t0.t[:, :, :, :], R=[t0], W=[self.pwbd])
            t1 = A.alloc([2, 2, 256], F32)
            S.dma("pool", t1.t[:, :, :, :], pw_d.rearrange("l (c p) n -> p l c n", p=128), W=[t1])
            self.cp("dve", self.pw.t[:, :, :, :], t1.t[:, :, :, :], R=[t1], W=[self.pw])
            for (dst_lb, dst_oml, nm, n) in ((self.lbtm, self.omltm, "lbtm", 256), (None, self.omlfm, "lbfm", 2)):
                d = A.alloc([n], F32)
                lb1 = A.alloc([n], F32)
                self.tt("dve", d.t[:, :], self.par(1, nm, 0, n), self.par(0, nm, 0, n), ALU.subtract, R=[self.prm], W=[d])
                self.act(lb1.t[:, :], d.t[:, :], AF.Sigmoid, R=[d], W=[lb1])
                if dst_lb is not None:
                    S.op("pool", lambda e: e.memset(dst_lb.t[:, 0, :], 0.0), W=[dst_lb])
                    self.cp("dve", dst_lb.t[:, 1, :], lb1.t[:, :], R=[lb1], W=[dst_lb])
                S.op("pool", lambda e: e.memset(dst_oml.t[:, 0, :], 1.0), W=[dst_oml])
                self.ts("dve", dst_oml.t[:, 1, :], lb1.t[:, :], -1.0, 1.0, ALU.mult, ALU.add, R=[lb1], W=[dst_oml])
            self.ts("dve", self.nomlfm.t[:, :, :], self.omlfm.t[:, :, :], -1.0, None, ALU.mult, None, R=[self.omlfm], W=[self.nomlfm])
            for l in range(2):
                S.op("pool", lambda e: e.memset(self.UP[l].t[:, :, :], 0.0), W=[self.UP[l]])
                S.op("pool", lambda e: e.memset(self.G[l].t[:, :, :], 0.0), W=[self.G[l]])
                S.op("pool", lambda e: e.memset(self.Sst[l].t[:, :, :], 0.0), W=[self.Sst[l]])
                S.op("pool", lambda e: e.memset(self.VX[l].t[:, :, :, 64:65], 1.0), W=[self.VX[l]])
                S.op("pool", lambda e: e.memset(self.KM[l].t[:, :, :], 0.0), W=[self.KM[l]])
            S.barrier(sp=True)

            self.coss = S.sbuf("coss", [128, 32], F32)
            self.sins = S.sbuf("sins", [128, 32], F32)
            self.ptab = S.sbuf("ptab", [128, 4], mybir.dt.int32)
            S.dma("sp", self.coss.t[:, :], coss_d, W=[self.coss])
            S.dma("sp", self.sins.t[:, :], sins_d, W=[self.sins])
            S.op("pool", lambda e: e.memset(self.ptab.t[:, :], 0), W=[self.ptab])
            with nc.allow_non_contiguous_dma(reason="tiny page-table transpose"):
                S.dma("sp", self.ptab.t[0:NP, :], pt_d.rearrange("s j -> j s"), W=[self.ptab])
            if DO_SAMPLE:
                self.sample_tile()
                self.end_first_pass()
            S.barrier(sp=True)

            for tile in range(NT):
                last = tile == NT - 1
                A.reset()
                for st in range(NST):
                    xin = A.alloc([D], F32)
                    r0 = tile * TT + st * 128
                    S.dma("pool", xin.t[:, :], x[r0:r0 + 128, :], W=[xin])
                    for g in range(2):
                        ps = self.ps()
                        for c4 in range(4):
                            c = g * 4 + c4
                            self.tr(ps.t[:, c4 * 128:(c4 + 1) * 128], xin.t[:, c * 128:(c + 1) * 128], R=[xin], W=[ps])
                        self.cp("act" if g == 0 else "dve", self.xT.t[:, g * 4:(g + 1) * 4, st * 128:(st + 1) * 128],
                                ps.t[:, :].rearrange("p (a b) -> p a b", a=4), R=[ps], W=[self.xT.c(g * 4 + i) for i in range(4)])
                for l in range(2):
                    if STAGE >= 1:
                        self.rmsnorm(l, "g1")
                    if STAGE >= 2:
                        self.ffn(l, self.W1g, self.W1u, self.W1d)
                    if STAGE >= 3:
                        self.rmsnorm(l, "gm")
                        self.mixer(l, tile, last)
                    if STAGE >= 8:
                        self.rmsnorm(l, "g2")
                        self.ffn(l, self.W2g, self.W2u, self.W2d)
                self.end_first_pass()
                A.reset()
                for st in range(NST):
                    yo = A.alloc([D], F32)
                    for g in range(2):
                        ps = self.ps()
                        for c4 in range(4):
                            c = g * 4 + c4
                            self.tr(ps.t[:, c4 * 128:(c4 + 1) * 128], self.xT.t[:, c, st * 128:(st + 1) * 128], R=self.xT.c(c), W=[ps])
                        self.cp("act" if g == 0 else "dve", yo.t[:, g * 512:(g + 1) * 512], ps.t[:, :], R=[ps], W=[yo])
                    r0 = tile * TT + st * 128
                    S.dma("pool", y[r0:r0 + 128, :], yo.t[:, :], R=[yo])
            S.finish()
            self.stats = (S.ninstr, S.nwaits)
        return nc

    def arena_bytes(self):
        TT = self.TT
        return 42 * 1024 if TT <= 256 else 72 * 1024


_NC_CACHE = {}


def _get_nc(SEQ, TT, NP, NPOOL):
    key = (SEQ, TT, NP, NPOOL)
    if key not in _NC_CACHE:
        b = Builder(SEQ, TT, NP, NPOOL)
        _NC_CACHE[key] = (b.build(), b)
    return _NC_CACHE[key]


def run_all(inp, n_cores, TT=256):
    B, SEQ, _ = inp["x_prompt"].shape
    DB, DS, _ = inp["x_sample"].shape
    NP = inp["page_table"].shape[1]
    NPOOL = inp["cache_k"].shape[1]
    PAST = NP * inp["cache_k"].shape[2]
    assert DS == 8 and DB == 4 * n_cores and B == n_cores
    nc, b = _get_nc(SEQ, TT, NP, NPOOL)
    f32 = lambda a: np.ascontiguousarray(np.asarray(a, np.float32))
    cst = _const_tables()
    prm = _params_table(inp)
    cos, sin = _rope_tables(np.arange(SEQ))
    coss, sins = _rope_tables(PAST + (np.arange(128) % 8))
    pw = f32(inp["pool_w"])
    pwbd = np.zeros((2, 2, 128, 128), np.float32)
    for l in range(2):
        for g in range(4):
            c, hf = g // 2, g % 2
            pwbd[l, c, hf * 64:(hf + 1) * 64, hf * 64:(hf + 1) * 64] = pw[l, g]
    shared = {
        "cst": cst, "prm": prm, "cos": cos, "sin": sin, "coss": coss, "sins": sins, "pwbd": pwbd, "cpw": f32(inp["conv_pw"]),
        "w1g": f32(inp["ffn1_w_gate"]), "w1u": f32(inp["ffn1_w_up"]), "w1d": f32(inp["ffn1_w_down"]),
        "w2g": f32(inp["ffn2_w_gate"]), "w2u": f32(inp["ffn2_w_up"]), "w2d": f32(inp["ffn2_w_down"]),
        "w_in": f32(inp["w_in"]), "w_out": f32(inp["w_out"]),
    }
    for nm, key in (("ck", "cache_k"), ("cv", "cache_v")):
        c5 = np.asarray(inp[key], np.float32)
        for l in range(2):
            c3 = c5[l].reshape(NPOOL, 16, 2048)
            for rc in range(16):
                shared["%s%d_%d" % (nm, l, rc)] = np.ascontiguousarray(c3[:, rc, :])
    xp = f32(inp["x_prompt"])
    xs = f32(inp["x_sample"])
    sp_, sc_, sh_ = f32(inp["state_pool"]), f32(inp["state_conv"]), f32(inp["state_hgrn"])
    pt = np.ascontiguousarray(np.asarray(inp["page_table"], np.int32))
    in_maps = []
    for c in range(n_cores):
        m = dict(shared, xp=xp[c])
        xs_c = np.zeros((256, D), np.float32)
        xs_c[0:32] = xs[4 * c:4 * c + 4].reshape(32, D)
        m["xs"] = xs_c
        m["sp"] = np.ascontiguousarray(sp_[:, 4 * c:4 * c + 4])
        m["sc"] = np.ascontiguousarray(sc_[:, 4 * c:4 * c + 4])
        m["sh"] = np.ascontiguousarray(sh_[:, 4 * c:4 * c + 4])
        m["pt"] = np.ascontiguousarray(pt[4 * c:4 * c + 4])
        in_maps.append(m)
    res = run_bass_kernel_spmd(nc, in_maps, core_ids=list(range(n_cores)))
    return res.results


def run_prompt(inp, SEQ, TT, n_cores):
    return run_all(inp, n_cores, TT)


def kernel(**inp):
    B, SEQ, _ = inp["x_prompt"].shape
    DB, DS, _ = inp["x_sample"].shape
    r = run_all(inp, B)
    cat = lambda k, ax: np.concatenate([r[c][k] for c in range(B)], axis=ax)
    y_prompt = np.stack([r[c]["y"] for c in range(B)])
    k_prompt = np.stack([r[c]["ok"] for c in range(B)], axis=1).reshape(2, B, SEQ, 4, 64)
    v_prompt = np.stack([r[c]["ov"] for c in range(B)], axis=1).reshape(2, B, SEQ, 4, 64)
    pool_prompt = np.stack([r[c]["opool"] for c in range(B)], axis=1)
    conv_prompt = np.stack([r[c]["oconv"] for c in range(B)], axis=1)
    hgrn_prompt = np.stack([r[c]["ohgrn"] for c in range(B)], axis=1)
    y_sample = cat("ys", 0).reshape(DB, DS, D)
    k_sample = cat("oks", 1).reshape(2, DB, DS, 4, 64)
    v_sample = cat("ovs", 1).reshape(2, DB, DS, 4, 64)
    pool_sample = cat("opools", 1)
    conv_sample = cat("oconvs", 1)
    hgrn_sample = cat("ohgrns", 1)
    return (y_prompt, y_sample, k_prompt, v_prompt, k_sample, v_sample,
            pool_prompt, pool_sample, conv_prompt, conv_sample, hgrn_prompt, hgrn_sample)
```

```python
import numpy as np
from contextlib import ExitStack
import concourse.bass as bass
import concourse.mybir as mybir
from concourse.bass_utils import run_bass_kernel_spmd

F32 = mybir.dt.float32
BF16 = mybir.dt.bfloat16
AF = mybir.ActivationFunctionType
ALU = mybir.AluOpType
AX = mybir.AxisListType

D = 1024
DFF = 2816
GW = 256
NEG = -1.0e30
STAGE = 9
SUB = 9
DO_SAMPLE = True
EPS = 1e-6
SCALE = 64 ** -0.5


class Cell:
    __slots__ = ("w", "r")

    def __init__(self):
        self.w = None
        self.r = {}


class Buf:
    def __init__(self, t, ncells=1):
        self.t = t
        self.cells = [Cell() for _ in range(ncells)]

    def c(self, i):
        return [self.cells[i]]

    @property
    def all(self):
        return self.cells


def _cells(items):
    out = []
    for it in items:
        if isinstance(it, Buf):
            out.extend(it.cells)
        elif isinstance(it, Cell):
            out.append(it)
        else:
            out.extend(_cells(it))
    return out


class Sched:
    ENG = ("pe", "act", "dve", "pool", "sp")

    def __init__(self, nc, stack, n_dma_sems=(10, 8)):
        self.nc = nc
        self.stack = stack
        self.eng = {"pe": nc.tensor, "act": nc.scalar, "dve": nc.vector, "pool": nc.gpsimd, "sp": nc.sync}
        self.sem = {}
        self.cnt = {}
        self.seen = {e: {} for e in self.ENG}
        for e in self.ENG:
            self.sem[e] = stack.enter_context(nc.semaphore("s_" + e))
            self.cnt[e] = 0
        self.dq = {}
        for q, n in zip(("sp", "pool"), n_dma_sems):
            sems = []
            for i in range(n):
                k = "d_%s_%d" % (q, i)
                self.sem[k] = stack.enter_context(nc.semaphore(k))
                self.cnt[k] = 0
                sems.append(k)
            self.dq[q] = [sems, 0]
        self.nwaits = 0
        self.ninstr = 0

    def sbuf(self, name, shape, dt, ncells=1):
        t = self.stack.enter_context(self.nc.sbuf_tensor("sb_" + name, list(shape), dt))
        return Buf(t, ncells)

    def psum(self, name, shape, dt=F32, ncells=1):
        t = self.stack.enter_context(self.nc.psum_tensor(name, list(shape), dt))
        return Buf(t, ncells)

    def _wait(self, e, key, val):
        if self.seen[e].get(key, 0) >= val:
            return
        self.eng[e].wait_ge(self.sem[key], val)
        self.seen[e][key] = val
        self.nwaits += 1

    def _deps(self, e, R, W):
        deps = {}
        for c in R:
            if c.w is not None:
                k, v = c.w
                if deps.get(k, 0) < v:
                    deps[k] = v
        for c in W:
            if c.w is not None:
                k, v = c.w
                if deps.get(k, 0) < v:
                    deps[k] = v
            for k, v in c.r.items():
                if deps.get(k, 0) < v:
                    deps[k] = v
        for k, v in deps.items():
            if k == e and e == "pe":
                continue
            self._wait(e, k, v)

    def _mark(self, key, val, R, W):
        for c in R:
            if c.r.get(key, 0) < val:
                c.r[key] = val
        for c in W:
            c.w = (key, val)
            c.r = {}

    def op(self, e, fn, R=(), W=(), SS=()):
        R = _cells(R)
        W = _cells(W)
        self._deps(e, R, W)
        for c in _cells(SS):
            if c.w is not None and c.w[0] == e:
                self._wait(e, e, c.w[1])
        ins = fn(self.eng[e])
        self.cnt[e] += 1
        ins.then_inc(self.sem[e], 1)
        self._mark(e, self.cnt[e], R, W)
        self.ninstr += 1
        return ins

    def dma(self, q, out, in_, R=(), W=(), **kw):
        R = _cells(R)
        W = _cells(W)
        sems, idx = self.dq[q]
        key = sems[idx % len(sems)]
        self.dq[q][1] = idx + 1
        if self.cnt[key] > 0:
            self._wait(q, key, self.cnt[key])
        self._deps(q, R, W)
        ins = self.eng[q].dma_start(out=out, in_=in_, **kw)
        self.cnt[key] += 16
        ins.then_inc(self.sem[key], 16)
        self._mark(key, self.cnt[key], R, W)
        self.ninstr += 1
        return ins

    def idma(self, out, in_, idx_ap, R=(), W=()):
        q = "pool"
        R = _cells(R)
        W = _cells(W)
        sems, idx = self.dq[q]
        key = sems[idx % len(sems)]
        self.dq[q][1] = idx + 1
        if self.cnt[key] > 0:
            self._wait(q, key, self.cnt[key])
        self._deps(q, R, W)
        ins = self.nc.gpsimd.indirect_dma_start(out=out, out_offset=None, in_=in_, in_offset=bass.IndirectOffsetOnAxis(ap=idx_ap, axis=0))
        self.cnt[key] += 16
        ins.then_inc(self.sem[key], 16)
        self._mark(key, self.cnt[key], R, W)
        self.ninstr += 1
        return ins

    def cc(self, kind, ins_, outs, R=(), W=(), ncores=8):
        q = "pool"
        R = _cells(R)
        W = _cells(W)
        sems, idx = self.dq[q]
        key = sems[idx % len(sems)]
        self.dq[q][1] = idx + 1
        if self.cnt[key] > 0:
            self._wait(q, key, self.cnt[key])
        self._deps(q, R, W)
        ins = self.nc.gpsimd.collective_compute(kind, op=ALU.bypass, replica_groups=[list(range(ncores))], ins=ins_, outs=outs)
        self.cnt[key] += 16
        ins.then_inc(self.sem[key], 16)
        self._mark(key, self.cnt[key], R, W)
        self.ninstr += 1
        return ins

    def barrier(self, sp=False):
        for e in (("pe", "act", "dve", "pool", "sp") if sp else ("pe", "act", "dve", "pool")):
            for k, v in self.cnt.items():
                if v > 0 and k != e:
                    self._wait(e, k, v)

    def finish(self):
        for k, v in self.cnt.items():
            if v > 0 and k != "sp":
                self._wait("sp", k, v)


class Arena:
    def __init__(self, S, name, nbytes):
        self.S = S
        self.t = S.stack.enter_context(S.nc.sbuf_tensor("sb_" + name, [128, nbytes // 4], F32))
        self.n = nbytes // 4
        self.off = 0

    def reset(self):
        self.S.barrier()
        self.off = 0

    def alloc(self, shape, dt=F32, ncells=1):
        n = 1
        for s in shape:
            n *= s
        words = n if dt == F32 else (n + 1) // 2
        assert self.off + words <= self.n, ("arena overflow", self.off, words, self.n)
        ap = self.t[:, self.off:self.off + words]
        self.off += words
        if dt != F32:
            ap = ap.bitcast(dt)
        if len(shape) == 2:
            ap = ap.rearrange("p (a b) -> p a b", a=shape[0])
        elif len(shape) == 3:
            ap = ap.rearrange("p (a b c) -> p a b c", a=shape[0], b=shape[1])
        return Buf(ap, ncells)


def _const_tables():
    p = np.arange(128)
    ident = np.eye(128, dtype=np.float32)
    ones = np.ones((128, 128), np.float32)
    same = (p[:, None] // 64) == (p[None, :] // 64)
    tri2 = (same & (p[:, None] <= p[None, :])).astype(np.float32)
    trirev2 = (same & (p[:, None] > p[None, :])).astype(np.float32)
    caus = (p[:, None] <= p[None, :]).astype(np.float32)
    blk64 = same.astype(np.float32)
    same8 = (p[:, None] // 8) == (p[None, :] // 8)
    tri8 = (same8 & (p[:, None] <= p[None, :])).astype(np.float32)
    trev8 = (same8 & (p[:, None] > p[None, :])).astype(np.float32)
    n64 = np.arange(64)
    pairm = ((p[:, None] // 2) == n64[None, :]).astype(np.float32) / 256.0
    pairmt = np.zeros((128, 128), np.float32)
    pairmt[:64, :] = ((p[None, :] // 2) == n64[:, None]).astype(np.float32)
    rowm = ((p[:, None] // 8) == np.arange(16)[None, :]).astype(np.float32)
    q8 = np.arange(8)
    ownm = (((p[:, None, None] // 8) == np.arange(16)[None, :, None]) & ((p[:, None, None] % 8) <= q8[None, None, :])).astype(np.float32).reshape(128, 128)
    hm = ((p[:, None] // 8) == np.arange(4)[None, :]).astype(np.float32)
    qsel = ((p[:, None] % 8) == q8[None, :]).astype(np.float32)
    return np.concatenate([ident, ones, tri2, trirev2, caus, blk64, tri8, trev8, pairm, pairmt, rowm, ownm, hm, qsel], axis=1)


C_ID, C_ONES, C_TRI2, C_TREV, C_CAUS, C_BLK, C_TRI8, C_TREV8 = [i * 128 for i in range(8)]
C_PAIRM = 1024
C_PAIRMT = C_PAIRM + 64
C_ROWM = C_PAIRMT + 128
C_OWNM = C_ROWM + 16
C_HM = C_OWNM + 128
C_QSEL = C_HM + 4
C_TOT = C_QSEL + 8


def _fm(v):
    return np.ascontiguousarray(np.asarray(v, np.float32).reshape(-1, 128).T)


PAR_L = {}


def _par_layout():
    off = 0
    for name, w in (("g1", 8), ("gm", 8), ("g2", 8), ("pscale", 2), ("cw", 62), ("cb", 2), ("clg", 2), ("clb", 2),
                    ("hnorm", 2), ("lbfm", 2), ("gq", 256), ("gk", 256), ("lbtm", 256)):
        PAR_L[name] = (off, w)
        off += w
    return off


PAR_W = _par_layout()
PAR_G = 2 * PAR_W
PG_EPS, PG_INVW, PG_ONE, PG_INVC = PAR_G, PAR_G + 1, PAR_G + 3, PAR_G + 4
PAR_TOT = PAR_G + 4 + 32


def _params_table(inp):
    T = np.zeros((128, PAR_TOT), np.float32)
    for l in range(2):
        b = l * PAR_W

        def put(name, arr):
            o, w = PAR_L[name]
            T[:, b + o:b + o + w] = arr

        put("g1", _fm(inp["ln_ffn1"][l]))
        put("gm", _fm(inp["ln_mix"][l]))
        put("g2", _fm(inp["ln_ffn2"][l]))
        put("pscale", _fm(inp["pool_scale"][l]))
        cw = np.asarray(inp["conv_w"][l], np.float32)
        put("cw", np.ascontiguousarray(cw.T.reshape(2, 128, 31).transpose(1, 0, 2)).reshape(128, 62))
        put("cb", _fm(inp["conv_b"][l]))
        put("clg", _fm(inp["conv_ln_g"][l]))
        put("clb", _fm(inp["conv_ln_b"][l]))
        put("hnorm", np.tile(np.asarray(inp["hgrn_norm"][l], np.float32).reshape(1, 64), (2, 1)).reshape(128, 1).repeat(2, axis=1))
        put("lbfm", _fm(inp["hgrn_lower_bounds"][l]))
        put("gq", np.tile(np.asarray(inp["q_norm"][l], np.float32).reshape(1, 64), (128, 4)))
        put("gk", np.tile(np.asarray(inp["k_norm"][l], np.float32).reshape(1, 64), (128, 4)))
        put("lbtm", np.tile(np.asarray(inp["hgrn_lower_bounds"][l], np.float32).reshape(1, 256), (128, 1)))
    T[:, PG_EPS] = EPS
    w_of = np.array([[2, 8], [4, 16]], np.float32)
    pw = w_of[(np.arange(128) // 64)]
    T[:, PG_INVW:PG_INVW + 2] = 1.0 / pw
    T[:, PG_ONE] = 1.0
    t = np.arange(16, dtype=np.float32)
    invc = 1.0 / np.minimum(pw[:, :, None], t[None, None, :] + 1.0)
    T[:, PG_INVC:PG_INVC + 32] = invc.reshape(128, 32)
    return T


def _rope_tables(pos):
    half = 8
    inv = (500000.0 ** (-np.arange(half, dtype=np.float32) * 2.0 / 16)).astype(np.float32)
    ang = pos.astype(np.float32)[:, None] * inv[None, :]
    cos = np.cos(ang).astype(np.float32)
    sin = np.sin(ang).astype(np.float32)
    return np.tile(cos, (1, 4)), np.tile(sin, (1, 4))


class Builder:
    def __init__(self, SEQ, TT, NP=128, NPOOL=5120):
        self.SEQ, self.TT = SEQ, TT
        self.NP, self.NPOOL = NP, NPOOL
        self.NT = SEQ // TT
        self.NST = TT // 128
        self.NQT = SEQ // 128
        self.NBLK = max(1, SEQ // 256)

    def mm(self, out, lhsT, rhs, start, stop, R, W, **kw):
        self.S.op("pe", lambda e: e.matmul(out, lhsT=lhsT, rhs=rhs, start=start, stop=stop, **kw), R=R, W=W)

    def tr(self, out, in_, R, W, n=128):
        idn = self.cst.t[0:n, C_ID:C_ID + n]
        self.S.op("pe", lambda e: e.transpose(out=out, in_=in_, identity=idn), R=list(R) + [self.cst], W=W)

    def act(self, out, in_, func, R, W, bias=None, scale=1.0, accum_out=None):
        kw = {}
        if bias is not None:
            kw["bias"] = bias
        if accum_out is not None:
            kw["accum_out"] = accum_out
        self.S.op("act", lambda e: e.activation(out=out, in_=in_, func=func, scale=scale, **kw), R=R, W=W)

    def tt(self, eng, out, in0, in1, op, R, W):
        self.S.op(eng, lambda e: e.tensor_tensor(out=out, in0=in0, in1=in1, op=op), R=R, W=W)

    def ts(self, eng, out, in0, s1, s2, op0, op1, R, W, SS=()):
        if op1 is None:
            self.S.op(eng, lambda e: e.tensor_scalar(out=out, in0=in0, scalar1=s1, scalar2=None, op0=op0), R=R, W=W, SS=SS)
        else:
            self.S.op(eng, lambda e: e.tensor_scalar(out=out, in0=in0, scalar1=s1, scalar2=s2, op0=op0, op1=op1), R=R, W=W, SS=SS)

    def stt(self, eng, out, in0, scalar, in1, op0, op1, R, W, SS=()):
        self.S.op(eng, lambda e: e.scalar_tensor_tensor(out=out, in0=in0, scalar=scalar, in1=in1, op0=op0, op1=op1), R=R, W=W, SS=SS)

    def cp(self, eng, out, in_, R, W):
        if eng == "act":
            self.S.op("act", lambda e: e.copy(out=out, in_=in_), R=R, W=W)
        else:
            self.S.op(eng, lambda e: e.tensor_copy(out=out, in_=in_), R=R, W=W)

    def ps(self):
        b = self.psb[self.psi % 6]
        self.psi += 1
        return b

    def par(self, l, name, j=0, n=1):
        o, w = PAR_L[name]
        c = l * PAR_W + o + j
        return self.prm.t[:, c:c + n]

    def wload(self, src, a, b, key):
        n = a * b
        if key not in self.wmap:
            idx = len(self.wmap)
            cell = Buf(None)
            self.wmap[key] = (idx, cell)
            i = self.wi
            self.wi += 1
            stg = self.wstg[i % len(self.wstg)]
            wb = self.wbf[i % len(self.wbf)]
            self.S.dma("sp", stg.t[:, 0:n].rearrange("p (a b) -> p a b", a=a), src, W=[stg])
            self.S.op("pool", lambda e: e.tensor_copy(out=wb.t[:, 0:n], in_=stg.t[:, 0:n]), R=[stg], W=[wb])
            self.S.dma("pool", self.wsc[idx, :, 0:n], wb.t[:, 0:n], R=[wb], W=[cell])
        else:
            idx, cell = self.wmap[key]
            wb = self.wring[self.wj % len(self.wring)]
            self.wj += 1
            self.S.dma("sp", wb.t[:, 0:n], self.wsc[idx, :, 0:n], R=[cell], W=[wb])
        return wb, wb.t[:, 0:n].rearrange("p (a b) -> p a b", a=a)

    def end_first_pass(self):
        if self.first_pass_done:
            return
        self.first_pass_done = True
        self.S.barrier(sp=True)
        extra = []
        for st in self.wstg:
            for hhalf in range(2):
                extra.append(Buf(st.t[:, hhalf * 1024:(hhalf + 1) * 1024].bitcast(BF16)))
        self.wring = list(self.wbf) + extra

    def rmsnorm(self, l, gname):
        S, TT = self.S, self.TT
        xT, hT, sq = self.xT, self.hT, self.sqb
        for c in range(8):
            self.tt("pool", sq.t[:, c, :], xT.t[:, c, :], xT.t[:, c, :], ALU.mult, R=xT.c(c), W=sq.c(c))
        ps = self.ps()
        for c in range(8):
            self.mm(ps.t[:, 0:TT], self.ones_bf.t[:, :], sq.t[:, c, :], c == 0, c == 7, R=[sq.c(c), self.ones_bf], W=[ps])
        rs = self.rstd
        self.act(rs.t[:, 0:TT], ps.t[:, 0:TT], AF.Sqrt, R=[ps, self.prm], W=[rs], bias=self.prm.t[:, PG_EPS:PG_EPS + 1], scale=1.0 / D)
        self.S.op("dve", lambda e: e.reciprocal(out=rs.t[:, 0:TT], in_=rs.t[:, 0:TT]), R=[rs], W=[rs])
        for c in range(8):
            self.stt("dve", hT.t[:, c, :], xT.t[:, c, :], self.par(l, gname, c), rs.t[:, 0:TT], ALU.mult, ALU.mult,
                     R=[xT.c(c), rs, self.prm], W=hT.c(c))

    def ffn(self, l, Wg, Wu, Wd):
        S, TT = self.S, self.TT
        xT, hT = self.xT, self.hT
        self.ar.reset()
        aT = self.ar.alloc([22, TT], BF16, ncells=22)
        sg = [self.ar.alloc([TT], F32), self.ar.alloc([TT], F32)]
        for j2 in range(11):
            wgb, wg = self.wload(Wg[l][:, j2 * 256:(j2 + 1) * 256].rearrange("(c p) n -> p c n", p=128), 8, 256, (id(Wg), l, "g", j2))
            wub, wu = self.wload(Wu[l][:, j2 * 256:(j2 + 1) * 256].rearrange("(c p) n -> p c n", p=128), 8, 256, (id(Wu), l, "u", j2))
            for jj in range(2):
                j = 2 * j2 + jj
                pg, pu = self.ps(), self.ps()
                for c in range(8):
                    self.mm(pg.t[:, 0:TT], wg[:, c, jj * 128:(jj + 1) * 128], hT.t[:, c, :], c == 0, c == 7, R=[wgb, hT.c(c)], W=[pg])
                for c in range(8):
                    self.mm(pu.t[:, 0:TT], wu[:, c, jj * 128:(jj + 1) * 128], hT.t[:, c, :], c == 0, c == 7, R=[wub, hT.c(c)], W=[pu])
                s = sg[j % 2]
                self.act(s.t[:, :], pg.t[:, 0:TT], AF.Silu, R=[pg], W=[s])
                self.tt("dve", aT.t[:, j, :], s.t[:, :], pu.t[:, 0:TT], ALU.mult, R=[s, pu], W=aT.c(j))
        for m in range(8):
            py = self.ps()
            for hf in range(2):
                wdb, wd = self.wload(Wd[l][hf * 1408:(hf + 1) * 1408, m * 128:(m + 1) * 128].rearrange("(j p) n -> p j n", p=128), 11, 128, (id(Wd), l, "d", m, hf))
                for jj in range(11):
                    j = hf * 11 + jj
                    self.mm(py.t[:, 0:TT], wd[:, jj, :], aT.t[:, j, :], j == 0, j == 21, R=[wdb, aT.c(j)], W=[py])
            self.stt("dve", xT.t[:, m, :], py.t[:, 0:TT], 0.5, xT.t[:, m, :], ALU.mult, ALU.add, R=[py, xT.c(m)], W=xT.c(m))

    def proj_fm(self, l, col0, dst, dcol0=0, post=None):
        TT, hT = self.TT, self.hT
        wb, w = self.wload(self.w_in[l][:, col0:col0 + 256].rearrange("(c p) n -> p c n", p=128), 8, 256, ("in", l, col0))
        for cc in range(2):
            ps = self.ps()
            for c in range(8):
                self.mm(ps.t[:, 0:TT], w[:, c, cc * 128:(cc + 1) * 128], hT.t[:, c, :], c == 0, c == 7, R=[wb, hT.c(c)], W=[ps])
            if post is not None:
                post(cc, ps)
            else:
                self.cp("act", dst.t[:, cc, dcol0:dcol0 + TT], ps.t[:, 0:TT], R=[ps], W=[dst])

    def proj_tm(self, l, col0, dst):
        hT = self.hT
        wb, w = self.wload(self.w_in[l][:, col0:col0 + 256].rearrange("(c p) n -> p c n", p=128), 8, 256, ("in", l, col0))
        for st in range(self.NST):
            ps = self.ps()
            for c in range(8):
                self.mm(ps.t[:, 0:256], hT.t[:, c, st * 128:(st + 1) * 128], w[:, c, :], c == 0, c == 7, R=[wb, hT.c(c)], W=[ps])
            self.cp("act", dst.t[:, st, :], ps.t[:, 0:256], R=[ps], W=dst.c(st))

    def store_rows_fm(self, src_ap_fn, src_buf, n, dram_ap):
        ps = self.ps()
        for c in range(2):
            self.tr(ps.t[0:n, c * 128:(c + 1) * 128], src_ap_fn(c), R=[src_buf], W=[ps])
        ob = self.ar.alloc([256], F32)
        self.cp("act", ob.t[0:n, :], ps.t[0:n, 0:256], R=[ps], W=[ob])
        self.S.dma("pool", dram_ap, ob.t[0:n, :], R=[ob])

    def pool_mix(self, l, tile, last):
        TT = self.TT
        UP = self.UP[l]
        W_ = TT + 15
        s2 = self.ar.alloc([2, W_], F32)
        s4 = self.ar.alloc([2, W_], F32)
        s8 = self.ar.alloc([2, W_], F32)
        s16 = self.ar.alloc([2, W_], F32)
        pooled = self.ar.alloc([2, TT], BF16)
        self.tt("dve", s2.t[:, :, 0:W_ - 1], UP.t[:, :, 1:W_], UP.t[:, :, 0:W_ - 1], ALU.add, R=[UP], W=[s2])
        self.tt("dve", s4.t[:, :, 0:W_ - 3], s2.t[:, :, 2:W_ - 1], s2.t[:, :, 0:W_ - 3], ALU.add, R=[s2], W=[s4])
        self.tt("dve", s8.t[:, :, 0:W_ - 7], s4.t[:, :, 4:W_ - 3], s4.t[:, :, 0:W_ - 7], ALU.add, R=[s4], W=[s8])
        self.tt("dve", s16.t[:, :, 0:W_ - 15], s8.t[:, :, 8:W_ - 7], s8.t[:, :, 0:W_ - 15], ALU.add, R=[s8], W=[s16])
        srcs = {(0, 0): (s2, 14), (1, 0): (s4, 12), (0, 1): (s8, 8), (1, 1): (s16, 0)}
        for (hf, c), (sb, o) in srcs.items():
            r = slice(hf * 64, hf * 64 + 64)
            self.stt("dve", pooled.t[r, c, :], sb.t[r, c, o:o + TT], self.prm.t[r, PG_INVW + c:PG_INVW + c + 1], UP.t[r, c, 15:15 + TT],
                     ALU.mult, ALU.subtract, R=[sb, UP, self.prm], W=[pooled])
            if tile == 0:
                tmp = self.ar.alloc([16], F32)
                self.tt("dve", tmp.t[r, :], sb.t[r, c, o:o + 16], self.prm.t[r, PG_INVC + c * 16:PG_INVC + c * 16 + 16], ALU.mult, R=[sb, self.prm], W=[tmp])
                self.tt("dve", pooled.t[r, c, 0:16], tmp.t[r, :], UP.t[r, c, 15:31], ALU.subtract, R=[tmp, UP], W=[pooled])
        for c in range(2):
            ps = self.ps()
            self.mm(ps.t[:, 0:TT], self.pwbd.t[:, l, c, :], pooled.t[:, c, :], True, True, R=[self.pwbd, pooled], W=[ps])
            self.ts("dve", self.ycat.t[:, c, :], ps.t[:, 0:TT], self.par(l, "pscale", c), None, ALU.mult, None, R=[ps, self.prm], W=self.ycat.c(c))
        if last:
            self.store_rows_fm(lambda c: UP.t[:, c, TT:TT + 15], UP, 15, self.o_pool[l])
        self.cp("pool", UP.t[:, :, 0:15], UP.t[:, :, TT:TT + 15], R=[UP], W=[UP])

    def conv_mix(self, l, tile, last):
        TT = self.TT
        G = self.G[l]
        acc = self.ar.alloc([2, TT], F32, ncells=2)
        sq = self.ar.alloc([2, TT], F32, ncells=2)
        eng = "dve"
        for c in range(2):
            self.ts(eng, acc.t[:, c, :], G.t[:, c, 0:TT], self.par(l, "cw", c * 31), self.par(l, "cb", c), ALU.mult, ALU.add, R=[G, self.prm], W=acc.c(c))
            self.ts(eng, sq.t[:, c, :], G.t[:, c, 1:1 + TT], self.par(l, "cw", c * 31 + 1), None, ALU.mult, None, R=[G, self.prm], W=sq.c(c))
        for j in range(2, 31):
            for c in range(2):
                tgt = acc if j % 2 == 0 else sq
                self.stt(eng, tgt.t[:, c, :], G.t[:, c, j:j + TT], self.par(l, "cw", c * 31 + j), tgt.t[:, c, :], ALU.mult, ALU.add,
                         R=[G, self.prm, tgt.c(c)], W=tgt.c(c))
        for c in range(2):
            self.tt(eng, acc.t[:, c, :], acc.t[:, c, :], sq.t[:, c, :], ALU.add, R=[acc.c(c), sq.c(c)], W=acc.c(c))
            self.tt(eng, sq.t[:, c, :], acc.t[:, c, :], acc.t[:, c, :], ALU.mult, R=acc.c(c), W=sq.c(c))
        self.conv_tail(l, acc, sq)
        if last:
            self.store_rows_fm(lambda c: G.t[:, c, TT:TT + 30], G, 30, self.o_conv[l])
        self.cp("pool", G.t[:, :, 0:30], G.t[:, :, TT:TT + 30], R=[G], W=[G])

    def conv_tail(self, l, acc, sq):
        TT = self.TT
        psm, psq = self.ps(), self.ps()
        onesf = self.cst.t[:, C_ONES:C_ONES + 128]
        for c in range(2):
            self.mm(psm.t[:, 0:TT], onesf, acc.t[:, c, :], c == 0, c == 1, R=[self.cst, acc.c(c)], W=[psm])
        for c in range(2):
            self.mm(psq.t[:, 0:TT], onesf, sq.t[:, c, :], c == 0, c == 1, R=[self.cst, sq.c(c)], W=[psq])
        mean = self.ar.alloc([TT], F32)
        m2 = self.ar.alloc([TT], F32)
        rstd = self.ar.alloc([TT], F32)
        self.S.op("act", lambda e: e.mul(out=mean.t[:, :], in_=psm.t[:, 0:TT], mul=1.0 / 256), R=[psm], W=[mean])
        self.tt("dve", m2.t[:, :], mean.t[:, :], mean.t[:, :], ALU.mult, R=[mean], W=[m2])
        self.stt("dve", rstd.t[:, :], psq.t[:, 0:TT], 1.0 / 256, m2.t[:, :], ALU.mult, ALU.subtract, R=[psq, m2], W=[rstd])
        self.act(rstd.t[:, :], rstd.t[:, :], AF.Sqrt, R=[rstd, self.prm], W=[rstd], bias=self.prm.t[:, PG_EPS:PG_EPS + 1])
        self.S.op("dve", lambda e: e.reciprocal(out=rstd.t[:, :], in_=rstd.t[:, :]), R=[rstd], W=[rstd])
        zs = self.ar.alloc([2, TT], BF16, ncells=2)
        for c in range(2):
            self.tt("dve", acc.t[:, c, :], acc.t[:, c, :], mean.t[:, :], ALU.subtract, R=[acc.c(c), mean], W=acc.c(c))
            self.tt("dve", acc.t[:, c, :], acc.t[:, c, :], rstd.t[:, :], ALU.mult, R=[acc.c(c), rstd], W=acc.c(c))
            self.act(zs.t[:, c, :], acc.t[:, c, :], AF.Silu, R=[acc.c(c), self.prm], W=zs.c(c), bias=self.par(l, "clb", c), scale=self.par(l, "clg", c))
        for co in range(2):
            ps = self.ps()
            for c in range(2):
                self.mm(ps.t[:, 0:TT], self.pw.t[:, l, c, co * 128:(co + 1) * 128], zs.t[:, c, :], c == 0, c == 1, R=[self.pw, zs.c(c)], W=[ps])
            self.cp("act", self.ycat.t[:, 4 + co, :], ps.t[:, 0:TT], R=[ps], W=self.ycat.c(4 + co))

    def hgrn_mix(self, l, tile, last):
        TT = self.TT
        A = self.ar
        HQ, HF, HG, HFt, HIt = self.HQ, self.HF, self.HG, self.HFt, self.HIt
        Sm = self.Sst[l]
        tri2 = self.cst.t[:, C_TRI2:C_TRI2 + 128]
        trev = self.cst.t[:, C_TREV:C_TREV + 128]
        one = self.prm.t[:, PG_ONE:PG_ONE + 1]
        for st in range(self.NST):
            cs = slice(st * 128, (st + 1) * 128)
            mark = A.off
            sig = A.alloc([256], F32)
            logf = A.alloc([256], F32)
            kin = A.alloc([256], F32)
            self.act(sig.t[:, :], HFt.t[:, st, :], AF.Sigmoid, R=HFt.c(st), W=[sig])
            self.tt("dve", sig.t[:, :], sig.t[:, :], self.omltm.t[:, l, :], ALU.mult, R=[sig, self.omltm], W=[sig])
            self.tt("dve", sig.t[:, :], sig.t[:, :], self.lbtm.t[:, l, :], ALU.add, R=[sig, self.lbtm], W=[sig])
            self.act(logf.t[:, :], sig.t[:, :], AF.Ln, R=[sig], W=[logf])
            self.ts("dve", kin.t[:, :], sig.t[:, :], -1.0, 1.0, ALU.mult, ALU.add, R=[sig], W=[kin])
            prev = self.ps()
            self.mm(prev.t[:, 0:256], trev, logf.t[:, :], True, True, R=[self.cst, logf], W=[prev])
            pbc = self.ps()
            for kc in range(2):
                self.mm(pbc.t[:, kc * 128:(kc + 1) * 128], logf.t[:, kc * 128:(kc + 1) * 128], tri2, True, True, R=[self.cst, logf], W=[pbc])
            er = A.alloc([256], F32)
            kh = A.alloc([256], BF16)
            vb = A.alloc([256], BF16)
            self.act(er.t[:, :], prev.t[:, 0:256], AF.Exp, R=[prev], W=[er])
            self.tt("dve", kh.t[:, :], kin.t[:, :], er.t[:, :], ALU.mult, R=[kin, er], W=[kh])
            self.cp("pool", vb.t[:, :], HIt.t[:, st, :], R=HIt.c(st), W=[vb])
            E = A.alloc([2, 128], F32)
            Ei = A.alloc([2, 128], F32)
            self.act(E.t[:, :, :], pbc.t[:, 0:256].rearrange("p (a b) -> p a b", a=2), AF.Exp, R=[pbc], W=[E])
            self.act(Ei.t[:, :, :], pbc.t[:, 0:256].rearrange("p (a b) -> p a b", a=2), AF.Exp, R=[pbc], W=[Ei], scale=-1.0)
            sT = A.alloc([2, 128], F32)
            self.act(sT.t[:, :, :], HF.t[:, :, cs], AF.Sigmoid, R=[HF], W=[sT])
            for c in range(2):
                self.ts("dve", sT.t[:, c, :], sT.t[:, c, :], self.nomlfm.t[:, l, c:c + 1], self.omlfm.t[:, l, c:c + 1], ALU.mult, ALU.add,
                        R=[sT, self.nomlfm, self.omlfm], W=[sT])
            qt_ = A.alloc([2, 128], BF16)
            kt_ = A.alloc([2, 128], BF16)
            self.tt("dve", qt_.t[:, :, :], HQ.t[:, :, cs], E.t[:, :, :], ALU.mult, R=[HQ, E], W=[qt_])
            self.tt("dve", kt_.t[:, :, :], sT.t[:, :, :], Ei.t[:, :, :], ALU.mult, R=[sT, Ei], W=[kt_])
            patt = [self.ps(), self.ps()]
            for h in range(4):
                pr, hf, r0 = h // 2, h % 2, (h % 2) * 64
                self.mm(patt[hf].t[:, pr * 128:(pr + 1) * 128], kt_.t[r0:r0 + 64, pr, :], qt_.t[r0:r0 + 64, pr, :], True, True, R=[kt_, qt_], W=[patt[hf]])
            att = A.alloc([4, 128], BF16)
            for h in range(4):
                pr, hf = h // 2, h % 2
                self.tt("dve", att.t[:, h, :], patt[hf].t[:, pr * 128:(pr + 1) * 128], tri2, ALU.mult, R=[patt[hf], self.cst], W=[att])
            pU = [self.ps(), self.ps()]
            for ci in range(2):
                for pr in range(2):
                    k0 = pr * 128
                    self.mm(pU[ci].t[:, k0:k0 + 128], kh.t[ci * 64:(ci + 1) * 64, pr * 128:(pr + 1) * 128], vb.t[ci * 64:(ci + 1) * 64, pr * 128:(pr + 1) * 128],
                            True, True, R=[kh, vb], W=[pU[ci]])
            sbf = [A.alloc([2, 64], BF16), A.alloc([2, 64], BF16)]
            self.cp("act", sbf[0].t[:, :, :], Sm.t[:, :, :], R=[Sm], W=[sbf[0]])
            for ci in range(2):
                for pr in range(2):
                    for hf in range(2):
                        r = slice(hf * 64, hf * 64 + 64)
                        k0 = pr * 128 + hf * 64
                        self.stt("dve", Sm.t[r, pr, :], Sm.t[r, pr, :], E.t[r, pr, ci * 64 + 63:ci * 64 + 64], pU[ci].t[r, k0:k0 + 64], ALU.mult, ALU.add,
                                 R=[Sm, E, pU[ci]], W=[Sm], SS=[E])
                if ci == 0:
                    self.cp("act", sbf[1].t[:, :, :], Sm.t[:, :, :], R=[Sm], W=[sbf[1]])
            po = [self.ps(), self.ps()]
            for h in range(4):
                pr, hf, r0 = h // 2, h % 2, (h % 2) * 64
                self.mm(po[hf].t[r0:r0 + 64, pr * 128:(pr + 1) * 128], vb.t[:, h * 64:(h + 1) * 64], att.t[:, h, :], True, False, R=[vb, att], W=[po[hf]])
                for ci in range(2):
                    self.mm(po[hf].t[r0:r0 + 64, pr * 128 + ci * 64:pr * 128 + ci * 64 + 64], sbf[ci].t[r0:r0 + 64, pr, :], qt_.t[r0:r0 + 64, pr, ci * 64:(ci + 1) * 64],
                            False, ci == 1, R=[sbf[ci], qt_], W=[po[hf]], skip_group_check=True)
            O = A.alloc([2, 128], F32)
            sqo = A.alloc([2, 128], BF16)
            for hf in range(2):
                r = slice(hf * 64, hf * 64 + 64)
                self.cp("act", O.t[r, :, :], po[hf].t[r, 0:256].rearrange("p (a b) -> p a b", a=2), R=[po[hf]], W=[O])
            self.tt("pool", sqo.t[:, :, :], O.t[:, :, :], O.t[:, :, :], ALU.mult, R=[O], W=[sqo])
            pss = self.ps()
            for pr in range(2):
                self.mm(pss.t[:, pr * 128:(pr + 1) * 128], self.blk_bf.t[:, :], sqo.t[:, pr, :], True, True, R=[self.blk_bf, sqo], W=[pss])
            rs = A.alloc([2, 128], F32)
            self.act(rs.t[:, :, :], pss.t[:, 0:256].rearrange("p (a b) -> p a b", a=2), AF.Sqrt, R=[pss, self.prm], W=[rs],
                     bias=self.prm.t[:, PG_EPS:PG_EPS + 1], scale=1.0 / 64)
            self.S.op("dve", lambda e: e.reciprocal(out=rs.t[:, :, :], in_=rs.t[:, :, :]), R=[rs], W=[rs])
            sgt = A.alloc([2, 128], F32)
            self.act(sgt.t[:, :, :], HG.t[:, :, cs], AF.Silu, R=[HG], W=[sgt])
            self.tt("dve", O.t[:, :, :], O.t[:, :, :], rs.t[:, :, :], ALU.mult, R=[O, rs], W=[O])
            for pr in range(2):
                self.stt("dve", self.ycat.t[:, 6 + pr, cs], O.t[:, pr, :], self.par(l, "hnorm", pr), sgt.t[:, pr, :], ALU.mult, ALU.mult,
                         R=[O, sgt, self.prm], W=self.ycat.c(6 + pr))
            self.S.barrier()
            A.off = mark
        if last:
            self.S.dma("pool", self.o_hgrn[l].rearrange("(pr hf) k v -> (hf k) pr v", hf=2), Sm.t[:, :, :], R=[Sm])

    def qk_norm_rope(self, l, st, cos_ap, sin_ap, csbufs):
        A = self.ar
        QN = A.alloc([256], F32)
        KN = A.alloc([256], F32)
        for src, dst, gname in ((self.Qt, QN, "gq"), (self.Kt, KN, "gk")):
            sq = A.alloc([256], F32)
            ss = A.alloc([4], F32)
            self.tt("pool", sq.t[:, :], src.t[:, st, :], src.t[:, st, :], ALU.mult, R=src.c(st), W=[sq])
            for h in range(4):
                self.S.op("dve", lambda e: e.tensor_reduce(out=ss.t[:, h:h + 1], in_=sq.t[:, h * 64:(h + 1) * 64], axis=AX.X, op=ALU.add), R=[sq], W=[ss])
            self.act(ss.t[:, :], ss.t[:, :], AF.Sqrt, R=[ss, self.prm], W=[ss], bias=self.prm.t[:, PG_EPS:PG_EPS + 1], scale=1.0 / 64)
            self.S.op("dve", lambda e: e.reciprocal(out=ss.t[:, :], in_=ss.t[:, :]), R=[ss], W=[ss])
            for h in range(4):
                hs = slice(h * 64, (h + 1) * 64)
                self.stt("dve", dst.t[:, hs], src.t[:, st, hs], ss.t[:, h:h + 1], self.par(l, gname, h * 64, 64), ALU.mult, ALU.mult,
                         R=[src.c(st), ss, self.prm], W=[dst], SS=[ss])
            v3 = dst.t[:, :].rearrange("p (h d) -> p h d", h=4)
            x1, x2 = v3[:, :, 0:8], v3[:, :, 8:16]
            cos = cos_ap.rearrange("p (h d) -> p h d", h=4)
            sin = sin_ap.rearrange("p (h d) -> p h d", h=4)
            tmp = [A.alloc([4, 8], F32) for _ in range(4)]
            self.tt("dve", tmp[0].t[:, :, :], x1, cos, ALU.mult, R=[dst] + csbufs, W=[tmp[0]])
            self.tt("dve", tmp[1].t[:, :, :], x2, sin, ALU.mult, R=[dst] + csbufs, W=[tmp[1]])
            self.tt("dve", tmp[2].t[:, :, :], x2, cos, ALU.mult, R=[dst] + csbufs, W=[tmp[2]])
            self.tt("dve", tmp[3].t[:, :, :], x1, sin, ALU.mult, R=[dst] + csbufs, W=[tmp[3]])
            self.tt("dve", x1, tmp[0].t[:, :, :], tmp[1].t[:, :, :], ALU.subtract, R=[tmp[0], tmp[1]], W=[dst])
            self.tt("dve", x2, tmp[2].t[:, :, :], tmp[3].t[:, :, :], ALU.add, R=[tmp[2], tmp[3]], W=[dst])
        return QN, KN

    def moba_mix(self, l, tile, last):
        TT = self.TT
        A = self.ar
        KT, VX, KP, KM = self.KT[l], self.VX[l], self.KP[l], self.KM[l]
        caus = self.caus_bf.t[:, :]
        for st in range(self.NST):
            qt = tile * self.NST + st
            cs = slice(st * 128, (st + 1) * 128)
            mark = A.off
            QN, KN = self.qk_norm_rope(l, st, self.cos.t[:, qt, :], self.sin.t[:, qt, :], [self.cos, self.sin])
            self.S.dma("pool", self.o_k[l, qt * 128:(qt + 1) * 128, :], KN.t[:, :], R=[KN])
            self.S.dma("pool", self.o_v[l, qt * 128:(qt + 1) * 128, :], self.Vt.t[:, st, :], R=self.Vt.c(st))
            if SUB < 2:
                self.S.barrier()
                A.off = mark
                continue
            QTb = A.alloc([2, 128], BF16)
            QTf = A.alloc([2, 128], F32)
            pq = self.ps()
            pk = self.ps()
            for pr in range(2):
                self.tr(pq.t[:, pr * 128:(pr + 1) * 128], QN.t[:, pr * 128:(pr + 1) * 128], R=[QN], W=[pq])
                self.tr(pk.t[:, pr * 128:(pr + 1) * 128], KN.t[:, pr * 128:(pr + 1) * 128], R=[KN], W=[pk])
            for pr in range(2):
                self.cp("dve", QTb.t[:, pr, :], pq.t[:, pr * 128:(pr + 1) * 128], R=[pq], W=[QTb])
                self.cp("dve", QTf.t[:, pr, :], pq.t[:, pr * 128:(pr + 1) * 128], R=[pq], W=[QTf])
                self.cp("dve", KT.t[:, pr, qt * 128:(qt + 1) * 128], pk.t[:, pr * 128:(pr + 1) * 128], R=[pk], W=[KT.c(qt)])
                self.S.op("dve", lambda e: e.tensor_reduce(out=KP.t[:, pr, qt:qt + 1], in_=pk.t[:, pr * 128:(pr + 1) * 128], axis=AX.X, op=ALU.add),
                          R=[pk], W=[KP])
            for h in range(4):
                self.cp("dve", VX.t[:, qt, h, 0:64], self.Vt.t[:, st, h * 64:(h + 1) * 64], R=self.Vt.c(st), W=VX.c(qt))
            if qt % 2 == 1:
                n = qt // 2
                self.tt("dve", KM.t[:, :, n:n + 1], KP.t[:, :, qt - 1:qt], KP.t[:, :, qt:qt + 1], ALU.add, R=[KP], W=[KM])
                self.ts("dve", KM.t[:, :, n:n + 1], KM.t[:, :, n:n + 1], 1.0 / 256, None, ALU.mult, None, R=[KM], W=[KM])
            if SUB < 3:
                self.S.barrier()
                A.off = mark
                continue
            ob = qt // 2
            dense = ob <= 3
            SEL = None
            if not dense:
                pg = [self.ps(), self.ps()]
                for h in range(4):
                    pr, hf, r0 = h // 2, h % 2, (h % 2) * 64
                    self.mm(pg[hf].t[:, pr * 16:pr * 16 + ob], QTf.t[r0:r0 + 64, pr, :], KM.t[r0:r0 + 64, pr, 0:ob], True, True, R=[QTf, KM], W=[pg[hf]])
                GATE = A.alloc([4, 16], F32)
                SEL = A.alloc([4, 16], F32)
                mx = A.alloc([4, 8], F32)
                self.S.op("pool", lambda e: e.memset(GATE.t[:, :, :], NEG), W=[GATE])
                for h in range(4):
                    pr, hf = h // 2, h % 2
                    self.cp("dve", GATE.t[:, h, 0:ob], pg[hf].t[:, pr * 16:pr * 16 + ob], R=[pg[hf]], W=[GATE])
                for h in range(4):
                    self.S.op("dve", lambda e: e.max(out=mx.t[:, h, :], in_=GATE.t[:, h, :]), R=[GATE], W=[mx])
                    self.ts("dve", SEL.t[:, h, :], GATE.t[:, h, :], mx.t[:, h, 2:3], None, ALU.is_ge, None, R=[GATE, mx], W=[SEL], SS=[mx])
            ACC = A.alloc([4, 65], F32)

            def group(kts, pacc):
                self.mm(pacc.t[:, 0:260], self.zero_bf.t[:, 0:128], self.zero_bf.t[:, 0:260], True, False, R=[self.zero_bf], W=[pacc])
                for i, kt in enumerate(kts):
                    ps_s = [self.ps(), self.ps()]
                    for h in range(4):
                        pr, hf, r0 = h // 2, h % 2, (h % 2) * 64
                        self.mm(ps_s[hf].t[:, pr * 128:(pr + 1) * 128], KT.t[r0:r0 + 64, pr, kt * 128:(kt + 1) * 128], QTb.t[r0:r0 + 64, pr, :], True, True,
                                R=[KT.c(kt), QTb], W=[ps_s[hf]])
                    PT = self.PT[self.pti % 2]
                    self.pti += 1
                    for hf in range(2):
                        self.act(PT.t[:, :].rearrange("p (pr hf q) -> p hf pr q", pr=2, hf=2)[:, hf], ps_s[hf].t[:, 0:256].rearrange("p (a b) -> p a b", a=2),
                                 AF.Exp, R=[ps_s[hf]], W=[PT], scale=SCALE)
                    if kt == qt:
                        for h in range(4):
                            self.tt("dve", PT.t[:, h * 128:(h + 1) * 128], PT.t[:, h * 128:(h + 1) * 128], caus, ALU.mult, R=[PT, self.caus_bf], W=[PT])
                    for h in range(4):
                        self.mm(pacc.t[:, h * 65:(h + 1) * 65], PT.t[:, h * 128:(h + 1) * 128], VX.t[:, kt, h, 0:65], False, (i == len(kts) - 1 and h == 3),
                                R=[PT, VX.c(kt)], W=[pacc], skip_group_check=True)

            pown = self.psb[6]
            own_kts = list(range(0 if dense else 2 * ob, qt + 1))
            group(own_kts, pown)
            self.cp("act", ACC.t[:, :, :], pown.t[:, 0:260].rearrange("p (h d) -> p h d", h=4), R=[pown], W=[ACC])
            if not dense:
                for n in range(ob):
                    pb = self.psb[7]
                    group([2 * n, 2 * n + 1], pb)
                    for h in range(4):
                        self.stt("dve", ACC.t[:, h, :], pb.t[:, h * 65:(h + 1) * 65], SEL.t[:, h, n:n + 1], ACC.t[:, h, :], ALU.mult, ALU.add,
                                 R=[pb, SEL, ACC], W=[ACC], SS=[SEL])
            rden = A.alloc([4], F32)
            Ot = A.alloc([256], F32)
            self.S.op("dve", lambda e: e.reciprocal(out=rden.t[:, :], in_=ACC.t[:, :, 64]), R=[ACC], W=[rden])
            for h in range(4):
                self.ts("dve", Ot.t[:, h * 64:(h + 1) * 64], ACC.t[:, h, 0:64], rden.t[:, h:h + 1], None, ALU.mult, None, R=[ACC, rden], W=[Ot], SS=[rden])
            po = self.ps()
            for pr in range(2):
                self.tr(po.t[:, pr * 128:(pr + 1) * 128], Ot.t[:, pr * 128:(pr + 1) * 128], R=[Ot], W=[po])
            for pr in range(2):
                self.cp("act", self.ycat.t[:, 2 + pr, cs], po.t[:, pr * 128:(pr + 1) * 128], R=[po], W=self.ycat.c(2 + pr))

    def mixer(self, l, tile, last):
        TT = self.TT
        A = self.ar
        A.reset()
        self.HQ = A.alloc([2, TT], F32)
        self.HF = A.alloc([2, TT], F32)
        self.HG = A.alloc([2, TT], F32)
        self.Qt = A.alloc([self.NST, 256], F32, ncells=self.NST)
        self.Kt = A.alloc([self.NST, 256], F32, ncells=self.NST)
        self.Vt = A.alloc([self.NST, 256], F32, ncells=self.NST)
        self.HFt = A.alloc([self.NST, 256], F32, ncells=self.NST)
        self.HIt = A.alloc([self.NST, 256], F32, ncells=self.NST)
        CA = A.alloc([2, TT], F32)
        UP, G = self.UP[l], self.G[l]
        self.proj_fm(l, 0, UP, 15)
        self.proj_tm(l, 256, self.Qt)
        self.proj_tm(l, 512, self.Kt)
        self.proj_tm(l, 768, self.Vt)
        self.proj_fm(l, 1024, CA)

        def glu(cc, ps):
            sb = A.alloc([TT], F32)
            self.act(sb.t[:, :], ps.t[:, 0:TT], AF.Sigmoid, R=[ps], W=[sb])
            self.tt("dve", G.t[:, cc, 30:30 + TT], CA.t[:, cc, :], sb.t[:, :], ALU.mult, R=[CA, sb], W=[G])
        self.proj_fm(l, 1280, None, post=glu)
        self.proj_fm(l, 1536, self.HQ)
        self.proj_fm(l, 1792, self.HF)
        self.proj_tm(l, 1792, self.HFt)
        self.proj_tm(l, 2048, self.HIt)
        self.proj_fm(l, 2304, self.HG)
        base = A.off
        for i, fn in enumerate((self.pool_mix, self.conv_mix, self.hgrn_mix, self.moba_mix)):
            if STAGE < 4 + i:
                continue
            fn(l, tile, last)
            self.S.barrier()
            A.off = base
        self.outproj(l)

    def outproj(self, l):
        TT = self.TT
        for mu in range(4):
            wb, w = self.wload(self.w_out[l][:, mu * 256:(mu + 1) * 256].rearrange("(c p) n -> p c n", p=128), 8, 256, ("out", l, mu))
            for mm_ in range(2):
                m = mu * 2 + mm_
                ps = self.ps()
                for c in range(8):
                    self.mm(ps.t[:, 0:TT], w[:, c, mm_ * 128:(mm_ + 1) * 128], self.ycat.t[:, c, :], c == 0, c == 7, R=[wb, self.ycat.c(c)], W=[ps])
                self.tt("dve", self.xT.t[:, m, :], self.xT.t[:, m, :], ps.t[:, 0:TT], ALU.add, R=[ps, self.xT.c(m)], W=self.xT.c(m))


    def pool_s(self, l, Us):
        A = self.ar
        UPs = A.alloc([2, 4 * 23], F32)
        v = [UPs.t[:, c, :].rearrange("p (b i) -> p b i", b=4) for c in range(2)]
        self.S.op("pool", lambda e: e.memset(UPs.t[:, :, :], 0.0), W=[UPs])
        stg = A.alloc([256], F32)
        self.S.dma("pool", stg.t[0:60, :], self.sp_d[l].rearrange("s i c -> (s i) c"), W=[stg])
        ps = self.ps()
        for c in range(2):
            self.tr(ps.t[:, c * 64:c * 64 + 60], stg.t[0:60, c * 128:(c + 1) * 128], R=[stg], W=[ps], n=60)
        for c in range(2):
            self.cp("dve", v[c][:, 0:4, 0:15], ps.t[:, c * 64:c * 64 + 60].rearrange("p (s i) -> p s i", s=4), R=[ps], W=[UPs])
            self.cp("dve", v[c][:, :, 15:23], Us.t[:, c, 0:32].rearrange("p (b t) -> p b t", b=4), R=[Us], W=[UPs])
        sb = [A.alloc([2, 4 * 23], F32) for _ in range(4)]
        sv = [[b_.t[:, c, :].rearrange("p (b i) -> p b i", b=4) for c in range(2)] for b_ in sb]
        for c in range(2):
            self.tt("dve", sv[0][c][:, :, 0:22], v[c][:, :, 1:23], v[c][:, :, 0:22], ALU.add, R=[UPs], W=[sb[0]])
            self.tt("dve", sv[1][c][:, :, 0:20], sv[0][c][:, :, 2:22], sv[0][c][:, :, 0:20], ALU.add, R=[sb[0]], W=[sb[1]])
            self.tt("dve", sv[2][c][:, :, 0:16], sv[1][c][:, :, 4:20], sv[1][c][:, :, 0:16], ALU.add, R=[sb[1]], W=[sb[2]])
            self.tt("dve", sv[3][c][:, :, 0:8], sv[2][c][:, :, 8:16], sv[2][c][:, :, 0:8], ALU.add, R=[sb[2]], W=[sb[3]])
        pooled = A.alloc([2, 256], BF16)
        self.S.op("pool", lambda e: e.memset(pooled.t[:, :, :], 0.0), W=[pooled])
        srcs = {(0, 0): (0, 14), (1, 0): (1, 12), (0, 1): (2, 8), (1, 1): (3, 0)}
        for (hf, c), (si, o) in srcs.items():
            r = slice(hf * 64, hf * 64 + 64)
            self.stt("dve", pooled.t[r, c, 0:32].rearrange("p (b t) -> p b t", b=4), sv[si][c][r, :, o:o + 8], self.prm.t[r, PG_INVW + c:PG_INVW + c + 1],
                     v[c][r, :, 15:23], ALU.mult, ALU.subtract, R=[sb[si], UPs, self.prm], W=[pooled])
        for c in range(2):
            ps2 = self.ps()
            self.mm(ps2.t[:, 0:256], self.pwbd.t[:, l, c, :], pooled.t[:, c, :], True, True, R=[self.pwbd, pooled], W=[ps2])
            self.ts("dve", self.ycat.t[:, c, :], ps2.t[:, 0:256], self.par(l, "pscale", c), None, ALU.mult, None, R=[ps2, self.prm], W=self.ycat.c(c))
        tmp = A.alloc([2, 60], F32)
        pso = self.ps()
        for c in range(2):
            self.cp("dve", tmp.t[:, c, :].rearrange("p (s i) -> p s i", s=4), v[c][:, 0:4, 8:23], R=[UPs], W=[tmp])
            self.tr(pso.t[0:60, c * 128:(c + 1) * 128], tmp.t[:, c, :], R=[tmp], W=[pso])
        ob = A.alloc([256], F32)
        self.cp("act", ob.t[0:60, :], pso.t[0:60, 0:256], R=[pso], W=[ob])
        self.S.dma("pool", self.o_pools[l].rearrange("s i c -> (s i) c"), ob.t[0:60, :], R=[ob])

    def conv_s(self, l, Gn):
        A = self.ar
        Gs = A.alloc([2, 4 * 38], F32)
        v = [Gs.t[:, c, :].rearrange("p (b i) -> p b i", b=4) for c in range(2)]
        self.S.op("pool", lambda e: e.memset(Gs.t[:, :, :], 0.0), W=[Gs])
        stg = A.alloc([256], F32)
        self.S.dma("pool", stg.t[0:120, :], self.sc_d[l].rearrange("s i c -> (s i) c"), W=[stg])
        ps = self.ps()
        for c in range(2):
            self.tr(ps.t[:, c * 128:c * 128 + 120], stg.t[0:120, c * 128:(c + 1) * 128], R=[stg], W=[ps], n=120)
        for c in range(2):
            self.cp("dve", v[c][:, 0:4, 0:30], ps.t[:, c * 128:c * 128 + 120].rearrange("p (s i) -> p s i", s=4), R=[ps], W=[Gs])
            self.cp("dve", v[c][:, :, 30:38], Gn.t[:, c, 0:32].rearrange("p (b t) -> p b t", b=4), R=[Gn], W=[Gs])
        acc = A.alloc([2, 256], F32, ncells=2)
        sq = A.alloc([2, 256], F32, ncells=2)
        self.S.op("pool", lambda e: e.memset(acc.t[:, :, :], 0.0), W=[acc])
        for c in range(2):
            av = acc.t[:, c, 0:32].rearrange("p (b t) -> p b t", b=4)
            self.ts("dve", av, v[c][:, :, 0:8], self.par(l, "cw", c * 31), self.par(l, "cb", c), ALU.mult, ALU.add, R=[Gs, self.prm], W=acc.c(c))
            for j in range(1, 31):
                self.stt("dve", av, v[c][:, :, j:j + 8], self.par(l, "cw", c * 31 + j), av, ALU.mult, ALU.add, R=[Gs, self.prm, acc.c(c)], W=acc.c(c))
            self.tt("dve", sq.t[:, c, :], acc.t[:, c, :], acc.t[:, c, :], ALU.mult, R=acc.c(c), W=sq.c(c))
        self.conv_tail(l, acc, sq)
        tmp = A.alloc([2, 120], F32)
        pso = self.ps()
        for c in range(2):
            self.cp("dve", tmp.t[:, c, :].rearrange("p (s i) -> p s i", s=4), v[c][:, 0:4, 8:38], R=[Gs], W=[tmp])
            self.tr(pso.t[0:120, c * 128:(c + 1) * 128], tmp.t[:, c, :], R=[tmp], W=[pso])
        ob = A.alloc([256], F32)
        self.cp("act", ob.t[0:120, :], pso.t[0:120, 0:256], R=[pso], W=[ob])
        self.S.dma("pool", self.o_convs[l].rearrange("s i c -> (s i) c"), ob.t[0:120, :], R=[ob])

    def hgrn_s(self, l):
        A = self.ar
        HQ, HF, HG, HFt, HIt = self.HQ, self.HF, self.HG, self.HFt, self.HIt
        tri = self.cst.t[:, C_TRI8:C_TRI8 + 128]
        trev = self.cst.t[:, C_TREV8:C_TREV8 + 128]
        st = 0
        cs = slice(0, 128)
        Ss = A.alloc([4, 2, 64], F32)
        self.S.dma("pool", Ss.t[:, :, :, :], self.sh_d[l].rearrange("s (pr hf) k v -> (hf k) s pr v", hf=2), W=[Ss])
        sig = A.alloc([256], F32)
        logf = A.alloc([256], F32)
        kin = A.alloc([256], F32)
        self.act(sig.t[:, :], HFt.t[:, st, :], AF.Sigmoid, R=HFt.c(st), W=[sig])
        self.tt("dve", sig.t[:, :], sig.t[:, :], self.omltm.t[:, l, :], ALU.mult, R=[sig, self.omltm], W=[sig])
        self.tt("dve", sig.t[:, :], sig.t[:, :], self.lbtm.t[:, l, :], ALU.add, R=[sig, self.lbtm], W=[sig])
        self.act(logf.t[:, :], sig.t[:, :], AF.Ln, R=[sig], W=[logf])
        self.ts("dve", kin.t[:, :], sig.t[:, :], -1.0, 1.0, ALU.mult, ALU.add, R=[sig], W=[kin])
        prev = self.ps()
        self.mm(prev.t[:, 0:256], trev, logf.t[:, :], True, True, R=[self.cst, logf], W=[prev])
        pbc = self.ps()
        for kc in range(2):
            self.mm(pbc.t[:, kc * 128:(kc + 1) * 128], logf.t[:, kc * 128:(kc + 1) * 128], tri, True, True, R=[self.cst, logf], W=[pbc])
        er = A.alloc([256], F32)
        kh = A.alloc([256], F32)
        vb = A.alloc([256], BF16)
        self.act(er.t[:, :], prev.t[:, 0:256], AF.Exp, R=[prev], W=[er])
        self.tt("dve", kh.t[:, :], kin.t[:, :], er.t[:, :], ALU.mult, R=[kin, er], W=[kh])
        self.cp("pool", vb.t[:, :], HIt.t[:, st, :], R=HIt.c(st), W=[vb])
        E = A.alloc([2, 128], F32)
        Ei = A.alloc([2, 128], F32)
        for kc in range(2):
            self.act(E.t[:, kc, :], pbc.t[:, kc * 128:(kc + 1) * 128], AF.Exp, R=[pbc], W=[E])
            self.act(Ei.t[:, kc, :], pbc.t[:, kc * 128:(kc + 1) * 128], AF.Exp, R=[pbc], W=[Ei], scale=-1.0)
        sT = A.alloc([2, 128], F32)
        self.act(sT.t[:, :, :], HF.t[:, :, cs], AF.Sigmoid, R=[HF], W=[sT])
        for c in range(2):
            self.ts("dve", sT.t[:, c, :], sT.t[:, c, :], self.nomlfm.t[:, l, c:c + 1], self.omlfm.t[:, l, c:c + 1], ALU.mult, ALU.add,
                    R=[sT, self.nomlfm, self.omlfm], W=[sT])
        qt_ = A.alloc([2, 128], BF16)
        kt_ = A.alloc([2, 128], BF16)
        self.tt("dve", qt_.t[:, :, :], HQ.t[:, :, cs], E.t[:, :, :], ALU.mult, R=[HQ, E], W=[qt_])
        self.tt("dve", kt_.t[:, :, :], sT.t[:, :, :], Ei.t[:, :, :], ALU.mult, R=[sT, Ei], W=[kt_])
        patt = [self.ps(), self.ps()]
        for h in range(4):
            pr, hf, r0 = h // 2, h % 2, (h % 2) * 64
            self.mm(patt[hf].t[:, pr * 128:(pr + 1) * 128], kt_.t[r0:r0 + 64, pr, :], qt_.t[r0:r0 + 64, pr, :], True, True, R=[kt_, qt_], W=[patt[hf]])
        att = A.alloc([4, 128], BF16)
        for h in range(4):
            pr, hf = h // 2, h % 2
            self.tt("dve", att.t[:, h, :], patt[hf].t[:, pr * 128:(pr + 1) * 128], tri, ALU.mult, R=[patt[hf], self.cst], W=[att])
        sbf = A.alloc([4, 2, 64], BF16)
        self.cp("act", sbf.t[:, :, :, :], Ss.t[:, :, :, :], R=[Ss], W=[sbf])
        po = [self.ps(), self.ps()]
        for h in range(4):
            pr, hf, r0 = h // 2, h % 2, (h % 2) * 64
            self.mm(po[hf].t[r0:r0 + 64, pr * 128:(pr + 1) * 128], vb.t[:, h * 64:(h + 1) * 64], att.t[:, h, :], True, False, R=[vb, att], W=[po[hf]])
            for sl in range(4):
                self.mm(po[hf].t[r0:r0 + 64, pr * 128 + sl * 8:pr * 128 + sl * 8 + 8], sbf.t[r0:r0 + 64, sl, pr, :], qt_.t[r0:r0 + 64, pr, sl * 8:(sl + 1) * 8],
                        False, sl == 3, R=[sbf, qt_], W=[po[hf]], skip_group_check=True)
        khm = A.alloc([256], BF16)
        for sl in range(4):
            self.ts("dve", khm.t[:, :], kh.t[:, :], self.cst.t[:, C_ROWM + sl:C_ROWM + sl + 1], None, ALU.mult, None, R=[kh, self.cst], W=[khm])
            pU = self.ps()
            for pr in range(2):
                self.mm(pU.t[:, pr * 128:(pr + 1) * 128], khm.t[:, pr * 128:(pr + 1) * 128], vb.t[:, pr * 128:(pr + 1) * 128], True, True, R=[khm, vb], W=[pU])
            for pr in range(2):
                for hf in range(2):
                    r = slice(hf * 64, hf * 64 + 64)
                    k0 = pr * 128 + hf * 64
                    self.stt("dve", Ss.t[r, sl, pr, :], Ss.t[r, sl, pr, :], E.t[r, pr, sl * 8 + 7:sl * 8 + 8], pU.t[r, k0:k0 + 64], ALU.mult, ALU.add,
                             R=[Ss, E, pU], W=[Ss], SS=[E])
        self.S.dma("pool", self.o_hgrns[l].rearrange("s (pr hf) k v -> (hf k) s pr v", hf=2), Ss.t[:, :, :, :], R=[Ss])
        O = A.alloc([2, 128], F32)
        sqo = A.alloc([2, 128], BF16)
        for hf in range(2):
            r = slice(hf * 64, hf * 64 + 64)
            self.cp("act", O.t[r, :, :], po[hf].t[r, 0:256].rearrange("p (a b) -> p a b", a=2), R=[po[hf]], W=[O])
        self.tt("pool", sqo.t[:, :, :], O.t[:, :, :], O.t[:, :, :], ALU.mult, R=[O], W=[sqo])
        pss = self.ps()
        for pr in range(2):
            self.mm(pss.t[:, pr * 128:(pr + 1) * 128], self.blk_bf.t[:, :], sqo.t[:, pr, :], True, True, R=[self.blk_bf, sqo], W=[pss])
        rs = A.alloc([2, 128], F32)
        self.act(rs.t[:, :, :], pss.t[:, 0:256].rearrange("p (a b) -> p a b", a=2), AF.Sqrt, R=[pss, self.prm], W=[rs],
                 bias=self.prm.t[:, PG_EPS:PG_EPS + 1], scale=1.0 / 64)
        self.S.op("dve", lambda e: e.reciprocal(out=rs.t[:, :, :], in_=rs.t[:, :, :]), R=[rs], W=[rs])
        sgt = A.alloc([2, 128], F32)
        self.act(sgt.t[:, :, :], HG.t[:, :, cs], AF.Silu, R=[HG], W=[sgt])
        self.tt("dve", O.t[:, :, :], O.t[:, :, :], rs.t[:, :, :], ALU.mult, R=[O, rs], W=[O])
        for pr in range(2):
            self.stt("dve", self.ycat.t[:, 6 + pr, cs], O.t[:, pr, :], self.par(l, "hnorm", pr), sgt.t[:, pr, :], ALU.mult, ALU.mult,
                     R=[O, sgt, self.prm], W=self.ycat.c(6 + pr))
            self.S.op("pool", lambda e: e.memset(self.ycat.t[:, 6 + pr, 128:256], 0.0), W=self.ycat.c(6 + pr))

    def moba_s(self, l):
        A = self.ar
        S = self.S
        NP, NB = self.NP, self.NP // 2
        st = 0
        QTb = A.alloc([2, 128], BF16)
        KTn = A.alloc([2, 128], BF16)
        Vxn = A.alloc([4, 66], BF16)
        mark2 = A.off
        QN, KN = self.qk_norm_rope(l, st, self.coss.t[:, :], self.sins.t[:, :], [self.coss, self.sins])
        S.dma("pool", self.o_ks[l], KN.t[0:32, :], R=[KN])
        S.dma("pool", self.o_vs[l], self.Vt.t[0:32, st, :], R=self.Vt.c(st))
        pq, pk = self.ps(), self.ps()
        for pr in range(2):
            self.tr(pq.t[:, pr * 128:(pr + 1) * 128], QN.t[:, pr * 128:(pr + 1) * 128], R=[QN], W=[pq])
            self.tr(pk.t[:, pr * 128:(pr + 1) * 128], KN.t[:, pr * 128:(pr + 1) * 128], R=[KN], W=[pk])
        for pr in range(2):
            self.cp("dve", QTb.t[:, pr, :], pq.t[:, pr * 128:(pr + 1) * 128], R=[pq], W=[QTb])
            self.cp("dve", KTn.t[:, pr, :], pk.t[:, pr * 128:(pr + 1) * 128], R=[pk], W=[KTn])
        S.op("pool", lambda e: e.memset(Vxn.t[:, :, :], 1.0), W=[Vxn])
        for h in range(4):
            self.cp("dve", Vxn.t[:, h, 0:64], self.Vt.t[:, st, h * 64:(h + 1) * 64], R=self.Vt.c(st), W=[Vxn])
        S.barrier()
        A.off = mark2
        self.KTc = [A.alloc([2, 4, 128], BF16)]
        self.Pof = A.alloc([32], F32)
        Sall = A.alloc([128, 32], F32)
        Kc = A.alloc([8, 256], F32)
        Vcb = A.alloc([8, 4, 66], BF16)
        S.op("pool", lambda e: e.memset(Vcb.t[:, :, :, :], 1.0), W=[Vcb])
        P = [self.hT, self.sqb]
        Pv = [b_.t[:, :, :].rearrange("p a b -> p (a b)").rearrange("p (r c) -> p r c", c=32) for b_ in P]
        gT = A.alloc([32], F32)
        GATE = A.alloc([64], F32)
        SEL = A.alloc([64], F32)
        mx = A.alloc([8], F32)
        selTs = A.alloc([32], F32)
        selT2 = A.alloc([32], F32)
        Po = A.alloc([32], BF16)
        ACC = A.alloc([264], F32)
        Osel = A.alloc([64], F32)
        den = A.alloc([1], F32)
        Oh = A.alloc([2, 128], F32)
        hm = self.cst.t[:, C_HM:C_HM + 4]
        for sl in range(4):
            idx = self.ptab.t[0:NP, sl:sl + 1]
            for rc in range(16):
                S.idma(Kc.t[0:NP, :, :].rearrange("p a b -> p (a b)"), self.ck_d[l][rc], idx, R=[self.ptab], W=[Kc])
                pS = [self.psb[6], self.psb[7]]
                for r4 in range(2):
                    KTc = self.KTc[0]
                    for pr in range(2):
                        pt_ = self.ps()
                        for ri in range(4):
                            r = r4 * 4 + ri
                            self.tr(pt_.t[:, ri * 128:ri * 128 + NP], Kc.t[0:NP, r, pr * 128:(pr + 1) * 128], R=[Kc], W=[pt_], n=NP)
                        self.cp("act" if pr == 0 else "dve", KTc.t[:, pr, :, 0:NP], pt_.t[:, :].rearrange("p (a b) -> p a b", a=4)[:, :, 0:NP], R=[pt_], W=[KTc])
                    for ri in range(4):
                        r = r4 * 4 + ri
                        for h in range(4):
                            pr, hf, r0 = h // 2, h % 2, (h % 2) * 64
                            c0 = (r * 2 + pr) * 8
                            self.mm(pS[hf].t[0:NP, c0:c0 + 8], KTc.t[r0:r0 + 64, pr, ri, 0:NP], QTb.t[r0:r0 + 64, pr, sl * 8:(sl + 1) * 8], True, True,
                                    R=[KTc, QTb], W=[pS[hf]])
                for hf in range(2):
                    dst = Sall.t[0:NP, rc * 8:(rc + 1) * 8, :].rearrange("p r (pr hf q) -> p hf r pr q", pr=2, hf=2)[:, hf]
                    self.cp("act" if hf == 0 else "dve", dst, pS[hf].t[0:NP, 0:128].rearrange("p (r pr q) -> p r pr q", r=8, pr=2), R=[pS[hf]], W=[Sall])
            if NB > 3:
                S.op("dve", lambda e: e.tensor_reduce(out=gT.t[0:NP, :], in_=Sall.t[0:NP, :, :].rearrange("p r c -> p c r"), axis=AX.X, op=ALU.add), R=[Sall], W=[gT])
                pg = self.ps()
                self.mm(pg.t[0:32, 0:NB], gT.t[0:NP, :], self.cst.t[0:NP, C_PAIRM:C_PAIRM + NB], True, True, R=[gT, self.cst], W=[pg])
                S.op("pool", lambda e: e.memset(GATE.t[0:32, :], NEG), W=[GATE])
                self.cp("dve", GATE.t[0:32, 0:NB], pg.t[0:32, 0:NB], R=[pg], W=[GATE])
                S.op("dve", lambda e: e.max(out=mx.t[0:32, :], in_=GATE.t[0:32, :]), R=[GATE], W=[mx])
                self.ts("dve", SEL.t[0:32, :], GATE.t[0:32, :], mx.t[0:32, 2:3], None, ALU.is_ge, None, R=[GATE, mx], W=[SEL], SS=[mx])
                pst = self.ps()
                self.tr(pst.t[0:64, 0:32], SEL.t[0:32, :], R=[SEL], W=[pst], n=32)
                self.cp("dve", selTs.t[0:64, :], pst.t[0:64, 0:32], R=[pst], W=[selTs])
                pe_ = self.ps()
                self.mm(pe_.t[0:NP, 0:32], self.cst.t[0:NB, C_PAIRMT:C_PAIRMT + NP], selTs.t[0:NB, :], True, True, R=[self.cst, selTs], W=[pe_])
                self.cp("dve", selT2.t[0:NP, :], pe_.t[0:NP, 0:32], R=[pe_], W=[selT2])
            else:
                S.op("pool", lambda e: e.memset(selT2.t[:, :], 1.0), W=[selT2])
            self.act(Sall.t[0:NP, :, :], Sall.t[0:NP, :, :], AF.Exp, R=[Sall], W=[Sall], scale=SCALE)
            for half in range(2):
                for c in range(32):
                    self.ts("dve", Pv[half][0:NP, :, c], Sall.t[0:NP, half * 64:(half + 1) * 64, c], selT2.t[0:NP, c:c + 1], None, ALU.mult, None,
                            R=[Sall, selT2], W=P[half].all, SS=[selT2])
            pacc = self.psb[6]
            self.mm(pacc.t[0:32, 0:264], self.zero_bf.t[:, 0:32], self.zero_bf.t[:, 0:264], True, False, R=[self.zero_bf], W=[pacc])
            for rc in range(16):
                S.idma(Kc.t[0:NP, :, :].rearrange("p a b -> p (a b)"), self.cv_d[l][rc], idx, R=[self.ptab], W=[Kc])
                for h in range(4):
                    self.cp("dve" if h % 2 == 0 else "pool", Vcb.t[0:NP, :, h, 0:64], Kc.t[0:NP, :, h * 64:(h + 1) * 64], R=[Kc], W=[Vcb])
                for ri in range(8):
                    r = rc * 8 + ri
                    self.mm(pacc.t[0:32, 0:264], Pv[r // 64][0:NP, r % 64, :], Vcb.t[0:NP, ri, :, :].rearrange("p h d -> p (h d)"), False, False,
                            R=[P[r // 64].all, Vcb], W=[pacc], skip_group_check=True)
            pso = [self.ps(), self.ps()]
            for h in range(4):
                pr, hf, r0 = h // 2, h % 2, (h % 2) * 64
                self.mm(pso[hf].t[:, pr * 8:(pr + 1) * 8], KTn.t[r0:r0 + 64, pr, :], QTb.t[r0:r0 + 64, pr, sl * 8:(sl + 1) * 8], True, True, R=[KTn, QTb], W=[pso[hf]])
            Pof = self.Pof
            for hf in range(2):
                self.act(Pof.t[:, :].rearrange("p (pr hf q) -> p hf pr q", pr=2, hf=2)[:, hf], pso[hf].t[:, 0:16].rearrange("p (a b) -> p a b", a=2), AF.Exp,
                         R=[pso[hf]], W=[Pof], scale=SCALE)
            for h in range(4):
                self.tt("dve", Po.t[:, h * 8:(h + 1) * 8], Pof.t[:, h * 8:(h + 1) * 8], self.cst.t[:, C_OWNM + sl * 8:C_OWNM + sl * 8 + 8], ALU.mult,
                        R=[Pof, self.cst], W=[Po])
            self.mm(pacc.t[0:32, 0:264], Po.t[:, :], Vxn.t[:, :, :].rearrange("p h d -> p (h d)"), False, True, R=[Po, Vxn], W=[pacc], skip_group_check=True)
            self.cp("act", ACC.t[0:32, :], pacc.t[0:32, 0:264], R=[pacc], W=[ACC])
            self.ts("dve", Osel.t[0:32, :], ACC.t[0:32, 0:64], hm[0:32, 0:1], None, ALU.mult, None, R=[ACC, self.cst], W=[Osel])
            self.ts("dve", den.t[0:32, :], ACC.t[0:32, 64:65], hm[0:32, 0:1], None, ALU.mult, None, R=[ACC, self.cst], W=[den])
            for h in range(1, 4):
                self.stt("dve", Osel.t[0:32, :], ACC.t[0:32, h * 66:h * 66 + 64], hm[0:32, h:h + 1], Osel.t[0:32, :], ALU.mult, ALU.add, R=[ACC, self.cst, Osel], W=[Osel])
                self.stt("dve", den.t[0:32, :], ACC.t[0:32, h * 66 + 64:h * 66 + 65], hm[0:32, h:h + 1], den.t[0:32, :], ALU.mult, ALU.add, R=[ACC, self.cst, den], W=[den])
            S.op("dve", lambda e: e.reciprocal(out=den.t[0:32, :], in_=den.t[0:32, :]), R=[den], W=[den])
            self.ts("dve", Osel.t[0:32, :], Osel.t[0:32, :], den.t[0:32, 0:1], None, ALU.mult, None, R=[Osel, den], W=[Osel], SS=[den])
            for pr in range(2):
                for hf in range(2):
                    self.ts("dve", Oh.t[0:32, pr, hf * 64:(hf + 1) * 64], Osel.t[0:32, :], hm[0:32, pr * 2 + hf:pr * 2 + hf + 1], None, ALU.mult, None,
                            R=[Osel, self.cst], W=[Oh])
            pf = self.ps()
            for pr in range(2):
                self.mm(pf.t[:, pr * 8:(pr + 1) * 8], Oh.t[0:32, pr, :], self.cst.t[0:32, C_QSEL:C_QSEL + 8], True, True, R=[Oh, self.cst], W=[pf])
            for pr in range(2):
                self.cp("dve", self.ycat.t[:, 2 + pr, sl * 8:(sl + 1) * 8], pf.t[:, pr * 8:(pr + 1) * 8], R=[pf], W=self.ycat.c(2 + pr))

    def mixer_s(self, l):
        TT = self.TT
        A = self.ar
        A.reset()
        self.Qt = A.alloc([self.NST, 256], F32, ncells=self.NST)
        self.Kt = A.alloc([self.NST, 256], F32, ncells=self.NST)
        self.Vt = A.alloc([self.NST, 256], F32, ncells=self.NST)
        mark_qkv = A.off
        self.HQ = A.alloc([2, TT], F32)
        self.HF = A.alloc([2, TT], F32)
        self.HG = A.alloc([2, TT], F32)
        self.HFt = A.alloc([self.NST, 256], F32, ncells=self.NST)
        self.HIt = A.alloc([self.NST, 256], F32, ncells=self.NST)
        CA = A.alloc([2, TT], F32)
        Us = A.alloc([2, TT], F32)
        Gn = A.alloc([2, TT], F32)
        self.proj_fm(l, 0, Us)
        self.proj_tm(l, 256, self.Qt)
        self.proj_tm(l, 512, self.Kt)
        self.proj_tm(l, 768, self.Vt)
        self.proj_fm(l, 1024, CA)

        def glu(cc, ps):
            sb = A.alloc([TT], F32)
            self.act(sb.t[:, :], ps.t[:, 0:TT], AF.Sigmoid, R=[ps], W=[sb])
            self.tt("dve", Gn.t[:, cc, :], CA.t[:, cc, :], sb.t[:, :], ALU.mult, R=[CA, sb], W=[Gn])
        self.proj_fm(l, 1280, None, post=glu)
        self.proj_fm(l, 1536, self.HQ)
        self.proj_fm(l, 1792, self.HF)
        self.proj_tm(l, 1792, self.HFt)
        self.proj_tm(l, 2048, self.HIt)
        self.proj_fm(l, 2304, self.HG)
        base = A.off
        self.pool_s(l, Us)
        self.S.barrier()
        A.off = base
        self.conv_s(l, Gn)
        self.S.barrier()
        A.off = base
        self.hgrn_s(l)
        self.S.barrier()
        A.off = mark_qkv
        for pr in range(2):
            self.S.op("pool", lambda e: e.memset(self.ycat.t[:, 2 + pr, :], 0.0), W=self.ycat.c(2 + pr))
        self.moba_s(l)
        self.S.barrier()
        self.outproj(l)

    def sample_tile(self):
        S, A = self.S, self.ar
        A.reset()
        for st in range(2):
            xin = A.alloc([D], F32)
            S.dma("pool", xin.t[:, :], self.xs_d[st * 128:(st + 1) * 128, :], W=[xin])
            for g in range(2):
                ps = self.ps()
                for c4 in range(4):
                    c = g * 4 + c4
                    self.tr(ps.t[:, c4 * 128:(c4 + 1) * 128], xin.t[:, c * 128:(c + 1) * 128], R=[xin], W=[ps])
                self.cp("act" if g == 0 else "dve", self.xT.t[:, g * 4:(g + 1) * 4, st * 128:(st + 1) * 128],
                        ps.t[:, :].rearrange("p (a b) -> p a b", a=4), R=[ps], W=[self.xT.c(g * 4 + i) for i in range(4)])
        for l in range(2):
            self.rmsnorm(l, "g1")
            self.ffn(l, self.W1g, self.W1u, self.W1d)
            self.rmsnorm(l, "gm")
            self.mixer_s(l)
            self.rmsnorm(l, "g2")
            self.ffn(l, self.W2g, self.W2u, self.W2d)
        A.reset()
        yo = A.alloc([D], F32)
        for g in range(2):
            ps = self.ps()
            for c4 in range(4):
                c = g * 4 + c4
                self.tr(ps.t[0:32, c4 * 128:(c4 + 1) * 128], self.xT.t[:, c, 0:32], R=self.xT.c(c), W=[ps])
            self.cp("act" if g == 0 else "dve", yo.t[0:32, g * 512:(g + 1) * 512], ps.t[0:32, :], R=[ps], W=[yo])
        S.dma("pool", self.ys_d, yo.t[0:32, :], R=[yo])

    def build(self):
        SEQ, TT, NT, NST, NQT = self.SEQ, self.TT, self.NT, self.NST, self.NQT
        nc = bass.Bass("TRN2", target_bir_lowering=False)
        self.nc = nc

        def din(name, shape):
            return nc.dram_tensor(name, list(shape), F32, kind="ExternalInput").ap()

        def dout(name, shape):
            return nc.dram_tensor(name, list(shape), F32, kind="ExternalOutput").ap()

        x = din("xp", [SEQ, D])
        cst_d = din("cst", [128, C_TOT])
        prm_d = din("prm", [128, PAR_TOT])
        cos_d = din("cos", [SEQ, 32])
        sin_d = din("sin", [SEQ, 32])
        pwbd_d = din("pwbd", [2, 2, 128, 128])
        pw_d = din("cpw", [2, 256, 256])
        self.W1g, self.W1u, self.W1d = din("w1g", [2, D, DFF]), din("w1u", [2, D, DFF]), din("w1d", [2, DFF, D])
        self.W2g, self.W2u, self.W2d = din("w2g", [2, D, DFF]), din("w2u", [2, D, DFF]), din("w2d", [2, DFF, D])
        self.w_in, self.w_out = din("w_in", [2, D, 2560]), din("w_out", [2, D, D])
        NP, NPOOL = self.NP, self.NPOOL
        self.xs_d = din("xs", [256, D])
        self.sp_d = din("sp", [2, 4, 15, 256])
        self.sc_d = din("sc", [2, 4, 30, 256])
        self.sh_d = din("sh", [2, 4, 4, 64, 64])
        pt_d = nc.dram_tensor("pt", [4, NP], mybir.dt.int32, kind="ExternalInput").ap()
        coss_d, sins_d = din("coss", [128, 32]), din("sins", [128, 32])
        self.ck_d = [[din("ck%d_%d" % (l, rc), [NPOOL, 2048]) for rc in range(16)] for l in range(2)]
        self.cv_d = [[din("cv%d_%d" % (l, rc), [NPOOL, 2048]) for rc in range(16)] for l in range(2)]
        self.ys_d = dout("ys", [32, D])
        self.o_ks, self.o_vs = dout("oks", [2, 32, 256]), dout("ovs", [2, 32, 256])
        self.o_pools, self.o_convs = dout("opools", [2, 4, 15, 256]), dout("oconvs", [2, 4, 30, 256])
        self.o_hgrns = dout("ohgrns", [2, 4, 4, 64, 64])
        y = dout("y", [SEQ, D])
        self.o_k, self.o_v = dout("ok", [2, SEQ, 256]), dout("ov", [2, SEQ, 256])
        self.o_pool, self.o_conv = dout("opool", [2, 15, 256]), dout("oconv", [2, 30, 256])
        self.o_hgrn = dout("ohgrn", [2, 4, 64, 64])

        with ExitStack() as stack:
            S = Sched(nc, stack)
            self.S = S
            self.psb = [S.psum("ps%d" % i, [128, 512]) for i in range(8)]
            self.psi = 0
            self.pti = 0
            self.wi = 0
            self.wj = 0
            self.wmap = {}
            self.first_pass_done = False
            self.wsc = nc.dram_tensor("wsc", [192, 128, 2048], BF16, kind="Internal").ap()
            self.wstg = [S.sbuf("wstg%d" % i, [128, 2048], F32) for i in range(2)]
            self.wbf = [S.sbuf("wbf%d" % i, [128, 2048], BF16) for i in range(2)]
            self.wring = list(self.wbf)
            self.cst = S.sbuf("cst", [128, C_TOT], F32)
            self.prm = S.sbuf("prm", [128, PAR_TOT], F32)
            self.ones_bf = S.sbuf("ones_bf", [128, 128], BF16)
            self.blk_bf = S.sbuf("blk_bf", [128, 128], BF16)
            self.caus_bf = S.sbuf("caus_bf", [128, 128], BF16)
            self.zero_bf = S.sbuf("zero_bf", [128, 264], BF16)
            self.cos = S.sbuf("cos", [128, NQT, 32], F32)
            self.sin = S.sbuf("sin", [128, NQT, 32], F32)
            self.pwbd = S.sbuf("pwbd", [128, 2, 2, 128], BF16)
            self.pw = S.sbuf("pw", [128, 2, 2, 256], BF16)
            self.lbtm = S.sbuf("lbtm", [128, 2, 256], F32)
            self.omltm = S.sbuf("omltm", [128, 2, 256], F32)
            self.omlfm = S.sbuf("omlfm", [128, 2, 2], F32)
            self.nomlfm = S.sbuf("nomlfm", [128, 2, 2], F32)
            self.xT = S.sbuf("xT", [128, 8, TT], F32, ncells=8)
            self.hT = S.sbuf("hT", [128, 8, TT], BF16, ncells=8)
            self.sqb = S.sbuf("sqb", [128, 8, TT], BF16, ncells=8)
            self.rstd = S.sbuf("rstd", [128, TT], F32)
            self.ycat = S.sbuf("ycat", [128, 8, TT], BF16, ncells=8)
            self.UP = [S.sbuf("UP%d" % l, [128, 2, 15 + TT], F32) for l in range(2)]
            self.G = [S.sbuf("G%d" % l, [128, 2, 30 + TT], F32) for l in range(2)]
            self.Sst = [S.sbuf("Sst%d" % l, [128, 2, 64], F32) for l in range(2)]
            self.KT = [S.sbuf("KT%d" % l, [128, 2, SEQ], BF16, ncells=NQT) for l in range(2)]
            self.VX = [S.sbuf("VX%d" % l, [128, NQT, 4, 66], BF16, ncells=NQT) for l in range(2)]
            self.KP = [S.sbuf("KP%d" % l, [128, 2, NQT], F32) for l in range(2)]
            self.KM = [S.sbuf("KM%d" % l, [128, 2, 16], F32) for l in range(2)]
            self.PT = [S.sbuf("PT%d" % i, [128, 512], BF16) for i in range(2)]
            self.ar = Arena(S, "arena", self.arena_bytes())
            A = self.ar

            S.dma("sp", self.cst.t[:, :], cst_d, W=[self.cst])
            S.dma("sp", self.prm.t[:, :], prm_d, W=[self.prm])
            S.dma("sp", self.cos.t[:, :, :], cos_d.rearrange("(q p) c -> p q c", p=128), W=[self.cos])
            S.dma("sp", self.sin.t[:, :, :], sin_d.rearrange("(q p) c -> p q c", p=128), W=[self.sin])
            self.cp("dve", self.ones_bf.t[:, :], self.cst.t[:, C_ONES:C_ONES + 128], R=[self.cst], W=[self.ones_bf])
            self.cp("dve", self.blk_bf.t[:, :], self.cst.t[:, C_BLK:C_BLK + 128], R=[self.cst], W=[self.blk_bf])
            self.cp("dve", self.caus_bf.t[:, :], self.cst.t[:, C_CAUS:C_CAUS + 128], R=[self.cst], W=[self.caus_bf])
            S.op("pool", lambda e: e.memset(self.zero_bf.t[:, :], 0.0), W=[self.zero_bf])
            t0 = A.alloc([2, 2, 128], F32)
            S.dma("pool", t0.t[:, :, :, :], pwbd_d.rearrange("l c p n -> p l c n"), W=[t0])
            self.cp("dve", self.pwbd.t[:, :, :, :], t0.t[:, :, :, :], R=[t0], W=[self.pwbd])
            t1 = A.alloc([2, 2, 256], F32)
            S.dma("pool", t1.t[:, :, :, :], pw_d.rearrange("l (c p) n -> p l c n", p=128), W=[t1])
            self.cp("dve", self.pw.t[:, :, :, :], t1.t[:, :, :, :], R=[t1], W=[self.pw])
            for (dst_lb, dst_oml, nm, n) in ((self.lbtm, self.omltm, "lbtm", 256), (None, self.omlfm, "lbfm", 2)):
                d = A.alloc([n], F32)
                lb1 = A.alloc([n], F32)
                self.tt("dve", d.t[:, :], self.par(1, nm, 0, n), self.par(0, nm, 0, n), ALU.subtract, R=[self.prm], W=[d])
                self.act(lb1.t[:, :], d.t[:, :], AF.Sigmoid, R=[d], W=[lb1])
                if dst_lb is not None:
                    S.op("pool", lambda e: e.memset(dst_lb.t[:, 0, :], 0.0), W=[dst_lb])
                    self.cp("dve", dst_lb.t[:, 1, :], lb1.t[:, :], R=[lb1], W=[dst_lb])
                S.op("pool", lambda e: e.memset(dst_oml.t[:, 0, :], 1.0), W=[dst_oml])
                self.ts("dve", dst_oml.t[:, 1, :], lb1.t[:, :], -1.0, 1.0, ALU.mult, ALU.add, R=[lb1], W=[dst_oml])
            self.ts("dve", self.nomlfm.t[:, :, :], self.omlfm.t[:, :, :], -1.0, None, ALU.mult, None, R=[self.omlfm], W=[self.nomlfm])
            for l in range(2):
                S.op("pool", lambda e: e.memset(self.UP[l].t[:, :, :], 0.0), W=[self.UP[l]])
                S.op("pool", lambda e: e.memset(self.G[l].t[:, :, :], 0.0), W=[self.G[l]])
                S.op("pool", lambda e: e.memset(self.Sst[l].t[:, :, :], 0.0), W=[self.Sst[l]])
                S.op("pool", lambda e: e.memset(self.VX[l].t[:, :, :, 64:65], 1.0), W=[self.VX[l]])
                S.op("pool", lambda e: e.memset(self.KM[l].t[:, :, :], 0.0), W=[self.KM[l]])
            S.barrier(sp=True)

            self.coss = S.sbuf("coss", [128, 32], F32)
            self.sins = S.sbuf("sins", [128, 32], F32)
            self.ptab = S.sbuf("ptab", [128, 4], mybir.dt.int32)
            S.dma("sp", self.coss.t[:, :], coss_d, W=[self.coss])
            S.dma("sp", self.sins.t[:, :], sins_d, W=[self.sins])
            S.op("pool", lambda e: e.memset(self.ptab.t[:, :], 0), W=[self.ptab])
            with nc.allow_non_contiguous_dma(reason="tiny page-table transpose"):
                S.dma("sp", self.ptab.t[0:NP, :], pt_d.rearrange("s j -> j s"), W=[self.ptab])
            if DO_SAMPLE:
                self.sample_tile()
                self.end_first_pass()
            S.barrier(sp=True)

            for tile in range(NT):
                last = tile == NT - 1
                A.reset()
                for st in range(NST):
                    xin = A.alloc([D], F32)
                    r0 = tile * TT + st * 128
                    S.dma("pool", xin.t[:, :], x[r0:r0 + 128, :], W=[xin])
                    for g in range(2):
                        ps = self.ps()
                        for c4 in range(4):
                            c = g * 4 + c4
                            self.tr(ps.t[:, c4 * 128:(c4 + 1) * 128], xin.t[:, c * 128:(c + 1) * 128], R=[xin], W=[ps])
                        self.cp("act" if g == 0 else "dve", self.xT.t[:, g * 4:(g + 1) * 4, st * 128:(st + 1) * 128],
                                ps.t[:, :].rearrange("p (a b) -> p a b", a=4), R=[ps], W=[self.xT.c(g * 4 + i) for i in range(4)])
                for l in range(2):
                    if STAGE >= 1:
                        self.rmsnorm(l, "g1")
                    if STAGE >= 2:
                        self.ffn(l, self.W1g, self.W1u, self.W1d)
                    if STAGE >= 3:
                        self.rmsnorm(l, "gm")
                        self.mixer(l, tile, last)
                    if STAGE >= 8:
                        self.rmsnorm(l, "g2")
                        self.ffn(l, self.W2g, self.W2u, self.W2d)
                self.end_first_pass()
                A.reset()
                for st in range(NST):
                    yo = A.alloc([D], F32)
                    for g in range(2):
                        ps = self.ps()
                        for c4 in range(4):
                            c = g * 4 + c4
                            self.tr(ps.t[:, c4 * 128:(c4 + 1) * 128], self.xT.t[:, c, st * 128:(st + 1) * 128], R=self.xT.c(c), W=[ps])
                        self.cp("act" if g == 0 else "dve", yo.t[:, g * 512:(g + 1) * 512], ps.t[:, :], R=[ps], W=[yo])
                    r0 = tile * TT + st * 128
                    S.dma("pool", y[r0:r0 + 128, :], yo.t[:, :], R=[yo])
            S.finish()
            self.stats = (S.ninstr, S.nwaits)
        return nc

    def arena_bytes(self):
        TT = self.TT
        return 42 * 1024 if TT <= 256 else 72 * 1024


_NC_CACHE = {}


def _get_nc(SEQ, TT, NP, NPOOL):
    key = (SEQ, TT, NP, NPOOL)
    if key not in _NC_CACHE:
        b = Builder(SEQ, TT, NP, NPOOL)
        _NC_CACHE[key] = (b.build(), b)
    return _NC_CACHE[key]


def run_all(inp, n_cores, TT=256):
    B, SEQ, _ = inp["x_prompt"].shape
    DB, DS, _ = inp["x_sample"].shape
    NP = inp["page_table"].shape[1]
    NPOOL = inp["cache_k"].shape[1]
    PAST = NP * inp["cache_k"].shape[2]
    assert DS == 8 and DB == 4 * n_cores and B == n_cores
    nc, b = _get_nc(SEQ, TT, NP, NPOOL)
    f32 = lambda a: np.ascontiguousarray(np.asarray(a, np.float32))
    cst = _const_tables()
    prm = _params_table(inp)
    cos, sin = _rope_tables(np.arange(SEQ))
    coss, sins = _rope_tables(PAST + (np.arange(128) % 8))
    pw = f32(inp["pool_w"])
    pwbd = np.zeros((2, 2, 128, 128), np.float32)
    for l in range(2):
        for g in range(4):
            c, hf = g // 2, g % 2
            pwbd[l, c, hf * 64:(hf + 1) * 64, hf * 64:(hf + 1) * 64] = pw[l, g]
    shared = {
        "cst": cst, "prm": prm, "cos": cos, "sin": sin, "coss": coss, "sins": sins, "pwbd": pwbd, "cpw": f32(inp["conv_pw"]),
        "w1g": f32(inp["ffn1_w_gate"]), "w1u": f32(inp["ffn1_w_up"]), "w1d": f32(inp["ffn1_w_down"]),
        "w2g": f32(inp["ffn2_w_gate"]), "w2u": f32(inp["ffn2_w_up"]), "w2d": f32(inp["ffn2_w_down"]),
        "w_in": f32(inp["w_in"]), "w_out": f32(inp["w_out"]),
    }
    for nm, key in (("ck", "cache_k"), ("cv", "cache_v")):
        c5 = np.asarray(inp[key], np.float32)
        for l in range(2):
            c3 = c5[l].reshape(NPOOL, 16, 2048)
            for rc in range(16):
                shared["%s%d_%d" % (nm, l, rc)] = np.ascontiguousarray(c3[:, rc, :])
    xp = f32(inp["x_prompt"])
    xs = f32(inp["x_sample"])
    sp_, sc_, sh_ = f32(inp["state_pool"]), f32(inp["state_conv"]), f32(inp["state_hgrn"])
    pt = np.ascontiguousarray(np.asarray(inp["page_table"], np.int32))
    in_maps = []
    for c in range(n_cores):
        m = dict(shared, xp=xp[c])
        xs_c = np.zeros((256, D), np.float32)
        xs_c[0:32] = xs[4 * c:4 * c + 4].reshape(32, D)
        m["xs"] = xs_c
        m["sp"] = np.ascontiguousarray(sp_[:, 4 * c:4 * c + 4])
        m["sc"] = np.ascontiguousarray(sc_[:, 4 * c:4 * c + 4])
        m["sh"] = np.ascontiguousarray(sh_[:, 4 * c:4 * c + 4])
        m["pt"] = np.ascontiguousarray(pt[4 * c:4 * c + 4])
        in_maps.append(m)
    res = run_bass_kernel_spmd(nc, in_maps, core_ids=list(range(n_cores)))
    return res.results


def run_prompt(inp, SEQ, TT, n_cores):
    return run_all(inp, n_cores, TT)


def kernel(**inp):
    B, SEQ, _ = inp["x_prompt"].shape
    DB, DS, _ = inp["x_sample"].shape
    r = run_all(inp, B)
    cat = lambda k, ax: np.concatenate([r[c][k] for c in range(B)], axis=ax)
    y_prompt = np.stack([r[c]["y"] for c in range(B)])
    k_prompt = np.stack([r[c]["ok"] for c in range(B)], axis=1).reshape(2, B, SEQ, 4, 64)
    v_prompt = np.stack([r[c]["ov"] for c in range(B)], axis=1).reshape(2, B, SEQ, 4, 64)
    pool_prompt = np.stack([r[c]["opool"] for c in range(B)], axis=1)
    conv_prompt = np.stack([r[c]["oconv"] for c in range(B)], axis=1)
    hgrn_prompt = np.stack([r[c]["ohgrn"] for c in range(B)], axis=1)
    y_sample = cat("ys", 0).reshape(DB, DS, D)
    k_sample = cat("oks", 1).reshape(2, DB, DS, 4, 64)
    v_sample = cat("ovs", 1).reshape(2, DB, DS, 4, 64)
    pool_sample = cat("opools", 1)
    conv_sample = cat("oconvs", 1)
    hgrn_sample = cat("ohgrns", 1)
    return (y_prompt, y_sample, k_prompt, v_prompt, k_sample, v_sample,
            pool_prompt, pool_sample, conv_prompt, conv_sample, hgrn_prompt, hgrn_sample)
```

```python
import numpy as np
from contextlib import ExitStack
import concourse.bass as bass
import concourse.mybir as mybir
from concourse.bass_utils import run_bass_kernel_spmd

F32 = mybir.dt.float32
BF16 = mybir.dt.bfloat16
AF = mybir.ActivationFunctionType
ALU = mybir.AluOpType
AX = mybir.AxisListType

D = 1024
DFF = 2816
GW = 256
NEG = -1.0e30
STAGE = 9
SUB = 9
DO_SAMPLE = True
EPS = 1e-6
SCALE = 64 ** -0.5


class Cell:
    __slots__ = ("w", "r")

    def __init__(self):
        self.w = None
        self.r = {}


class Buf:
    def __init__(self, t, ncells=1):
        self.t = t
        self.cells = [Cell() for _ in range(ncells)]

    def c(self, i):
        return [self.cells[i]]

    @property
    def all(self):
        return self.cells


def _cells(items):
    out = []
    for it in items:
        if isinstance(it, Buf):
            out.extend(it.cells)
        elif isinstance(it, Cell):
            out.append(it)
        else:
            out.extend(_cells(it))
    return out


class Sched:
    ENG = ("pe", "act", "dve", "pool", "sp")

    def __init__(self, nc, stack, n_dma_sems=(10, 8)):
        self.nc = nc
        self.stack = stack
        self.eng = {"pe": nc.tensor, "act": nc.scalar, "dve": nc.vector, "pool": nc.gpsimd, "sp": nc.sync}
        self.sem = {}
        self.cnt = {}
        self.seen = {e: {} for e in self.ENG}
        for e in self.ENG:
            self.sem[e] = stack.enter_context(nc.semaphore("s_" + e))
            self.cnt[e] = 0
        self.dq = {}
        for q, n in zip(("sp", "pool"), n_dma_sems):
            sems = []
            for i in range(n):
                k = "d_%s_%d" % (q, i)
                self.sem[k] = stack.enter_context(nc.semaphore(k))
                self.cnt[k] = 0
                sems.append(k)
            self.dq[q] = [sems, 0]
        self.nwaits = 0
        self.ninstr = 0

    def sbuf(self, name, shape, dt, ncells=1):
        t = self.stack.enter_context(self.nc.sbuf_tensor("sb_" + name, list(shape), dt))
        return Buf(t, ncells)

    def psum(self, name, shape, dt=F32, ncells=1):
        t = self.stack.enter_context(self.nc.psum_tensor(name, list(shape), dt))
        return Buf(t, ncells)

    def _wait(self, e, key, val):
        if self.seen[e].get(key, 0) >= val:
            return
        self.eng[e].wait_ge(self.sem[key], val)
        self.seen[e][key] = val
        self.nwaits += 1

    def _deps(self, e, R, W):
        deps = {}
        for c in R:
            if c.w is not None:
                k, v = c.w
                if deps.get(k, 0) < v:
                    deps[k] = v
        for c in W:
            if c.w is not None:
                k, v = c.w
                if deps.get(k, 0) < v:
                    deps[k] = v
            for k, v in c.r.items():
                if deps.get(k, 0) < v:
                    deps[k] = v
        for k, v in deps.items():
            if k == e and e == "pe":
                continue
            self._wait(e, k, v)

    def _mark(self, key, val, R, W):
        for c in R:
            if c.r.get(key, 0) < val:
                c.r[key] = val
        for c in W:
            c.w = (key, val)
            c.r = {}

    def op(self, e, fn, R=(), W=(), SS=()):
        R = _cells(R)
        W = _cells(W)
        self._deps(e, R, W)
        for c in _cells(SS):
            if c.w is not None and c.w[0] == e:
                self._wait(e, e, c.w[1])
        ins = fn(self.eng[e])
        self.cnt[e] += 1
        ins.then_inc(self.sem[e], 1)
        self._mark(e, self.cnt[e], R, W)
        self.ninstr += 1
        return ins

    def dma(self, q, out, in_, R=(), W=(), **kw):
        R = _cells(R)
        W = _cells(W)
        sems, idx = self.dq[q]
        key = sems[idx % len(sems)]
        self.dq[q][1] = idx + 1
        if self.cnt[key] > 0:
            self._wait(q, key, self.cnt[key])
        self._deps(q, R, W)
        ins = self.eng[q].dma_start(out=out, in_=in_, **kw)
        self.cnt[key] += 16
        ins.then_inc(self.sem[key], 16)
        self._mark(key, self.cnt[key], R, W)
        self.ninstr += 1
        return ins

    def idma(self, out, in_, idx_ap, R=(), W=()):
        q = "pool"
        R = _cells(R)
        W = _cells(W)
        sems, idx = self.dq[q]
        key = sems[idx % len(sems)]
        self.dq[q][1] = idx + 1
        if self.cnt[key] > 0:
            self._wait(q, key, self.cnt[key])
        self._deps(q, R, W)
        ins = self.nc.gpsimd.indirect_dma_start(out=out, out_offset=None, in_=in_, in_offset=bass.IndirectOffsetOnAxis(ap=idx_ap, axis=0))
        self.cnt[key] += 16
        ins.then_inc(self.sem[key], 16)
        self._mark(key, self.cnt[key], R, W)
        self.ninstr += 1
        return ins

    def cc(self, kind, ins_, outs, R=(), W=(), ncores=8):
        q = "pool"
        R = _cells(R)
        W = _cells(W)
        sems, idx = self.dq[q]
        key = sems[idx % len(sems)]
        self.dq[q][1] = idx + 1
        if self.cnt[key] > 0:
            self._wait(q, key, self.cnt[key])
        self._deps(q, R, W)
        ins = self.nc.gpsimd.collective_compute(kind, op=ALU.bypass, replica_groups=[list(range(ncores))], ins=ins_, outs=outs)
        self.cnt[key] += 16
        ins.then_inc(self.sem[key], 16)
        self._mark(key, self.cnt[key], R, W)
        self.ninstr += 1
        return ins

    def barrier(self, sp=False):
        for e in (("pe", "act", "dve", "pool", "sp") if sp else ("pe", "act", "dve", "pool")):
            for k, v in self.cnt.items():
                if v > 0 and k != e:
                    self._wait(e, k, v)

    def finish(self):
        for k, v in self.cnt.items():
            if v > 0 and k != "sp":
                self._wait("sp", k, v)


class Arena:
    def __init__(self, S, name, nbytes):
        self.S = S
        self.t = S.stack.enter_context(S.nc.sbuf_tensor("sb_" + name, [128, nbytes // 4], F32))
        self.n = nbytes // 4
        self.off = 0

    def reset(self):
        self.S.barrier()
        self.off = 0

    def alloc(self, shape, dt=F32, ncells=1):
        n = 1
        for s in shape:
            n *= s
        words = n if dt == F32 else (n + 1) // 2
        assert self.off + words <= self.n, ("arena overflow", self.off, words, self.n)
        ap = self.t[:, self.off:self.off + words]
        self.off += words
        if dt != F32:
            ap = ap.bitcast(dt)
        if len(shape) == 2:
            ap = ap.rearrange("p (a b) -> p a b", a=shape[0])
        elif len(shape) == 3:
            ap = ap.rearrange("p (a b c) -> p a b c", a=shape[0], b=shape[1])
        return Buf(ap, ncells)


def _const_tables():
    p = np.arange(128)
    ident = np.eye(128, dtype=np.float32)
    ones = np.ones((128, 128), np.float32)
    same = (p[:, None] // 64) == (p[None, :] // 64)
    tri2 = (same & (p[:, None] <= p[None, :])).astype(np.float32)
    trirev2 = (same & (p[:, None] > p[None, :])).astype(np.float32)
    caus = (p[:, None] <= p[None, :]).astype(np.float32)
    blk64 = same.astype(np.float32)
    same8 = (p[:, None] // 8) == (p[None, :] // 8)
    tri8 = (same8 & (p[:, None] <= p[None, :])).astype(np.float32)
    trev8 = (same8 & (p[:, None] > p[None, :])).astype(np.float32)
    n64 = np.arange(64)
    pairm = ((p[:, None] // 2) == n64[None, :]).astype(np.float32) / 256.0
    pairmt = np.zeros((128, 128), np.float32)
    pairmt[:64, :] = ((p[None, :] // 2) == n64[:, None]).astype(np.float32)
    rowm = ((p[:, None] // 8) == np.arange(16)[None, :]).astype(np.float32)
    q8 = np.arange(8)
    ownm = (((p[:, None, None] // 8) == np.arange(16)[None, :, None]) & ((p[:, None, None] % 8) <= q8[None, None, :])).astype(np.float32).reshape(128, 128)
    hm = ((p[:, None] // 8) == np.arange(4)[None, :]).astype(np.float32)
    qsel = ((p[:, None] % 8) == q8[None, :]).astype(np.float32)
    return np.concatenate([ident, ones, tri2, trirev2, caus, blk64, tri8, trev8, pairm, pairmt, rowm, ownm, hm, qsel], axis=1)


C_ID, C_ONES, C_TRI2, C_TREV, C_CAUS, C_BLK, C_TRI8, C_TREV8 = [i * 128 for i in range(8)]
C_PAIRM = 1024
C_PAIRMT = C_PAIRM + 64
C_ROWM = C_PAIRMT + 128
C_OWNM = C_ROWM + 16
C_HM = C_OWNM + 128
C_QSEL = C_HM + 4
C_TOT = C_QSEL + 8


def _fm(v):
    return np.ascontiguousarray(np.asarray(v, np.float32).reshape(-1, 128).T)


PAR_L = {}


def _par_layout():
    off = 0
    for name, w in (("g1", 8), ("gm", 8), ("g2", 8), ("pscale", 2), ("cw", 62), ("cb", 2), ("clg", 2), ("clb", 2),
                    ("hnorm", 2), ("lbfm", 2), ("gq", 256), ("gk", 256), ("lbtm", 256)):
        PAR_L[name] = (off, w)
        off += w
    return off


PAR_W = _par_layout()
PAR_G = 2 * PAR_W
PG_EPS, PG_INVW, PG_ONE, PG_INVC = PAR_G, PAR_G + 1, PAR_G + 3, PAR_G + 4
PAR_TOT = PAR_G + 4 + 32


def _params_table(inp):
    T = np.zeros((128, PAR_TOT), np.float32)
    for l in range(2):
        b = l * PAR_W

        def put(name, arr):
            o, w = PAR_L[name]
            T[:, b + o:b + o + w] = arr

        put("g1", _fm(inp["ln_ffn1"][l]))
        put("gm", _fm(inp["ln_mix"][l]))
        put("g2", _fm(inp["ln_ffn2"][l]))
        put("pscale", _fm(inp["pool_scale"][l]))
        cw = np.asarray(inp["conv_w"][l], np.float32)
        put("cw", np.ascontiguousarray(cw.T.reshape(2, 128, 31).transpose(1, 0, 2)).reshape(128, 62))
        put("cb", _fm(inp["conv_b"][l]))
        put("clg", _fm(inp["conv_ln_g"][l]))
        put("clb", _fm(inp["conv_ln_b"][l]))
        put("hnorm", np.tile(np.asarray(inp["hgrn_norm"][l], np.float32).reshape(1, 64), (2, 1)).reshape(128, 1).repeat(2, axis=1))
        put("lbfm", _fm(inp["hgrn_lower_bounds"][l]))
        put("gq", np.tile(np.asarray(inp["q_norm"][l], np.float32).reshape(1, 64), (128, 4)))
        put("gk", np.tile(np.asarray(inp["k_norm"][l], np.float32).reshape(1, 64), (128, 4)))
        put("lbtm", np.tile(np.asarray(inp["hgrn_lower_bounds"][l], np.float32).reshape(1, 256), (128, 1)))
    T[:, PG_EPS] = EPS
    w_of = np.array([[2, 8], [4, 16]], np.float32)
    pw = w_of[(np.arange(128) // 64)]
    T[:, PG_INVW:PG_INVW + 2] = 1.0 / pw
    T[:, PG_ONE] = 1.0
    t = np.arange(16, dtype=np.float32)
    invc = 1.0 / np.minimum(pw[:, :, None], t[None, None, :] + 1.0)
    T[:, PG_INVC:PG_INVC + 32] = invc.reshape(128, 32)
    return T


def _rope_tables(pos):
    half = 8
    inv = (500000.0 ** (-np.arange(half, dtype=np.float32) * 2.0 / 16)).astype(np.float32)
    ang = pos.astype(np.float32)[:, None] * inv[None, :]
    cos = np.cos(ang).astype(np.float32)
    sin = np.sin(ang).astype(np.float32)
    return np.tile(cos, (1, 4)), np.tile(sin, (1, 4))


class Builder:
    def __init__(self, SEQ, TT, NP=128, NPOOL=5120):
        self.SEQ, self.TT = SEQ, TT
        self.NP, self.NPOOL = NP, NPOOL
        self.NT = SEQ // TT
        self.NST = TT // 128
        self.NQT = SEQ // 128
        self.NBLK = max(1, SEQ // 256)

    def mm(self, out, lhsT, rhs, start, stop, R, W, **kw):
        self.S.op("pe", lambda e: e.matmul(out, lhsT=lhsT, rhs=rhs, start=start, stop=stop, **kw), R=R, W=W)

    def tr(self, out, in_, R, W, n=128):
        idn = self.cst.t[0:n, C_ID:C_ID + n]
        self.S.op("pe", lambda e: e.transpose(out=out, in_=in_, identity=idn), R=list(R) + [self.cst], W=W)

    def act(self, out, in_, func, R, W, bias=None, scale=1.0, accum_out=None):
        kw = {}
        if bias is not None:
            kw["bias"] = bias
        if accum_out is not None:
            kw["accum_out"] = accum_out
        self.S.op("act", lambda e: e.activation(out=out, in_=in_, func=func, scale=scale, **kw), R=R, W=W)

    def tt(self, eng, out, in0, in1, op, R, W):
        self.S.op(eng, lambda e: e.tensor_tensor(out=out, in0=in0, in1=in1, op=op), R=R, W=W)

    def ts(self, eng, out, in0, s1, s2, op0, op1, R, W, SS=()):
        if op1 is None:
            self.S.op(eng, lambda e: e.tensor_scalar(out=out, in0=in0, scalar1=s1, scalar2=None, op0=op0), R=R, W=W, SS=SS)
        else:
            self.S.op(eng, lambda e: e.tensor_scalar(out=out, in0=in0, scalar1=s1, scalar2=s2, op0=op0, op1=op1), R=R, W=W, SS=SS)

    def stt(self, eng, out, in0, scalar, in1, op0, op1, R, W, SS=()):
        self.S.op(eng, lambda e: e.scalar_tensor_tensor(out=out, in0=in0, scalar=scalar, in1=in1, op0=op0, op1=op1), R=R, W=W, SS=SS)

    def cp(self, eng, out, in_, R, W):
        if eng == "act":
            self.S.op("act", lambda e: e.copy(out=out, in_=in_), R=R, W=W)
        else:
            self.S.op(eng, lambda e: e.tensor_copy(out=out, in_=in_), R=R, W=W)

    def ps(self):
        b = self.psb[self.psi % 6]
        self.psi += 1
        return b

    def par(self, l, name, j=0, n=1):
        o, w = PAR_L[name]
        c = l * PAR_W + o + j
        return self.prm.t[:, c:c + n]

    def wload(self, src, a, b, key):
        n = a * b
        if key not in self.wmap:
            idx = len(self.wmap)
            cell = Buf(None)
            self.wmap[key] = (idx, cell)
            i = self.wi
            self.wi += 1
            stg = self.wstg[i % len(self.wstg)]
            wb = self.wbf[i % len(self.wbf)]
            self.S.dma("sp", stg.t[:, 0:n].rearrange("p (a b) -> p a b", a=a), src, W=[stg])
            self.S.op("pool", lambda e: e.tensor_copy(out=wb.t[:, 0:n], in_=stg.t[:, 0:n]), R=[stg], W=[wb])
            self.S.dma("pool", self.wsc[idx, :, 0:n], wb.t[:, 0:n], R=[wb], W=[cell])
        else:
            idx, cell = self.wmap[key]
            wb = self.wring[self.wj % len(self.wring)]
            self.wj += 1
            self.S.dma("sp", wb.t[:, 0:n], self.wsc[idx, :, 0:n], R=[cell], W=[wb])
        return wb, wb.t[:, 0:n].rearrange("p (a b) -> p a b", a=a)

    def end_first_pass(self):
        if self.first_pass_done:
            return
        self.first_pass_done = True
        self.S.barrier(sp=True)
        extra = []
        for st in self.wstg:
            for hhalf in range(2):
                extra.append(Buf(st.t[:, hhalf * 1024:(hhalf + 1) * 1024].bitcast(BF16)))
        self.wring = list(self.wbf) + extra

    def rmsnorm(self, l, gname):
        S, TT = self.S, self.TT
        xT, hT, sq = self.xT, self.hT, self.sqb
        for c in range(8):
            self.tt("pool", sq.t[:, c, :], xT.t[:, c, :], xT.t[:, c, :], ALU.mult, R=xT.c(c), W=sq.c(c))
        ps = self.ps()
        for c in range(8):
            self.mm(ps.t[:, 0:TT], self.ones_bf.t[:, :], sq.t[:, c, :], c == 0, c == 7, R=[sq.c(c), self.ones_bf], W=[ps])
        rs = self.rstd
        self.act(rs.t[:, 0:TT], ps.t[:, 0:TT], AF.Sqrt, R=[ps, self.prm], W=[rs], bias=self.prm.t[:, PG_EPS:PG_EPS + 1], scale=1.0 / D)
        self.S.op("dve", lambda e: e.reciprocal(out=rs.t[:, 0:TT], in_=rs.t[:, 0:TT]), R=[rs], W=[rs])
        for c in range(8):
            self.stt("dve", hT.t[:, c, :], xT.t[:, c, :], self.par(l, gname, c), rs.t[:, 0:TT], ALU.mult, ALU.mult,
                     R=[xT.c(c), rs, self.prm], W=hT.c(c))

    def ffn(self, l, Wg, Wu, Wd):
        S, TT = self.S, self.TT
        xT, hT = self.xT, self.hT
        self.ar.reset()
        aT = self.ar.alloc([22, TT], BF16, ncells=22)
        sg = [self.ar.alloc([TT], F32), self.ar.alloc([TT], F32)]
        for j2 in range(11):
            wgb, wg = self.wload(Wg[l][:, j2 * 256:(j2 + 1) * 256].rearrange("(c p) n -> p c n", p=128), 8, 256, (id(Wg), l, "g", j2))
            wub, wu = self.wload(Wu[l][:, j2 * 256:(j2 + 1) * 256].rearrange("(c p) n -> p c n", p=128), 8, 256, (id(Wu), l, "u", j2))
            for jj in range(2):
                j = 2 * j2 + jj
                pg, pu = self.ps(), self.ps()
                for c in range(8):
                    self.mm(pg.t[:, 0:TT], wg[:, c, jj * 128:(jj + 1) * 128], hT.t[:, c, :], c == 0, c == 7, R=[wgb, hT.c(c)], W=[pg])
                for c in range(8):
                    self.mm(pu.t[:, 0:TT], wu[:, c, jj * 128:(jj + 1) * 128], hT.t[:, c, :], c == 0, c == 7, R=[wub, hT.c(c)], W=[pu])
                s = sg[j % 2]
                self.act(s.t[:, :], pg.t[:, 0:TT], AF.Silu, R=[pg], W=[s])
                self.tt("dve", aT.t[:, j, :], s.t[:, :], pu.t[:, 0:TT], ALU.mult, R=[s, pu], W=aT.c(j))
        for m in range(8):
            py = self.ps()
            for hf in range(2):
                wdb, wd = self.wload(Wd[l][hf * 1408:(hf + 1) * 1408, m * 128:(m + 1) * 128].rearrange("(j p) n -> p j n", p=128), 11, 128, (id(Wd), l, "d", m, hf))
                for jj in range(11):
                    j = hf * 11 + jj
                    self.mm(py.t[:, 0:TT], wd[:, jj, :], aT.t[:, j, :], j == 0, j == 21, R=[wdb, aT.c(j)], W=[py])
            self.stt("dve", xT.t[:, m, :], py.t[:, 0:TT], 0.5, xT.t[:, m, :], ALU.mult, ALU.add, R=[py, xT.c(m)], W=xT.c(m))

    def proj_fm(self, l, col0, dst, dcol0=0, post=None):
        TT, hT = self.TT, self.hT
        wb, w = self.wload(self.w_in[l][:, col0:col0 + 256].rearrange("(c p) n -> p c n", p=128), 8, 256, ("in", l, col0))
        for cc in range(2):
            ps = self.ps()
            for c in range(8):
                self.mm(ps.t[:, 0:TT], w[:, c, cc * 128:(cc + 1) * 128], hT.t[:, c, :], c == 0, c == 7, R=[wb, hT.c(c)], W=[ps])
            if post is not None:
                post(cc, ps)
            else:
                self.cp("act", dst.t[:, cc, dcol0:dcol0 + TT], ps.t[:, 0:TT], R=[ps], W=[dst])

    def proj_tm(self, l, col0, dst):
        hT = self.hT
        wb, w = self.wload(self.w_in[l][:, col0:col0 + 256].rearrange("(c p) n -> p c n", p=128), 8, 256, ("in", l, col0))
        for st in range(self.NST):
            ps = self.ps()
            for c in range(8):
                self.mm(ps.t[:, 0:256], hT.t[:, c, st * 128:(st + 1) * 128], w[:, c, :], c == 0, c == 7, R=[wb, hT.c(c)], W=[ps])
            self.cp("act", dst.t[:, st, :], ps.t[:, 0:256], R=[ps], W=dst.c(st))

    def store_rows_fm(self, src_ap_fn, src_buf, n, dram_ap):
        ps = self.ps()
        for c in range(2):
            self.tr(ps.t[0:n, c * 128:(c + 1) * 128], src_ap_fn(c), R=[src_buf], W=[ps])
        ob = self.ar.alloc([256], F32)
        self.cp("act", ob.t[0:n, :], ps.t[0:n, 0:256], R=[ps], W=[ob])
        self.S.dma("pool", dram_ap, ob.t[0:n, :], R=[ob])

    def pool_mix(self, l, tile, last):
        TT = self.TT
        UP = self.UP[l]
        W_ = TT + 15
        s2 = self.ar.alloc([2, W_], F32)
        s4 = self.ar.alloc([2, W_], F32)
        s8 = self.ar.alloc([2, W_], F32)
        s16 = self.ar.alloc([2, W_], F32)
        pooled = self.ar.alloc([2, TT], BF16)
        self.tt("dve", s2.t[:, :, 0:W_ - 1], UP.t[:, :, 1:W_], UP.t[:, :, 0:W_ - 1], ALU.add, R=[UP], W=[s2])
        self.tt("dve", s4.t[:, :, 0:W_ - 3], s2.t[:, :, 2:W_ - 1], s2.t[:, :, 0:W_ - 3], ALU.add, R=[s2], W=[s4])
        self.tt("dve", s8.t[:, :, 0:W_ - 7], s4.t[:, :, 4:W_ - 3], s4.t[:, :, 0:W_ - 7], ALU.add, R=[s4], W=[s8])
        self.tt("dve", s16.t[:, :, 0:W_ - 15], s8.t[:, :, 8:W_ - 7], s8.t[:, :, 0:W_ - 15], ALU.add, R=[s8], W=[s16])
        srcs = {(0, 0): (s2, 14), (1, 0): (s4, 12), (0, 1): (s8, 8), (1, 1): (s16, 0)}
        for (hf, c), (sb, o) in srcs.items():
            r = slice(hf * 64, hf * 64 + 64)
            self.stt("dve", pooled.t[r, c, :], sb.t[r, c, o:o + TT], self.prm.t[r, PG_INVW + c:PG_INVW + c + 1], UP.t[r, c, 15:15 + TT],
                     ALU.mult, ALU.subtract, R=[sb, UP, self.prm], W=[pooled])
            if tile == 0:
                tmp = self.ar.alloc([16], F32)
                self.tt("dve", tmp.t[r, :], sb.t[r, c, o:o + 16], self.prm.t[r, PG_INVC + c * 16:PG_INVC + c * 16 + 16], ALU.mult, R=[sb, self.prm], W=[tmp])
                self.tt("dve", pooled.t[r, c, 0:16], tmp.t[r, :], UP.t[r, c, 15:31], ALU.subtract, R=[tmp, UP], W=[pooled])
        for c in range(2):
            ps = self.ps()
            self.mm(ps.t[:, 0:TT], self.pwbd.t[:, l, c, :], pooled.t[:, c, :], True, True, R=[self.pwbd, pooled], W=[ps])
            self.ts("dve", self.ycat.t[:, c, :], ps.t[:, 0:TT], self.par(l, "pscale", c), None, ALU.mult, None, R=[ps, self.prm], W=self.ycat.c(c))
        if last:
            self.store_rows_fm(lambda c: UP.t[:, c, TT:TT + 15], UP, 15, self.o_pool[l])
        self.cp("pool", UP.t[:, :, 0:15], UP.t[:, :, TT:TT + 15], R=[UP], W=[UP])

    def conv_mix(self, l, tile, last):
        TT = self.TT
        G = self.G[l]
        acc = self.ar.alloc([2, TT], F32, ncells=2)
        sq = self.ar.alloc([2, TT], F32, ncells=2)
        eng = "dve"
        for c in range(2):
            self.ts(eng, acc.t[:, c, :], G.t[:, c, 0:TT], self.par(l, "cw", c * 31), self.par(l, "cb", c), ALU.mult, ALU.add, R=[G, self.prm], W=acc.c(c))
            self.ts(eng, sq.t[:, c, :], G.t[:, c, 1:1 + TT], self.par(l, "cw", c * 31 + 1), None, ALU.mult, None, R=[G, self.prm], W=sq.c(c))
        for j in range(2, 31):
            for c in range(2):
                tgt = acc if j % 2 == 0 else sq
                self.stt(eng, tgt.t[:, c, :], G.t[:, c, j:j + TT], self.par(l, "cw", c * 31 + j), tgt.t[:, c, :], ALU.mult, ALU.add,
                         R=[G, self.prm, tgt.c(c)], W=tgt.c(c))
        for c in range(2):
            self.tt(eng, acc.t[:, c, :], acc.t[:, c, :], sq.t[:, c, :], ALU.add, R=[acc.c(c), sq.c(c)], W=acc.c(c))
            self.tt(eng, sq.t[:, c, :], acc.t[:, c, :], acc.t[:, c, :], ALU.mult, R=acc.c(c), W=sq.c(c))
        self.conv_tail(l, acc, sq)
        if last:
            self.store_rows_fm(lambda c: G.t[:, c, TT:TT + 30], G, 30, self.o_conv[l])
        self.cp("pool", G.t[:, :, 0:30], G.t[:, :, TT:TT + 30], R=[G], W=[G])

    def conv_tail(self, l, acc, sq):
        TT = self.TT
        psm, psq = self.ps(), self.ps()
        onesf = self.cst.t[:, C_ONES:C_ONES + 128]
        for c in range(2):
            self.mm(psm.t[:, 0:TT], onesf, acc.t[:, c, :], c == 0, c == 1, R=[self.cst, acc.c(c)], W=[psm])
        for c in range(2):
            self.mm(psq.t[:, 0:TT], onesf, sq.t[:, c, :], c == 0, c == 1, R=[self.cst, sq.c(c)], W=[psq])
        mean = self.ar.alloc([TT], F32)
        m2 = self.ar.alloc([TT], F32)
        rstd = self.ar.alloc([TT], F32)
        self.S.op("act", lambda e: e.mul(out=mean.t[:, :], in_=psm.t[:, 0:TT], mul=1.0 / 256), R=[psm], W=[mean])
        self.tt("dve", m2.t[:, :], mean.t[:, :], mean.t[:, :], ALU.mult, R=[mean], W=[m2])
        self.stt("dve", rstd.t[:, :], psq.t[:, 0:TT], 1.0 / 256, m2.t[:, :], ALU.mult, ALU.subtract, R=[psq, m2], W=[rstd])
        self.act(rstd.t[:, :], rstd.t[:, :], AF.Sqrt, R=[rstd, self.prm], W=[rstd], bias=self.prm.t[:, PG_EPS:PG_EPS + 1])
        self.S.op("dve", lambda e: e.reciprocal(out=rstd.t[:, :], in_=rstd.t[:, :]), R=[rstd], W=[rstd])
        zs = self.ar.alloc([2, TT], BF16, ncells=2)
        for c in range(2):
            self.tt("dve", acc.t[:, c, :], acc.t[:, c, :], mean.t[:, :], ALU.subtract, R=[acc.c(c), mean], W=acc.c(c))
            self.tt("dve", acc.t[:, c, :], acc.t[:, c, :], rstd.t[:, :], ALU.mult, R=[acc.c(c), rstd], W=acc.c(c))
            self.act(zs.t[:, c, :], acc.t[:, c, :], AF.Silu, R=[acc.c(c), self.prm], W=zs.c(c), bias=self.par(l, "clb", c), scale=self.par(l, "clg", c))
        for co in range(2):
            ps = self.ps()
            for c in range(2):
                self.mm(ps.t[:, 0:TT], self.pw.t[:, l, c, co * 128:(co + 1) * 128], zs.t[:, c, :], c == 0, c == 1, R=[self.pw, zs.c(c)], W=[ps])
            self.cp("act", self.ycat.t[:, 4 + co, :], ps.t[:, 0:TT], R=[ps], W=self.ycat.c(4 + co))

    def hgrn_mix(self, l, tile, last):
        TT = self.TT
        A = self.ar
        HQ, HF, HG, HFt, HIt = self.HQ, self.HF, self.HG, self.HFt, self.HIt
        Sm = self.Sst[l]
        tri2 = self.cst.t[:, C_TRI2:C_TRI2 + 128]
        trev = self.cst.t[:, C_TREV:C_TREV + 128]
        one = self.prm.t[:, PG_ONE:PG_ONE + 1]
        for st in range(self.NST):
            cs = slice(st * 128, (st + 1) * 128)
            mark = A.off
            sig = A.alloc([256], F32)
            logf = A.alloc([256], F32)
            kin = A.alloc([256], F32)
            self.act(sig.t[:, :], HFt.t[:, st, :], AF.Sigmoid, R=HFt.c(st), W=[sig])
            self.tt("dve", sig.t[:, :], sig.t[:, :], self.omltm.t[:, l, :], ALU.mult, R=[sig, self.omltm], W=[sig])
            self.tt("dve", sig.t[:, :], sig.t[:, :], self.lbtm.t[:, l, :], ALU.add, R=[sig, self.lbtm], W=[sig])
            self.act(logf.t[:, :], sig.t[:, :], AF.Ln, R=[sig], W=[logf])
            self.ts("dve", kin.t[:, :], sig.t[:, :], -1.0, 1.0, ALU.mult, ALU.add, R=[sig], W=[kin])
            prev = self.ps()
            self.mm(prev.t[:, 0:256], trev, logf.t[:, :], True, True, R=[self.cst, logf], W=[prev])
            pbc = self.ps()
            for kc in range(2):
                self.mm(pbc.t[:, kc * 128:(kc + 1) * 128], logf.t[:, kc * 128:(kc + 1) * 128], tri2, True, True, R=[self.cst, logf], W=[pbc])
            er = A.alloc([256], F32)
            kh = A.alloc([256], BF16)
            vb = A.alloc([256], BF16)
            self.act(er.t[:, :], prev.t[:, 0:256], AF.Exp, R=[prev], W=[er])
            self.tt("dve", kh.t[:, :], kin.t[:, :], er.t[:, :], ALU.mult, R=[kin, er], W=[kh])
            self.cp("pool", vb.t[:, :], HIt.t[:, st, :], R=HIt.c(st), W=[vb])
            E = A.alloc([2, 128], F32)
            Ei = A.alloc([2, 128], F32)
            self.act(E.t[:, :, :], pbc.t[:, 0:256].rearrange("p (a b) -> p a b", a=2), AF.Exp, R=[pbc], W=[E])
            self.act(Ei.t[:, :, :], pbc.t[:, 0:256].rearrange("p (a b) -> p a b", a=2), AF.Exp, R=[pbc], W=[Ei], scale=-1.0)
            sT = A.alloc([2, 128], F32)
            self.act(sT.t[:, :, :], HF.t[:, :, cs], AF.Sigmoid, R=[HF], W=[sT])
            for c in range(2):
                self.ts("dve", sT.t[:, c, :], sT.t[:, c, :], self.nomlfm.t[:, l, c:c + 1], self.omlfm.t[:, l, c:c + 1], ALU.mult, ALU.add,
                        R=[sT, self.nomlfm, self.omlfm], W=[sT])
            qt_ = A.alloc([2, 128], BF16)
            kt_ = A.alloc([2, 128], BF16)
            self.tt("dve", qt_.t[:, :, :], HQ.t[:, :, cs], E.t[:, :, :], ALU.mult, R=[HQ, E], W=[qt_])
            self.tt("dve", kt_.t[:, :, :], sT.t[:, :, :], Ei.t[:, :, :], ALU.mult, R=[sT, Ei], W=[kt_])
            patt = [self.ps(), self.ps()]
            for h in range(4):
                pr, hf, r0 = h // 2, h % 2, (h % 2) * 64
                self.mm(patt[hf].t[:, pr * 128:(pr + 1) * 128], kt_.t[r0:r0 + 64, pr, :], qt_.t[r0:r0 + 64, pr, :], True, True, R=[kt_, qt_], W=[patt[hf]])
            att = A.alloc([4, 128], BF16)
            for h in range(4):
                pr, hf = h // 2, h % 2
                self.tt("dve", att.t[:, h, :], patt[hf].t[:, pr * 128:(pr + 1) * 128], tri2, ALU.mult, R=[patt[hf], self.cst], W=[att])
            pU = [self.ps(), self.ps()]
            for ci in range(2):
                for pr in range(2):
                    k0 = pr * 128
                    self.mm(pU[ci].t[:, k0:k0 + 128], kh.t[ci * 64:(ci + 1) * 64, pr * 128:(pr + 1) * 128], vb.t[ci * 64:(ci + 1) * 64, pr * 128:(pr + 1) * 128],
                            True, True, R=[kh, vb], W=[pU[ci]])
            sbf = [A.alloc([2, 64], BF16), A.alloc([2, 64], BF16)]
            self.cp("act", sbf[0].t[:, :, :], Sm.t[:, :, :], R=[Sm], W=[sbf[0]])
            for ci in range(2):
                for pr in range(2):
                    for hf in range(2):
                        r = slice(hf * 64, hf * 64 + 64)
                        k0 = pr * 128 + hf * 64
                        self.stt("dve", Sm.t[r, pr, :], Sm.t[r, pr, :], E.t[r, pr, ci * 64 + 63:ci * 64 + 64], pU[ci].t[r, k0:k0 + 64], ALU.mult, ALU.add,
                                 R=[Sm, E, pU[ci]], W=[Sm], SS=[E])
                if ci == 0:
                    self.cp("act", sbf[1].t[:, :, :], Sm.t[:, :, :], R=[Sm], W=[sbf[1]])
            po = [self.ps(), self.ps()]
            for h in range(4):
                pr, hf, r0 = h // 2, h % 2, (h % 2) * 64
                self.mm(po[hf].t[r0:r0 + 64, pr * 128:(pr + 1) * 128], vb.t[:, h * 64:(h + 1) * 64], att.t[:, h, :], True, False, R=[vb, att], W=[po[hf]])
                for ci in range(2):
                    self.mm(po[hf].t[r0:r0 + 64, pr * 128 + ci * 64:pr * 128 + ci * 64 + 64], sbf[ci].t[r0:r0 + 64, pr, :], qt_.t[r0:r0 + 64, pr, ci * 64:(ci + 1) * 64],
                            False, ci == 1, R=[sbf[ci], qt_], W=[po[hf]], skip_group_check=True)
            O = A.alloc([2, 128], F32)
            sqo = A.alloc([2, 128], BF16)
            for hf in range(2):
                r = slice(hf * 64, hf * 64 + 64)
                self.cp("act", O.t[r, :, :], po[hf].t[r, 0:256].rearrange("p (a b) -> p a b", a=2), R=[po[hf]], W=[O])
            self.tt("pool", sqo.t[:, :, :], O.t[:, :, :], O.t[:, :, :], ALU.mult, R=[O], W=[sqo])
            pss = self.ps()
            for pr in range(2):
                self.mm(pss.t[:, pr * 128:(pr + 1) * 128], self.blk_bf.t[:, :], sqo.t[:, pr, :], True, True, R=[self.blk_bf, sqo], W=[pss])
            rs = A.alloc([2, 128], F32)
            self.act(rs.t[:, :, :], pss.t[:, 0:256].rearrange("p (a b) -> p a b", a=2), AF.Sqrt, R=[pss, self.prm], W=[rs],
                     bias=self.prm.t[:, PG_EPS:PG_EPS + 1], scale=1.0 / 64)
            self.S.op("dve", lambda e: e.reciprocal(out=rs.t[:, :, :], in_=rs.t[:, :, :]), R=[rs], W=[rs])
            sgt = A.alloc([2, 128], F32)
            self.act(sgt.t[:, :, :], HG.t[:, :, cs], AF.Silu, R=[HG], W=[sgt])
            self.tt("dve", O.t[:, :, :], O.t[:, :, :], rs.t[:, :, :], ALU.mult, R=[O, rs], W=[O])
            for pr in range(2):
                self.stt("dve", self.ycat.t[:, 6 + pr, cs], O.t[:, pr, :], self.par(l, "hnorm", pr), sgt.t[:, pr, :], ALU.mult, ALU.mult,
                         R=[O, sgt, self.prm], W=self.ycat.c(6 + pr))
            self.S.barrier()
            A.off = mark
        if last:
            self.S.dma("pool", self.o_hgrn[l].rearrange("(pr hf) k v -> (hf k) pr v", hf=2), Sm.t[:, :, :], R=[Sm])

    def qk_norm_rope(self, l, st, cos_ap, sin_ap, csbufs):
        A = self.ar
        QN = A.alloc([256], F32)
        KN = A.alloc([256], F32)
        for src, dst, gname in ((self.Qt, QN, "gq"), (self.Kt, KN, "gk")):
            sq = A.alloc([256], F32)
            ss = A.alloc([4], F32)
            self.tt("pool", sq.t[:, :], src.t[:, st, :], src.t[:, st, :], ALU.mult, R=src.c(st), W=[sq])
            for h in range(4):
                self.S.op("dve", lambda e: e.tensor_reduce(out=ss.t[:, h:h + 1], in_=sq.t[:, h * 64:(h + 1) * 64], axis=AX.X, op=ALU.add), R=[sq], W=[ss])
            self.act(ss.t[:, :], ss.t[:, :], AF.Sqrt, R=[ss, self.prm], W=[ss], bias=self.prm.t[:, PG_EPS:PG_EPS + 1], scale=1.0 / 64)
            self.S.op("dve", lambda e: e.reciprocal(out=ss.t[:, :], in_=ss.t[:, :]), R=[ss], W=[ss])
            for h in range(4):
                hs = slice(h * 64, (h + 1) * 64)
                self.stt("dve", dst.t[:, hs], src.t[:, st, hs], ss.t[:, h:h + 1], self.par(l, gname, h * 64, 64), ALU.mult, ALU.mult,
                         R=[src.c(st), ss, self.prm], W=[dst], SS=[ss])
            v3 = dst.t[:, :].rearrange("p (h d) -> p h d", h=4)
            x1, x2 = v3[:, :, 0:8], v3[:, :, 8:16]
            cos = cos_ap.rearrange("p (h d) -> p h d", h=4)
            sin = sin_ap.rearrange("p (h d) -> p h d", h=4)
            tmp = [A.alloc([4, 8], F32) for _ in range(4)]
            self.tt("dve", tmp[0].t[:, :, :], x1, cos, ALU.mult, R=[dst] + csbufs, W=[tmp[0]])
            self.tt("dve", tmp[1].t[:, :, :], x2, sin, ALU.mult, R=[dst] + csbufs, W=[tmp[1]])
            self.tt("dve", tmp[2].t[:, :, :], x2, cos, ALU.mult, R=[dst] + csbufs, W=[tmp[2]])
            self.tt("dve", tmp[3].t[:, :, :], x1, sin, ALU.mult, R=[dst] + csbufs, W=[tmp[3]])
            self.tt("dve", x1, tmp[0].t[:, :, :], tmp[1].t[:, :, :], ALU.subtract, R=[tmp[0], tmp[1]], W=[dst])
            self.tt("dve", x2, tmp[2].t[:, :, :], tmp[3].t[:, :, :], ALU.add, R=[tmp[2], tmp[3]], W=[dst])
        return QN, KN

    def moba_mix(self, l, tile, last):
        TT = self.TT
        A = self.ar
        KT, VX, KP, KM = self.KT[l], self.VX[l], self.KP[l], self.KM[l]
        caus = self.caus_bf.t[:, :]
        for st in range(self.NST):
            qt = tile * self.NST + st
            cs = slice(st * 128, (st + 1) * 128)
            mark = A.off
            QN, KN = self.qk_norm_rope(l, st, self.cos.t[:, qt, :], self.sin.t[:, qt, :], [self.cos, self.sin])
            self.S.dma("pool", self.o_k[l, qt * 128:(qt + 1) * 128, :], KN.t[:, :], R=[KN])
            self.S.dma("pool", self.o_v[l, qt * 128:(qt + 1) * 128, :], self.Vt.t[:, st, :], R=self.Vt.c(st))
            if SUB < 2:
                self.S.barrier()
                A.off = mark
                continue
            QTb = A.alloc([2, 128], BF16)
            QTf = A.alloc([2, 128], F32)
            pq = self.ps()
            pk = self.ps()
            for pr in range(2):
                self.tr(pq.t[:, pr * 128:(pr + 1) * 128], QN.t[:, pr * 128:(pr + 1) * 128], R=[QN], W=[pq])
                self.tr(pk.t[:, pr * 128:(pr + 1) * 128], KN.t[:, pr * 128:(pr + 1) * 128], R=[KN], W=[pk])
            for pr in range(2):
                self.cp("dve", QTb.t[:, pr, :], pq.t[:, pr * 128:(pr + 1) * 128], R=[pq], W=[QTb])
                self.cp("dve", QTf.t[:, pr, :], pq.t[:, pr * 128:(pr + 1) * 128], R=[pq], W=[QTf])
                self.cp("dve", KT.t[:, pr, qt * 128:(qt + 1) * 128], pk.t[:, pr * 128:(pr + 1) * 128], R=[pk], W=[KT.c(qt)])
                self.S.op("dve", lambda e: e.tensor_reduce(out=KP.t[:, pr, qt:qt + 1], in_=pk.t[:, pr * 128:(pr + 1) * 128], axis=AX.X, op=ALU.add),
                          R=[pk], W=[KP])
            for h in range(4):
                self.cp("dve", VX.t[:, qt, h, 0:64], self.Vt.t[:, st, h * 64:(h + 1) * 64], R=self.Vt.c(st), W=VX.c(qt))
            if qt % 2 == 1:
                n = qt // 2
                self.tt("dve", KM.t[:, :, n:n + 1], KP.t[:, :, qt - 1:qt], KP.t[:, :, qt:qt + 1], ALU.add, R=[KP], W=[KM])
                self.ts("dve", KM.t[:, :, n:n + 1], KM.t[:, :, n:n + 1], 1.0 / 256, None, ALU.mult, None, R=[KM], W=[KM])
            if SUB < 3:
                self.S.barrier()
                A.off = mark
                continue
            ob = qt // 2
            dense = ob <= 3
            SEL = None
            if not dense:
                pg = [self.ps(), self.ps()]
                for h in range(4):
                    pr, hf, r0 = h // 2, h % 2, (h % 2) * 64
                    self.mm(pg[hf].t[:, pr * 16:pr * 16 + ob], QTf.t[r0:r0 + 64, pr, :], KM.t[r0:r0 + 64, pr, 0:ob], True, True, R=[QTf, KM], W=[pg[hf]])
                GATE = A.alloc([4, 16], F32)
                SEL = A.alloc([4, 16], F32)
                mx = A.alloc([4, 8], F32)
                self.S.op("pool", lambda e: e.memset(GATE.t[:, :, :], NEG), W=[GATE])
                for h in range(4):
                    pr, hf = h // 2, h % 2
                    self.cp("dve", GATE.t[:, h, 0:ob], pg[hf].t[:, pr * 16:pr * 16 + ob], R=[pg[hf]], W=[GATE])
                for h in range(4):
                    self.S.op("dve", lambda e: e.max(out=mx.t[:, h, :], in_=GATE.t[:, h, :]), R=[GATE], W=[mx])
                    self.ts("dve", SEL.t[:, h, :], GATE.t[:, h, :], mx.t[:, h, 2:3], None, ALU.is_ge, None, R=[GATE, mx], W=[SEL], SS=[mx])
            ACC = A.alloc([4, 65], F32)

            def group(kts, pacc):
                self.mm(pacc.t[:, 0:260], self.zero_bf.t[:, 0:128], self.zero_bf.t[:, 0:260], True, False, R=[self.zero_bf], W=[pacc])
                for i, kt in enumerate(kts):
                    ps_s = [self.ps(), self.ps()]
                    for h in range(4):
                        pr, hf, r0 = h // 2, h % 2, (h % 2) * 64
                        self.mm(ps_s[hf].t[:, pr * 128:(pr + 1) * 128], KT.t[r0:r0 + 64, pr, kt * 128:(kt + 1) * 128], QTb.t[r0:r0 + 64, pr, :], True, True,
                                R=[KT.c(kt), QTb], W=[ps_s[hf]])
                    PT = self.PT[self.pti % 2]
                    self.pti += 1
                    for hf in range(2):
                        self.act(PT.t[:, :].rearrange("p (pr hf q) -> p hf pr q", pr=2, hf=2)[:, hf], ps_s[hf].t[:, 0:256].rearrange("p (a b) -> p a b", a=2),
                                 AF.Exp, R=[ps_s[hf]], W=[PT], scale=SCALE)
                    if kt == qt:
                        for h in range(4):
                            self.tt("dve", PT.t[:, h * 128:(h + 1) * 128], PT.t[:, h * 128:(h + 1) * 128], caus, ALU.mult, R=[PT, self.caus_bf], W=[PT])
                    for h in range(4):
                        self.mm(pacc.t[:, h * 65:(h + 1) * 65], PT.t[:, h * 128:(h + 1) * 128], VX.t[:, kt, h, 0:65], False, (i == len(kts) - 1 and h == 3),
                                R=[PT, VX.c(kt)], W=[pacc], skip_group_check=True)

            pown = self.psb[6]
            own_kts = list(range(0 if dense else 2 * ob, qt + 1))
            group(own_kts, pown)
            self.cp("act", ACC.t[:, :, :], pown.t[:, 0:260].rearrange("p (h d) -> p h d", h=4), R=[pown], W=[ACC])
            if not dense:
                for n in range(ob):
                    pb = self.psb[7]
                    group([2 * n, 2 * n + 1], pb)
                    for h in range(4):
                        self.stt("dve", ACC.t[:, h, :], pb.t[:, h * 65:(h + 1) * 65], SEL.t[:, h, n:n + 1], ACC.t[:, h, :], ALU.mult, ALU.add,
                                 R=[pb, SEL, ACC], W=[ACC], SS=[SEL])
            rden = A.alloc([4], F32)
            Ot = A.alloc([256], F32)
            self.S.op("dve", lambda e: e.reciprocal(out=rden.t[:, :], in_=ACC.t[:, :, 64]), R=[ACC], W=[rden])
            for h in range(4):
                self.ts("dve", Ot.t[:, h * 64:(h + 1) * 64], ACC.t[:, h, 0:64], rden.t[:, h:h + 1], None, ALU.mult, None, R=[ACC, rden], W=[Ot], SS=[rden])
            po = self.ps()
            for pr in range(2):
                self.tr(po.t[:, pr * 128:(pr + 1) * 128], Ot.t[:, pr * 128:(pr + 1) * 128], R=[Ot], W=[po])
            for pr in range(2):
                self.cp("act", self.ycat.t[:, 2 + pr, cs], po.t[:, pr * 128:(pr + 1) * 128], R=[po], W=self.ycat.c(2 + pr))

    def mixer(self, l, tile, last):
        TT = self.TT
        A = self.ar
        A.reset()
        self.HQ = A.alloc([2, TT], F32)
        self.HF = A.alloc([2, TT], F32)
        self.HG = A.alloc([2, TT], F32)
        self.Qt = A.alloc([self.NST, 256], F32, ncells=self.NST)
        self.Kt = A.alloc([self.NST, 256], F32, ncells=self.NST)
        self.Vt = A.alloc([self.NST, 256], F32, ncells=self.NST)
        self.HFt = A.alloc([self.NST, 256], F32, ncells=self.NST)
        self.HIt = A.alloc([self.NST, 256], F32, ncells=self.NST)
        CA = A.alloc([2, TT], F32)
        UP, G = self.UP[l], self.G[l]
        self.proj_fm(l, 0, UP, 15)
        self.proj_tm(l, 256, self.Qt)
        self.proj_tm(l, 512, self.Kt)
        self.proj_tm(l, 768, self.Vt)
        self.proj_fm(l, 1024, CA)

        def glu(cc, ps):
            sb = A.alloc([TT], F32)
            self.act(sb.t[:, :], ps.t[:, 0:TT], AF.Sigmoid, R=[ps], W=[sb])
            self.tt("dve", G.t[:, cc, 30:30 + TT], CA.t[:, cc, :], sb.t[:, :], ALU.mult, R=[CA, sb], W=[G])
        self.proj_fm(l, 1280, None, post=glu)
        self.proj_fm(l, 1536, self.HQ)
        self.proj_fm(l, 1792, self.HF)
        self.proj_tm(l, 1792, self.HFt)
        self.proj_tm(l, 2048, self.HIt)
        self.proj_fm(l, 2304, self.HG)
        base = A.off
        for i, fn in enumerate((self.pool_mix, self.conv_mix, self.hgrn_mix, self.moba_mix)):
            if STAGE < 4 + i:
                continue
            fn(l, tile, last)
            self.S.barrier()
            A.off = base
        self.outproj(l)

    def outproj(self, l):
        TT = self.TT
        for mu in range(4):
            wb, w = self.wload(self.w_out[l][:, mu * 256:(mu + 1) * 256].rearrange("(c p) n -> p c n", p=128), 8, 256, ("out", l, mu))
            for mm_ in range(2):
                m = mu * 2 + mm_
                ps = self.ps()
                for c in range(8):
                    self.mm(ps.t[:, 0:TT], w[:, c, mm_ * 128:(mm_ + 1) * 128], self.ycat.t[:, c, :], c == 0, c == 7, R=[wb, self.ycat.c(c)], W=[ps])
                self.tt("dve", self.xT.t[:, m, :], self.xT.t[:, m, :], ps.t[:, 0:TT], ALU.add, R=[ps, self.xT.c(m)], W=self.xT.c(m))


    def pool_s(self, l, Us):
        A = self.ar
        UPs = A.alloc([2, 4 * 23], F32)
        v = [UPs.t[:, c, :].rearrange("p (b i) -> p b i", b=4) for c in range(2)]
        self.S.op("pool", lambda e: e.memset(UPs.t[:, :, :], 0.0), W=[UPs])
        stg = A.alloc([256], F32)
        self.S.dma("pool", stg.t[0:60, :], self.sp_d[l].rearrange("s i c -> (s i) c"), W=[stg])
        ps = self.ps()
        for c in range(2):
            self.tr(ps.t[:, c * 64:c * 64 + 60], stg.t[0:60, c * 128:(c + 1) * 128], R=[stg], W=[ps], n=60)
        for c in range(2):
            self.cp("dve", v[c][:, 0:4, 0:15], ps.t[:, c * 64:c * 64 + 60].rearrange("p (s i) -> p s i", s=4), R=[ps], W=[UPs])
            self.cp("dve", v[c][:, :, 15:23], Us.t[:, c, 0:32].rearrange("p (b t) -> p b t", b=4), R=[Us], W=[UPs])
        sb = [A.alloc([2, 4 * 23], F32) for _ in range(4)]
        sv = [[b_.t[:, c, :].rearrange("p (b i) -> p b i", b=4) for c in range(2)] for b_ in sb]
        for c in range(2):
            self.tt("dve", sv[0][c][:, :, 0:22], v[c][:, :, 1:23], v[c][:, :, 0:22], ALU.add, R=[UPs], W=[sb[0]])
            self.tt("dve", sv[1][c][:, :, 0:20], sv[0][c][:, :, 2:22], sv[0][c][:, :, 0:20], ALU.add, R=[sb[0]], W=[sb[1]])
            self.tt("dve", sv[2][c][:, :, 0:16], sv[1][c][:, :, 4:20], sv[1][c][:, :, 0:16], ALU.add, R=[sb[1]], W=[sb[2]])
            self.tt("dve", sv[3][c][:, :, 0:8], sv[2][c][:, :, 8:16], sv[2][c][:, :, 0:8], ALU.add, R=[sb[2]], W=[sb[3]])
        pooled = A.alloc([2, 256], BF16)
        self.S.op("pool", lambda e: e.memset(pooled.t[:, :, :], 0.0), W=[pooled])
        srcs = {(0, 0): (0, 14), (1, 0): (1, 12), (0, 1): (2, 8), (1, 1): (3, 0)}
        for (hf, c), (si, o) in srcs.items():
            r = slice(hf * 64, hf * 64 + 64)
            self.stt("dve", pooled.t[r, c, 0:32].rearrange("p (b t) -> p b t", b=4), sv[si][c][r, :, o:o + 8], self.prm.t[r, PG_INVW + c:PG_INVW + c + 1],
                     v[c][r, :, 15:23], ALU.mult, ALU.subtract, R=[sb[si], UPs, self.prm], W=[pooled])
        for c in range(2):
            ps2 = self.ps()
            self.mm(ps2.t[:, 0:256], self.pwbd.t[:, l, c, :], pooled.t[:, c, :], True, True, R=[self.pwbd, pooled], W=[ps2])
            self.ts("dve", self.ycat.t[:, c, :], ps2.t[:, 0:256], self.par(l, "pscale", c), None, ALU.mult, None, R=[ps2, self.prm], W=self.ycat.c(c))
        tmp = A.alloc([2, 60], F32)
        pso = self.ps()
        for c in range(2):
            self.cp("dve", tmp.t[:, c, :].rearrange("p (s i) -> p s i", s=4), v[c][:, 0:4, 8:23], R=[UPs], W=[tmp])
            self.tr(pso.t[0:60, c * 128:(c + 1) * 128], tmp.t[:, c, :], R=[tmp], W=[pso])
        ob = A.alloc([256], F32)
        self.cp("act", ob.t[0:60, :], pso.t[0:60, 0:256], R=[pso], W=[ob])
        self.S.dma("pool", self.o_pools[l].rearrange("s i c -> (s i) c"), ob.t[0:60, :], R=[ob])

    def conv_s(self, l, Gn):
        A = self.ar
        Gs = A.alloc([2, 4 * 38], F32)
        v = [Gs.t[:, c, :].rearrange("p (b i) -> p b i", b=4) for c in range(2)]
        self.S.op("pool", lambda e: e.memset(Gs.t[:, :, :], 0.0), W=[Gs])
        stg = A.alloc([256], F32)
        self.S.dma("pool", stg.t[0:120, :], self.sc_d[l].rearrange("s i c -> (s i) c"), W=[stg])
        ps = self.ps()
        for c in range(2):
            self.tr(ps.t[:, c * 128:c * 128 + 120], stg.t[0:120, c * 128:(c + 1) * 128], R=[stg], W=[ps], n=120)
        for c in range(2):
            self.cp("dve", v[c][:, 0:4, 0:30], ps.t[:, c * 128:c * 128 + 120].rearrange("p (s i) -> p s i", s=4), R=[ps], W=[Gs])
            self.cp("dve", v[c][:, :, 30:38], Gn.t[:, c, 0:32].rearrange("p (b t) -> p b t", b=4), R=[Gn], W=[Gs])
        acc = A.alloc([2, 256], F32, ncells=2)
        sq = A.alloc([2, 256], F32, ncells=2)
        self.S.op("pool", lambda e: e.memset(acc.t[:, :, :], 0.0), W=[acc])
        for c in range(2):
            av = acc.t[:, c, 0:32].rearrange("p (b t) -> p b t", b=4)
            self.ts("dve", av, v[c][:, :, 0:8], self.par(l, "cw", c * 31), self.par(l, "cb", c), ALU.mult, ALU.add, R=[Gs, self.prm], W=acc.c(c))
            for j in range(1, 31):
                self.stt("dve", av, v[c][:, :, j:j + 8], self.par(l, "cw", c * 31 + j), av, ALU.mult, ALU.add, R=[Gs, self.prm, acc.c(c)], W=acc.c(c))
            self.tt("dve", sq.t[:, c, :], acc.t[:, c, :], acc.t[:, c, :], ALU.mult, R=acc.c(c), W=sq.c(c))
        self.conv_tail(l, acc, sq)
        tmp = A.alloc([2, 120], F32)
        pso = self.ps()
        for c in range(2):
            self.cp("dve", tmp.t[:, c, :].rearrange("p (s i) -> p s i", s=4), v[c][:, 0:4, 8:38], R=[Gs], W=[tmp])
            self.tr(pso.t[0:120, c * 128:(c + 1) * 128], tmp.t[:, c, :], R=[tmp], W=[pso])
        ob = A.alloc([256], F32)
        self.cp("act", ob.t[0:120, :], pso.t[0:120, 0:256], R=[pso], W=[ob])
        self.S.dma("pool", self.o_convs[l].rearrange("s i c -> (s i) c"), ob.t[0:120, :], R=[ob])

    def hgrn_s(self, l):
        A = self.ar
        HQ, HF, HG, HFt, HIt = self.HQ, self.HF, self.HG, self.HFt, self.HIt
        tri = self.cst.t[:, C_TRI8:C_TRI8 + 128]
        trev = self.cst.t[:, C_TREV8:C_TREV8 + 128]
        st = 0
        cs = slice(0, 128)
        Ss = A.alloc([4, 2, 64], F32)
        self.S.dma("pool", Ss.t[:, :, :, :], self.sh_d[l].rearrange("s (pr hf) k v -> (hf k) s pr v", hf=2), W=[Ss])
        sig = A.alloc([256], F32)
        logf = A.alloc([256], F32)
        kin = A.alloc([256], F32)
        self.act(sig.t[:, :], HFt.t[:, st, :], AF.Sigmoid, R=HFt.c(st), W=[sig])
        self.tt("dve", sig.t[:, :], sig.t[:, :], self.omltm.t[:, l, :], ALU.mult, R=[sig, self.omltm], W=[sig])
        self.tt("dve", sig.t[:, :], sig.t[:, :], self.lbtm.t[:, l, :], ALU.add, R=[sig, self.lbtm], W=[sig])
        self.act(logf.t[:, :], sig.t[:, :], AF.Ln, R=[sig], W=[logf])
        self.ts("dve", kin.t[:, :], sig.t[:, :], -1.0, 1.0, ALU.mult, ALU.add, R=[sig], W=[kin])
        prev = self.ps()
        self.mm(prev.t[:, 0:256], trev, logf.t[:, :], True, True, R=[self.cst, logf], W=[prev])
        pbc = self.ps()
        for kc in range(2):
            self.mm(pbc.t[:, kc * 128:(kc + 1) * 128], logf.t[:, kc * 128:(kc + 1) * 128], tri, True, True, R=[self.cst, logf], W=[pbc])
        er = A.alloc([256], F32)
        kh = A.alloc([256], F32)
        vb = A.alloc([256], BF16)
        self.act(er.t[:, :], prev.t[:, 0:256], AF.Exp, R=[prev], W=[er])
        self.tt("dve", kh.t[:, :], kin.t[:, :], er.t[:, :], ALU.mult, R=[kin, er], W=[kh])
        self.cp("pool", vb.t[:, :], HIt.t[:, st, :], R=HIt.c(st), W=[vb])
        E = A.alloc([2, 128], F32)
        Ei = A.alloc([2, 128], F32)
        for kc in range(2):
            self.act(E.t[:, kc, :], pbc.t[:, kc * 128:(kc + 1) * 128], AF.Exp, R=[pbc], W=[E])
            self.act(Ei.t[:, kc, :], pbc.t[:, kc * 128:(kc + 1) * 128], AF.Exp, R=[pbc], W=[Ei], scale=-1.0)
        sT = A.alloc([2, 128], F32)
        self.act(sT.t[:, :, :], HF.t[:, :, cs], AF.Sigmoid, R=[HF], W=[sT])
        for c in range(2):
            self.ts("dve", sT.t[:, c, :], sT.t[:, c, :], self.nomlfm.t[:, l, c:c + 1], self.omlfm.t[:, l, c:c + 1], ALU.mult, ALU.add,
                    R=[sT, self.nomlfm, self.omlfm], W=[sT])
        qt_ = A.alloc([2, 128], BF16)
        kt_ = A.alloc([2, 128], BF16)
        self.tt("dve", qt_.t[:, :, :], HQ.t[:, :, cs], E.t[:, :, :], ALU.mult, R=[HQ, E], W=[qt_])
        self.tt("dve", kt_.t[:, :, :], sT.t[:, :, :], Ei.t[:, :, :], ALU.mult, R=[sT, Ei], W=[kt_])
        patt = [self.ps(), self.ps()]
        for h in range(4):
            pr, hf, r0 = h // 2, h % 2, (h % 2) * 64
            self.mm(patt[hf].t[:, pr * 128:(pr + 1) * 128], kt_.t[r0:r0 + 64, pr, :], qt_.t[r0:r0 + 64, pr, :], True, True, R=[kt_, qt_], W=[patt[hf]])
        att = A.alloc([4, 128], BF16)
        for h in range(4):
            pr, hf = h // 2, h % 2
            self.tt("dve", att.t[:, h, :], patt[hf].t[:, pr * 128:(pr + 1) * 128], tri, ALU.mult, R=[patt[hf], self.cst], W=[att])
        sbf = A.alloc([4, 2, 64], BF16)
        self.cp("act", sbf.t[:, :, :, :], Ss.t[:, :, :, :], R=[Ss], W=[sbf])
        po = [self.ps(), self.ps()]
        for h in range(4):
            pr, hf, r0 = h // 2, h % 2, (h % 2) * 64
            self.mm(po[hf].t[r0:r0 + 64, pr * 128:(pr + 1) * 128], vb.t[:, h * 64:(h + 1) * 64], att.t[:, h, :], True, False, R=[vb, att], W=[po[hf]])
            for sl in range(4):
                self.mm(po[hf].t[r0:r0 + 64, pr * 128 + sl * 8:pr * 128 + sl * 8 + 8], sbf.t[r0:r0 + 64, sl, pr, :], qt_.t[r0:r0 + 64, pr, sl * 8:(sl + 1) * 8],
                        False, sl == 3, R=[sbf, qt_], W=[po[hf]], skip_group_check=True)
        khm = A.alloc([256], BF16)
        for sl in range(4):
            self.ts("dve", khm.t[:, :], kh.t[:, :], self.cst.t[:, C_ROWM + sl:C_ROWM + sl + 1], None, ALU.mult, None, R=[kh, self.cst], W=[khm])
            pU = self.ps()
            for pr in range(2):
                self.mm(pU.t[:, pr * 128:(pr + 1) * 128], khm.t[:, pr * 128:(pr + 1) * 128], vb.t[:, pr * 128:(pr + 1) * 128], True, True, R=[khm, vb], W=[pU])
            for pr in range(2):
                for hf in range(2):
                    r = slice(hf * 64, hf * 64 + 64)
                    k0 = pr * 128 + hf * 64
                    self.stt("dve", Ss.t[r, sl, pr, :], Ss.t[r, sl, pr, :], E.t[r, pr, sl * 8 + 7:sl * 8 + 8], pU.t[r, k0:k0 + 64], ALU.mult, ALU.add,
                             R=[Ss, E, pU], W=[Ss], SS=[E])
        self.S.dma("pool", self.o_hgrns[l].rearrange("s (pr hf) k v -> (hf k) s pr v", hf=2), Ss.t[:, :, :, :], R=[Ss])
        O = A.alloc([2, 128], F32)
        sqo = A.alloc([2, 128], BF16)
        for hf in range(2):
            r = slice(hf * 64, hf * 64 + 64)
            self.cp("act", O.t[r, :, :], po[hf].t[r, 0:256].rearrange("p (a b) -> p a b", a=2), R=[po[hf]], W=[O])
        self.tt("pool", sqo.t[:, :, :], O.t[:, :, :], O.t[:, :, :], ALU.mult, R=[O], W=[sqo])
        pss = self.ps()
        for pr in range(2):
            self.mm(pss.t[:, pr * 128:(pr + 1) * 128], self.blk_bf.t[:, :], sqo.t[:, pr, :], True, True, R=[self.blk_bf, sqo], W=[pss])
        rs = A.alloc([2, 128], F32)
        self.act(rs.t[:, :, :], pss.t[:, 0:256].rearrange("p (a b) -> p a b", a=2), AF.Sqrt, R=[pss, self.prm], W=[rs],
                 bias=self.prm.t[:, PG_EPS:PG_EPS + 1], scale=1.0 / 64)
        self.S.op("dve", lambda e: e.reciprocal(out=rs.t[:, :, :], in_=rs.t[:, :, :]), R=[rs], W=[rs])
        sgt = A.alloc([2, 128], F32)
        self.act(sgt.t[:, :, :], HG.t[:, :, cs], AF.Silu, R=[HG], W=[sgt])
        self.tt("dve", O.t[:, :, :], O.t[:, :, :], rs.t[:, :, :], ALU.mult, R=[O, rs], W=[O])
        for pr in range(2):
            self.stt("dve", self.ycat.t[:, 6 + pr, cs], O.t[:, pr, :], self.par(l, "hnorm", pr), sgt.t[:, pr, :], ALU.mult, ALU.mult,
                     R=[O, sgt, self.prm], W=self.ycat.c(6 + pr))
            self.S.op("pool", lambda e: e.memset(self.ycat.t[:, 6 + pr, 128:256], 0.0), W=self.ycat.c(6 + pr))

    def moba_s(self, l):
        A = self.ar
        S = self.S
        NP, NB = self.NP, self.NP // 2
        st = 0
        QTb = A.alloc([2, 128], BF16)
        KTn = A.alloc([2, 128], BF16)
        Vxn = A.alloc([4, 66], BF16)
        mark2 = A.off
        QN, KN = self.qk_norm_rope(l, st, self.coss.t[:, :], self.sins.t[:, :], [self.coss, self.sins])
        S.dma("pool", self.o_ks[l], KN.t[0:32, :], R=[KN])
        S.dma("pool", self.o_vs[l], self.Vt.t[0:32, st, :], R=self.Vt.c(st))
        pq, pk = self.ps(), self.ps()
        for pr in range(2):
            self.tr(pq.t[:, pr * 128:(pr + 1) * 128], QN.t[:, pr * 128:(pr + 1) * 128], R=[QN], W=[pq])
            self.tr(pk.t[:, pr * 128:(pr + 1) * 128], KN.t[:, pr * 128:(pr + 1) * 128], R=[KN], W=[pk])
        for pr in range(2):
            self.cp("dve", QTb.t[:, pr, :], pq.t[:, pr * 128:(pr + 1) * 128], R=[pq], W=[QTb])
            self.cp("dve", KTn.t[:, pr, :], pk.t[:, pr * 128:(pr + 1) * 128], R=[pk], W=[KTn])
        S.op("pool", lambda e: e.memset(Vxn.t[:, :, :], 1.0), W=[Vxn])
        for h in range(4):
            self.cp("dve", Vxn.t[:, h, 0:64], self.Vt.t[:, st, h * 64:(h + 1) * 64], R=self.Vt.c(st), W=[Vxn])
        S.barrier()
        A.off = mark2
        self.KTc = [A.alloc([2, 4, 128], BF16)]
        self.Pof = A.alloc([32], F32)
        Sall = A.alloc([128, 32], F32)
        Kcs = [A.alloc([4, 256], F32), A.alloc([4, 256], F32)]
        Vcbs = [A.alloc([4, 4, 66], BF16), A.alloc([4, 4, 66], BF16)]
        for Vcb in Vcbs:
            S.op("pool", lambda e: e.memset(Vcb.t[:, :, :, :], 1.0), W=[Vcb])
        P = [self.hT, self.sqb]
        Pv = [b_.t[:, :, :].rearrange("p a b -> p (a b)").rearrange("p (r c) -> p r c", c=32) for b_ in P]
        gT = A.alloc([32], F32)
        GATE = A.alloc([64], F32)
        SEL = A.alloc([64], F32)
        mx = A.alloc([8], F32)
        selTs = A.alloc([32], F32)
        selT2 = A.alloc([32], F32)
        Po = A.alloc([32], BF16)
        ACC = A.alloc([264], F32)
        Osel = A.alloc([64], F32)
        den = A.alloc([1], F32)
        Oh = A.alloc([2, 128], F32)
        hm = self.cst.t[:, C_HM:C_HM + 4]
        for sl in range(4):
            idx = self.ptab.t[0:NP, sl:sl + 1]
            for rc in range(32):
                Kc = Kcs[rc % 2]
                S.idma(Kc.t[0:NP, :, :].rearrange("p a b -> p (a b)"), self.ck_d[l][rc], idx, R=[self.ptab], W=[Kc])
                pS = [self.psb[6], self.psb[7]]
                KTc = self.KTc[0]
                for pr in range(2):
                    pt_ = self.ps()
                    for ri in range(4):
                        self.tr(pt_.t[:, ri * 128:ri * 128 + NP], Kc.t[0:NP, ri, pr * 128:(pr + 1) * 128], R=[Kc], W=[pt_], n=NP)
                    self.cp("act" if pr == 0 else "dve", KTc.t[:, pr, :, 0:NP], pt_.t[:, :].rearrange("p (a b) -> p a b", a=4)[:, :, 0:NP], R=[pt_], W=[KTc])
                for ri in range(4):
                    for h in range(4):
                        pr, hf, r0 = h // 2, h % 2, (h % 2) * 64
                        c0 = (ri * 2 + pr) * 8
                        self.mm(pS[hf].t[0:NP, c0:c0 + 8], KTc.t[r0:r0 + 64, pr, ri, 0:NP], QTb.t[r0:r0 + 64, pr, sl * 8:(sl + 1) * 8], True, True,
                                R=[KTc, QTb], W=[pS[hf]])
                for hf in range(2):
                    dst = Sall.t[0:NP, rc * 4:(rc + 1) * 4, :].rearrange("p r (pr hf q) -> p hf r pr q", pr=2, hf=2)[:, hf]
                    self.cp("act" if hf == 0 else "dve", dst, pS[hf].t[0:NP, 0:64].rearrange("p (r pr q) -> p r pr q", r=4, pr=2), R=[pS[hf]], W=[Sall])
            if NB > 3:
                S.op("dve", lambda e: e.tensor_reduce(out=gT.t[0:NP, :], in_=Sall.t[0:NP, :, :].rearrange("p r c -> p c r"), axis=AX.X, op=ALU.add), R=[Sall], W=[gT])
                pg = self.ps()
                self.mm(pg.t[0:32, 0:NB], gT.t[0:NP, :], self.cst.t[0:NP, C_PAIRM:C_PAIRM + NB], True, True, R=[gT, self.cst], W=[pg])
                S.op("pool", lambda e: e.memset(GATE.t[0:32, :], NEG), W=[GATE])
                self.cp("dve", GATE.t[0:32, 0:NB], pg.t[0:32, 0:NB], R=[pg], W=[GATE])
                S.op("dve", lambda e: e.max(out=mx.t[0:32, :], in_=GATE.t[0:32, :]), R=[GATE], W=[mx])
                self.ts("dve", SEL.t[0:32, :], GATE.t[0:32, :], mx.t[0:32, 2:3], None, ALU.is_ge, None, R=[GATE, mx], W=[SEL], SS=[mx])
                pst = self.ps()
                self.tr(pst.t[0:64, 0:32], SEL.t[0:32, :], R=[SEL], W=[pst], n=32)
                self.cp("dve", selTs.t[0:64, :], pst.t[0:64, 0:32], R=[pst], W=[selTs])
                pe_ = self.ps()
                self.mm(pe_.t[0:NP, 0:32], self.cst.t[0:NB, C_PAIRMT:C_PAIRMT + NP], selTs.t[0:NB, :], True, True, R=[self.cst, selTs], W=[pe_])
                self.cp("dve", selT2.t[0:NP, :], pe_.t[0:NP, 0:32], R=[pe_], W=[selT2])
            else:
                S.op("pool", lambda e: e.memset(selT2.t[:, :], 1.0), W=[selT2])
            self.act(Sall.t[0:NP, :, :], Sall.t[0:NP, :, :], AF.Exp, R=[Sall], W=[Sall], scale=SCALE)
            for half in range(2):
                for c in range(32):
                    self.ts("dve", Pv[half][0:NP, :, c], Sall.t[0:NP, half * 64:(half + 1) * 64, c], selT2.t[0:NP, c:c + 1], None, ALU.mult, None,
                            R=[Sall, selT2], W=P[half].all, SS=[selT2])
            pacc = self.psb[6]
            self.mm(pacc.t[0:32, 0:264], self.zero_bf.t[:, 0:32], self.zero_bf.t[:, 0:264], True, False, R=[self.zero_bf], W=[pacc])
            for rc in range(32):
                Kc = Kcs[rc % 2]
                Vcb = Vcbs[rc % 2]
                S.idma(Kc.t[0:NP, :, :].rearrange("p a b -> p (a b)"), self.cv_d[l][rc], idx, R=[self.ptab], W=[Kc])
                for h in range(4):
                    self.cp("dve" if h % 2 == 0 else "pool", Vcb.t[0:NP, :, h, 0:64], Kc.t[0:NP, :, h * 64:(h + 1) * 64], R=[Kc], W=[Vcb])
                for ri in range(4):
                    r = rc * 4 + ri
                    self.mm(pacc.t[0:32, 0:264], Pv[r // 64][0:NP, r % 64, :], Vcb.t[0:NP, ri, :, :].rearrange("p h d -> p (h d)"), False, False,
                            R=[P[r // 64].all, Vcb], W=[pacc], skip_group_check=True)
            pso = [self.ps(), self.ps()]
            for h in range(4):
                pr, hf, r0 = h // 2, h % 2, (h % 2) * 64
                self.mm(pso[hf].t[:, pr * 8:(pr + 1) * 8], KTn.t[r0:r0 + 64, pr, :], QTb.t[r0:r0 + 64, pr, sl * 8:(sl + 1) * 8], True, True, R=[KTn, QTb], W=[pso[hf]])
            Pof = self.Pof
            for hf in range(2):
                self.act(Pof.t[:, :].rearrange("p (pr hf q) -> p hf pr q", pr=2, hf=2)[:, hf], pso[hf].t[:, 0:16].rearrange("p (a b) -> p a b", a=2), AF.Exp,
                         R=[pso[hf]], W=[Pof], scale=SCALE)
            for h in range(4):
                self.tt("dve", Po.t[:, h * 8:(h + 1) * 8], Pof.t[:, h * 8:(h + 1) * 8], self.cst.t[:, C_OWNM + sl * 8:C_OWNM + sl * 8 + 8], ALU.mult,
                        R=[Pof, self.cst], W=[Po])
            self.mm(pacc.t[0:32, 0:264], Po.t[:, :], Vxn.t[:, :, :].rearrange("p h d -> p (h d)"), False, True, R=[Po, Vxn], W=[pacc], skip_group_check=True)
            self.cp("act", ACC.t[0:32, :], pacc.t[0:32, 0:264], R=[pacc], W=[ACC])
            self.ts("dve", Osel.t[0:32, :], ACC.t[0:32, 0:64], hm[0:32, 0:1], None, ALU.mult, None, R=[ACC, self.cst], W=[Osel])
            self.ts("dve", den.t[0:32, :], ACC.t[0:32, 64:65], hm[0:32, 0:1], None, ALU.mult, None, R=[ACC, self.cst], W=[den])
            for h in range(1, 4):
                self.stt("dve", Osel.t[0:32, :], ACC.t[0:32, h * 66:h * 66 + 64], hm[0:32, h:h + 1], Osel.t[0:32, :], ALU.mult, ALU.add, R=[ACC, self.cst, Osel], W=[Osel])
                self.stt("dve", den.t[0:32, :], ACC.t[0:32, h * 66 + 64:h * 66 + 65], hm[0:32, h:h + 1], den.t[0:32, :], ALU.mult, ALU.add, R=[ACC, self.cst, den], W=[den])
            S.op("dve", lambda e: e.reciprocal(out=den.t[0:32, :], in_=den.t[0:32, :]), R=[den], W=[den])
            self.ts("dve", Osel.t[0:32, :], Osel.t[0:32, :], den.t[0:32, 0:1], None, ALU.mult, None, R=[Osel, den], W=[Osel], SS=[den])
            for pr in range(2):
                for hf in range(2):
                    self.ts("dve", Oh.t[0:32, pr, hf * 64:(hf + 1) * 64], Osel.t[0:32, :], hm[0:32, pr * 2 + hf:pr * 2 + hf + 1], None, ALU.mult, None,
                            R=[Osel, self.cst], W=[Oh])
            pf = self.ps()
            for pr in range(2):
                self.mm(pf.t[:, pr * 8:(pr + 1) * 8], Oh.t[0:32, pr, :], self.cst.t[0:32, C_QSEL:C_QSEL + 8], True, True, R=[Oh, self.cst], W=[pf])
            for pr in range(2):
                self.cp("dve", self.ycat.t[:, 2 + pr, sl * 8:(sl + 1) * 8], pf.t[:, pr * 8:(pr + 1) * 8], R=[pf], W=self.ycat.c(2 + pr))

    def mixer_s(self, l):
        TT = self.TT
        A = self.ar
        A.reset()
        self.Qt = A.alloc([self.NST, 256], F32, ncells=self.NST)
        self.Kt = A.alloc([self.NST, 256], F32, ncells=self.NST)
        self.Vt = A.alloc([self.NST, 256], F32, ncells=self.NST)
        mark_qkv = A.off
        self.HQ = A.alloc([2, TT], F32)
        self.HF = A.alloc([2, TT], F32)
        self.HG = A.alloc([2, TT], F32)
        self.HFt = A.alloc([self.NST, 256], F32, ncells=self.NST)
        self.HIt = A.alloc([self.NST, 256], F32, ncells=self.NST)
        CA = A.alloc([2, TT], F32)
        Us = A.alloc([2, TT], F32)
        Gn = A.alloc([2, TT], F32)
        self.proj_fm(l, 0, Us)
        self.proj_tm(l, 256, self.Qt)
        self.proj_tm(l, 512, self.Kt)
        self.proj_tm(l, 768, self.Vt)
        self.proj_fm(l, 1024, CA)

        def glu(cc, ps):
            sb = A.alloc([TT], F32)
            self.act(sb.t[:, :], ps.t[:, 0:TT], AF.Sigmoid, R=[ps], W=[sb])
            self.tt("dve", Gn.t[:, cc, :], CA.t[:, cc, :], sb.t[:, :], ALU.mult, R=[CA, sb], W=[Gn])
        self.proj_fm(l, 1280, None, post=glu)
        self.proj_fm(l, 1536, self.HQ)
        self.proj_fm(l, 1792, self.HF)
        self.proj_tm(l, 1792, self.HFt)
        self.proj_tm(l, 2048, self.HIt)
        self.proj_fm(l, 2304, self.HG)
        base = A.off
        self.pool_s(l, Us)
        self.S.barrier()
        A.off = base
        self.conv_s(l, Gn)
        self.S.barrier()
        A.off = base
        self.hgrn_s(l)
        self.S.barrier()
        A.off = mark_qkv
        for pr in range(2):
            self.S.op("pool", lambda e: e.memset(self.ycat.t[:, 2 + pr, :], 0.0), W=self.ycat.c(2 + pr))
        self.moba_s(l)
        self.S.barrier()
        self.outproj(l)

    def sample_tile(self):
        S, A = self.S, self.ar
        A.reset()
        for st in range(2):
            xin = A.alloc([D], F32)
            S.dma("pool", xin.t[:, :], self.xs_d[st * 128:(st + 1) * 128, :], W=[xin])
            for g in range(2):
                ps = self.ps()
                for c4 in range(4):
                    c = g * 4 + c4
                    self.tr(ps.t[:, c4 * 128:(c4 + 1) * 128], xin.t[:, c * 128:(c + 1) * 128], R=[xin], W=[ps])
                self.cp("act" if g == 0 else "dve", self.xT.t[:, g * 4:(g + 1) * 4, st * 128:(st + 1) * 128],
                        ps.t[:, :].rearrange("p (a b) -> p a b", a=4), R=[ps], W=[self.xT.c(g * 4 + i) for i in range(4)])
        for l in range(2):
            self.rmsnorm(l, "g1")
            self.ffn(l, self.W1g, self.W1u, self.W1d)
            self.rmsnorm(l, "gm")
            self.mixer_s(l)
            self.rmsnorm(l, "g2")
            self.ffn(l, self.W2g, self.W2u, self.W2d)
        A.reset()
        yo = A.alloc([D], F32)
        for g in range(2):
            ps = self.ps()
            for c4 in range(4):
                c = g * 4 + c4
                self.tr(ps.t[0:32, c4 * 128:(c4 + 1) * 128], self.xT.t[:, c, 0:32], R=self.xT.c(c), W=[ps])
            self.cp("act" if g == 0 else "dve", yo.t[0:32, g * 512:(g + 1) * 512], ps.t[0:32, :], R=[ps], W=[yo])
        S.dma("pool", self.ys_d, yo.t[0:32, :], R=[yo])

    def build(self):
        SEQ, TT, NT, NST, NQT = self.SEQ, self.TT, self.NT, self.NST, self.NQT
        nc = bass.Bass("TRN2", target_bir_lowering=False)
        self.nc = nc

        def din(name, shape):
            return nc.dram_tensor(name, list(shape), F32, kind="ExternalInput").ap()

        def dout(name, shape):
            return nc.dram_tensor(name, list(shape), F32, kind="ExternalOutput").ap()

        x = din("xp", [SEQ, D])
        cst_d = din("cst", [128, C_TOT])
        prm_d = din("prm", [128, PAR_TOT])
        cos_d = din("cos", [SEQ, 32])
        sin_d = din("sin", [SEQ, 32])
        pwbd_d = din("pwbd", [2, 2, 128, 128])
        pw_d = din("cpw", [2, 256, 256])
        self.W1g, self.W1u, self.W1d = din("w1g", [2, D, DFF]), din("w1u", [2, D, DFF]), din("w1d", [2, DFF, D])
        self.W2g, self.W2u, self.W2d = din("w2g", [2, D, DFF]), din("w2u", [2, D, DFF]), din("w2d", [2, DFF, D])
        self.w_in, self.w_out = din("w_in", [2, D, 2560]), din("w_out", [2, D, D])
        NP, NPOOL = self.NP, self.NPOOL
        self.xs_d = din("xs", [256, D])
        self.sp_d = din("sp", [2, 4, 15, 256])
        self.sc_d = din("sc", [2, 4, 30, 256])
        self.sh_d = din("sh", [2, 4, 4, 64, 64])
        pt_d = nc.dram_tensor("pt", [4, NP], mybir.dt.int32, kind="ExternalInput").ap()
        coss_d, sins_d = din("coss", [128, 32]), din("sins", [128, 32])
        self.ck_d = [[din("ck%d_%d" % (l, rc), [NPOOL, 1024]) for rc in range(32)] for l in range(2)]
        self.cv_d = [[din("cv%d_%d" % (l, rc), [NPOOL, 1024]) for rc in range(32)] for l in range(2)]
        self.ys_d = dout("ys", [32, D])
        self.o_ks, self.o_vs = dout("oks", [2, 32, 256]), dout("ovs", [2, 32, 256])
        self.o_pools, self.o_convs = dout("opools", [2, 4, 15, 256]), dout("oconvs", [2, 4, 30, 256])
        self.o_hgrns = dout("ohgrns", [2, 4, 4, 64, 64])
        y = dout("y", [SEQ, D])
        self.o_k, self.o_v = dout("ok", [2, SEQ, 256]), dout("ov", [2, SEQ, 256])
        self.o_pool, self.o_conv = dout("opool", [2, 15, 256]), dout("oconv", [2, 30, 256])
        self.o_hgrn = dout("ohgrn", [2, 4, 64, 64])

        with ExitStack() as stack:
            S = Sched(nc, stack)
            self.S = S
            self.psb = [S.psum("ps%d" % i, [128, 512]) for i in range(8)]
            self.psi = 0
            self.pti = 0
            self.wi = 0
            self.wj = 0
            self.wmap = {}
            self.first_pass_done = False
            self.wsc = nc.dram_tensor("wsc", [192, 128, 2048], BF16, kind="Internal").ap()
            self.wstg = [S.sbuf("wstg%d" % i, [128, 2048], F32) for i in range(2)]
            self.wbf = [S.sbuf("wbf%d" % i, [128, 2048], BF16) for i in range(2)]
            self.wring = list(self.wbf)
            self.cst = S.sbuf("cst", [128, C_TOT], F32)
            self.prm = S.sbuf("prm", [128, PAR_TOT], F32)
            self.ones_bf = S.sbuf("ones_bf", [128, 128], BF16)
            self.blk_bf = S.sbuf("blk_bf", [128, 128], BF16)
            self.caus_bf = S.sbuf("caus_bf", [128, 128], BF16)
            self.zero_bf = S.sbuf("zero_bf", [128, 264], BF16)
            self.cos = S.sbuf("cos", [128, NQT, 32], F32)
            self.sin = S.sbuf("sin", [128, NQT, 32], F32)
            self.pwbd = S.sbuf("pwbd", [128, 2, 2, 128], BF16)
            self.pw = S.sbuf("pw", [128, 2, 2, 256], BF16)
            self.lbtm = S.sbuf("lbtm", [128, 2, 256], F32)
            self.omltm = S.sbuf("omltm", [128, 2, 256], F32)
            self.omlfm = S.sbuf("omlfm", [128, 2, 2], F32)
            self.nomlfm = S.sbuf("nomlfm", [128, 2, 2], F32)
            self.xT = S.sbuf("xT", [128, 8, TT], F32, ncells=8)
            self.hT = S.sbuf("hT", [128, 8, TT], BF16, ncells=8)
            self.sqb = S.sbuf("sqb", [128, 8, TT], BF16, ncells=8)
            self.rstd = S.sbuf("rstd", [128, TT], F32)
            self.ycat = S.sbuf("ycat", [128, 8, TT], BF16, ncells=8)
            self.UP = [S.sbuf("UP%d" % l, [128, 2, 15 + TT], F32) for l in range(2)]
            self.G = [S.sbuf("G%d" % l, [128, 2, 30 + TT], F32) for l in range(2)]
            self.Sst = [S.sbuf("Sst%d" % l, [128, 2, 64], F32) for l in range(2)]
            self.KT = [S.sbuf("KT%d" % l, [128, 2, SEQ], BF16, ncells=NQT) for l in range(2)]
            self.VX = [S.sbuf("VX%d" % l, [128, NQT, 4, 66], BF16, ncells=NQT) for l in range(2)]
            self.KP = [S.sbuf("KP%d" % l, [128, 2, NQT], F32) for l in range(2)]
            self.KM = [S.sbuf("KM%d" % l, [128, 2, 16], F32) for l in range(2)]
            self.PT = [S.sbuf("PT%d" % i, [128, 512], BF16) for i in range(2)]
            self.ar = Arena(S, "arena", self.arena_bytes())
            A = self.ar

            S.dma("sp", self.cst.t[:, :], cst_d, W=[self.cst])
            S.dma("sp", self.prm.t[:, :], prm_d, W=[self.prm])
            S.dma("sp", self.cos.t[:, :, :], cos_d.rearrange("(q p) c -> p q c", p=128), W=[self.cos])
            S.dma("sp", self.sin.t[:, :, :], sin_d.rearrange("(q p) c -> p q c", p=128), W=[self.sin])
            self.cp("dve", self.ones_bf.t[:, :], self.cst.t[:, C_ONES:C_ONES + 128], R=[self.cst], W=[self.ones_bf])
            self.cp("dve", self.blk_bf.t[:, :], self.cst.t[:, C_BLK:C_BLK + 128], R=[self.cst], W=[self.blk_bf])
            self.cp("dve", self.caus_bf.t[:, :], self.cst.t[:, C_CAUS:C_CAUS + 128], R=[self.cst], W=[self.caus_bf])
            S.op("pool", lambda e: e.memset(self.zero_bf.t[:, :], 0.0), W=[self.zero_bf])
            t0 = A.alloc([2, 2, 128], F32)
            S.dma("pool", t0.t[:, :, :, :], pwbd_d.rearrange("l c p n -> p l c n"), W=[t0])
            self.cp("dve", self.pwbd.t[:, :, :, :], t0.t[:, :, :, :], R=[t0], W=[self.pwbd])
            t1 = A.alloc([2, 2, 256], F32)
            S.dma("pool", t1.t[:, :, :, :], pw_d.rearrange("l (c p) n -> p l c n", p=128), W=[t1])
            self.cp("dve", self.pw.t[:, :, :, :], t1.t[:, :, :, :], R=[t1], W=[self.pw])
            for (dst_lb, dst_oml, nm, n) in ((self.lbtm, self.omltm, "lbtm", 256), (None, self.omlfm, "lbfm", 2)):
                d = A.alloc([n], F32)
                lb1 = A.alloc([n], F32)
                self.tt("dve", d.t[:, :], self.par(1, nm, 0, n), self.par(0, nm, 0, n), ALU.subtract, R=[self.prm], W=[d])
                self.act(lb1.t[:, :], d.t[:, :], AF.Sigmoid, R=[d], W=[lb1])
                if dst_lb is not None:
                    S.op("pool", lambda e: e.memset(dst_lb.t[:, 0, :], 0.0), W=[dst_lb])
                    self.cp("dve", dst_lb.t[:, 1, :], lb1.t[:, :], R=[lb1], W=[dst_lb])
                S.op("pool", lambda e: e.memset(dst_oml.t[:, 0, :], 1.0), W=[dst_oml])
                self.ts("dve", dst_oml.t[:, 1, :], lb1.t[:, :], -1.0, 1.0, ALU.mult, ALU.add, R=[lb1], W=[dst_oml])
            self.ts("dve", self.nomlfm.t[:, :, :], self.omlfm.t[:, :, :], -1.0, None, ALU.mult, None, R=[self.omlfm], W=[self.nomlfm])
            for l in range(2):
                S.op("pool", lambda e: e.memset(self.UP[l].t[:, :, :], 0.0), W=[self.UP[l]])
                S.op("pool", lambda e: e.memset(self.G[l].t[:, :, :], 0.0), W=[self.G[l]])
                S.op("pool", lambda e: e.memset(self.Sst[l].t[:, :, :], 0.0), W=[self.Sst[l]])
                S.op("pool", lambda e: e.memset(self.VX[l].t[:, :, :, 64:65], 1.0), W=[self.VX[l]])
                S.op("pool", lambda e: e.memset(self.KM[l].t[:, :, :], 0.0), W=[self.KM[l]])
            S.barrier(sp=True)

            self.coss = S.sbuf("coss", [128, 32], F32)
            self.sins = S.sbuf("sins", [128, 32], F32)
            self.ptab = S.sbuf("ptab", [128, 4], mybir.dt.int32)
            S.dma("sp", self.coss.t[:, :], coss_d, W=[self.coss])
            S.dma("sp", self.sins.t[:, :], sins_d, W=[self.sins])
            S.op("pool", lambda e: e.memset(self.ptab.t[:, :], 0), W=[self.ptab])
            with nc.allow_non_contiguous_dma(reason="tiny page-table transpose"):
                S.dma("sp", self.ptab.t[0:NP, :], pt_d.rearrange("s j -> j s"), W=[self.ptab])
            if DO_SAMPLE:
                self.sample_tile()
                self.end_first_pass()
            S.barrier(sp=True)

            for tile in range(NT):
                last = tile == NT - 1
                A.reset()
                for st in range(NST):
                    xin = A.alloc([D], F32)
                    r0 = tile * TT + st * 128
                    S.dma("pool", xin.t[:, :], x[r0:r0 + 128, :], W=[xin])
                    for g in range(2):
                        ps = self.ps()
                        for c4 in range(4):
                            c = g * 4 + c4
                            self.tr(ps.t[:, c4 * 128:(c4 + 1) * 128], xin.t[:, c * 128:(c + 1) * 128], R=[xin], W=[ps])
                        self.cp("act" if g == 0 else "dve", self.xT.t[:, g * 4:(g + 1) * 4, st * 128:(st + 1) * 128],
                                ps.t[:, :].rearrange("p (a b) -> p a b", a=4), R=[ps], W=[self.xT.c(g * 4 + i) for i in range(4)])
                for l in range(2):
                    if STAGE >= 1:
                        self.rmsnorm(l, "g1")
                    if STAGE >= 2:
                        self.ffn(l, self.W1g, self.W1u, self.W1d)
                    if STAGE >= 3:
                        self.rmsnorm(l, "gm")
                        self.mixer(l, tile, last)
                    if STAGE >= 8:
                        self.rmsnorm(l, "g2")
                        self.ffn(l, self.W2g, self.W2u, self.W2d)
                self.end_first_pass()
                A.reset()
                for st in range(NST):
                    yo = A.alloc([D], F32)
                    for g in range(2):
                        ps = self.ps()
                        for c4 in range(4):
                            c = g * 4 + c4
                            self.tr(ps.t[:, c4 * 128:(c4 + 1) * 128], self.xT.t[:, c, st * 128:(st + 1) * 128], R=self.xT.c(c), W=[ps])
                        self.cp("act" if g == 0 else "dve", yo.t[:, g * 512:(g + 1) * 512], ps.t[:, :], R=[ps], W=[yo])
                    r0 = tile * TT + st * 128
                    S.dma("pool", y[r0:r0 + 128, :], yo.t[:, :], R=[yo])
            S.finish()
            self.stats = (S.ninstr, S.nwaits)
        return nc

    def arena_bytes(self):
        TT = self.TT
        return 42 * 1024 if TT <= 256 else 72 * 1024


_NC_CACHE = {}


def _get_nc(SEQ, TT, NP, NPOOL):
    key = (SEQ, TT, NP, NPOOL)
    if key not in _NC_CACHE:
        b = Builder(SEQ, TT, NP, NPOOL)
        _NC_CACHE[key] = (b.build(), b)
    return _NC_CACHE[key]


def run_all(inp, n_cores, TT=256):
    B, SEQ, _ = inp["x_prompt"].shape
    DB, DS, _ = inp["x_sample"].shape
    NP = inp["page_table"].shape[1]
    NPOOL = inp["cache_k"].shape[1]
    PAST = NP * inp["cache_k"].shape[2]
    assert DS == 8 and DB == 4 * n_cores and B == n_cores
    nc, b = _get_nc(SEQ, TT, NP, NPOOL)
    f32 = lambda a: np.ascontiguousarray(np.asarray(a, np.float32))
    cst = _const_tables()
    prm = _params_table(inp)
    cos, sin = _rope_tables(np.arange(SEQ))
    coss, sins = _rope_tables(PAST + (np.arange(128) % 8))
    pw = f32(inp["pool_w"])
    pwbd = np.zeros((2, 2, 128, 128), np.float32)
    for l in range(2):
        for g in range(4):
            c, hf = g // 2, g % 2
            pwbd[l, c, hf * 64:(hf + 1) * 64, hf * 64:(hf + 1) * 64] = pw[l, g]
    shared = {
        "cst": cst, "prm": prm, "cos": cos, "sin": sin, "coss": coss, "sins": sins, "pwbd": pwbd, "cpw": f32(inp["conv_pw"]),
        "w1g": f32(inp["ffn1_w_gate"]), "w1u": f32(inp["ffn1_w_up"]), "w1d": f32(inp["ffn1_w_down"]),
        "w2g": f32(inp["ffn2_w_gate"]), "w2u": f32(inp["ffn2_w_up"]), "w2d": f32(inp["ffn2_w_down"]),
        "w_in": f32(inp["w_in"]), "w_out": f32(inp["w_out"]),
    }
    for nm, key in (("ck", "cache_k"), ("cv", "cache_v")):
        c5 = np.asarray(inp[key], np.float32)
        for l in range(2):
            c3 = c5[l].reshape(NPOOL, 32, 1024)
            for rc in range(32):
                shared["%s%d_%d" % (nm, l, rc)] = np.ascontiguousarray(c3[:, rc, :])
    xp = f32(inp["x_prompt"])
    xs = f32(inp["x_sample"])
    sp_, sc_, sh_ = f32(inp["state_pool"]), f32(inp["state_conv"]), f32(inp["state_hgrn"])
    pt = np.ascontiguousarray(np.asarray(inp["page_table"], np.int32))
    in_maps = []
    for c in range(n_cores):
        m = dict(shared, xp=xp[c])
        xs_c = np.zeros((256, D), np.float32)
        xs_c[0:32] = xs[4 * c:4 * c + 4].reshape(32, D)
        m["xs"] = xs_c
        m["sp"] = np.ascontiguousarray(sp_[:, 4 * c:4 * c + 4])
        m["sc"] = np.ascontiguousarray(sc_[:, 4 * c:4 * c + 4])
        m["sh"] = np.ascontiguousarray(sh_[:, 4 * c:4 * c + 4])
        m["pt"] = np.ascontiguousarray(pt[4 * c:4 * c + 4])
        in_maps.append(m)
    res = run_bass_kernel_spmd(nc, in_maps, core_ids=list(range(n_cores)))
    return res.results


def run_prompt(inp, SEQ, TT, n_cores):
    return run_all(inp, n_cores, TT)


def kernel(**inp):
    B, SEQ, _ = inp["x_prompt"].shape
    DB, DS, _ = inp["x_sample"].shape
    r = run_all(inp, B)
    cat = lambda k, ax: np.concatenate([r[c][k] for c in range(B)], axis=ax)
    y_prompt = np.stack([r[c]["y"] for c in range(B)])
    k_prompt = np.stack([r[c]["ok"] for c in range(B)], axis=1).reshape(2, B, SEQ, 4, 64)
    v_prompt = np.stack([r[c]["ov"] for c in range(B)], axis=1).reshape(2, B, SEQ, 4, 64)
    pool_prompt = np.stack([r[c]["opool"] for c in range(B)], axis=1)
    conv_prompt = np.stack([r[c]["oconv"] for c in range(B)], axis=1)
    hgrn_prompt = np.stack([r[c]["ohgrn"] for c in range(B)], axis=1)
    y_sample = cat("ys", 0).reshape(DB, DS, D)
    k_sample = cat("oks", 1).reshape(2, DB, DS, 4, 64)
    v_sample = cat("ovs", 1).reshape(2, DB, DS, 4, 64)
    pool_sample = cat("opools", 1)
    conv_sample = cat("oconvs", 1)
    hgrn_sample = cat("ohgrns", 1)
    return (y_prompt, y_sample, k_prompt, v_prompt, k_sample, v_sample,
            pool_prompt, pool_sample, conv_prompt, conv_sample, hgrn_prompt, hgrn_sample)
```
